# Optimizing a Trainium2 kernel written in Bass

```python
import math
import jax, jax.numpy as jnp
from jax import lax
import numpy as np

D_MODEL = 1024
BATCH = 8
SEQ = 8192
DEPTH = 2
DEC_BATCH = 4
DEC_SEQ = 8192
PAST_LEN = 128

GROUP_WIDTH = 64
D_HYENA = 384
D_CONF = 256
D_RG = 384
D_MIX = D_HYENA + D_CONF + D_RG
RG_HEADS = D_RG // GROUP_WIDTH
HYENA_SHORT_K = 3
HYENA_EMB = 33
HYENA_BANDS = (HYENA_EMB - 1) // 2
HYENA_FILTER_ORDER = 64
HYENA_TARGET = 1e-2
HYENA_FAST_DECAY_PCT = 0.3
HYENA_SLOW_DECAY_PCT = 1.5
CONF_K = 31
RG_CONV_K = 4
RG_C = 8.0
IN_COLS = 3 * D_HYENA + 2 * D_CONF + D_RG + D_MIX
EPS = 1e-6

kernel_name = "hybrid_hyena_conformer_rglru_encoder"


def rmsnorm(x, g):
    xf = x.astype(jnp.float32)
    y = xf * lax.rsqrt(jnp.mean(xf * xf, axis=-1, keepdims=True) + EPS)
    return (y * g.astype(jnp.float32)).astype(x.dtype)


def layernorm(x, g, b):
    xf = x.astype(jnp.float32)
    mu = jnp.mean(xf, axis=-1, keepdims=True)
    xc = xf - mu
    var = jnp.mean(xc * xc, axis=-1, keepdims=True)
    y = xc * lax.rsqrt(var + EPS) * g.astype(jnp.float32) + b.astype(jnp.float32)
    return y.astype(x.dtype)


def dwconv(x, w, b, pad_left, pad_right):
    C = x.shape[-1]
    y = lax.conv_general_dilated(
        x, w[:, None, :].astype(x.dtype), window_strides=(1,),
        padding=[(pad_left, pad_right)],
        dimension_numbers=("NWC", "WIO", "NWC"), feature_group_count=C)
    return y + b.astype(x.dtype)


def hyena_filters(L, w1, b1, w2, b2, w3, b3, w4, freq):
    f32 = jnp.float32
    t = jnp.linspace(0.0, 1.0, L, dtype=f32)[:, None]
    w = (2.0 * math.pi / L) * jnp.arange(L, dtype=f32)[:, None]
    f = jnp.linspace(1e-4, HYENA_BANDS - 1, HYENA_BANDS, dtype=f32)[None]
    z = jnp.concatenate([t, jnp.cos(f * w), -jnp.sin(f * w)], axis=-1)
    fr = freq.astype(f32)
    hdn = jnp.sin(fr * (z @ w1.astype(f32) + b1.astype(f32)))
    hdn = jnp.sin(fr * (hdn @ w2.astype(f32) + b2.astype(f32)))
    hdn = jnp.sin(fr * (hdn @ w3.astype(f32) + b3.astype(f32)))
    h = hdn @ w4.astype(f32)
    max_decay = math.log(HYENA_TARGET) / HYENA_FAST_DECAY_PCT
    min_decay = math.log(HYENA_TARGET) / HYENA_SLOW_DECAY_PCT
    deltas = jnp.abs(jnp.linspace(min_decay, max_decay, D_HYENA, dtype=f32))
    decay = jnp.exp(-t * deltas[None])
    return h[:, :D_HYENA] * decay, h[:, D_HYENA:] * decay


def two_sided_fftconv(u, h_fwd, h_bwd):
    L, C = h_fwd.shape
    k = jnp.concatenate([h_fwd, jnp.zeros((1, C), jnp.float32), h_bwd[1:][::-1]], axis=0)
    u_f = jnp.fft.rfft(u, n=2 * L, axis=1)
    k_f = jnp.fft.rfft(k, n=2 * L, axis=0)
    return jnp.fft.irfft(u_f * k_f[None], n=2 * L, axis=1)[:, :L]


def hyena_branch(p, conv_w, conv_b, w1, b1, w2, b2, w3, b3, w4, freq, d_skip):
    L = p.shape[1]
    uc = dwconv(p, conv_w, conv_b, 1, 1)
    x0, x1, v = uc[..., :D_HYENA], uc[..., D_HYENA:2 * D_HYENA], uc[..., 2 * D_HYENA:]
    v = (v * x1).astype(jnp.float32)
    h_fwd, h_bwd = hyena_filters(L, w1, b1, w2, b2, w3, b3, w4, freq)
    y = two_sided_fftconv(v, h_fwd, h_bwd) + v * d_skip.astype(jnp.float32)
    return (y * x0.astype(jnp.float32)).astype(p.dtype)


def conformer_branch(p, dw_w, dw_b, ln_g, ln_b, pw_w, pw_b):
    a, b = p[..., :D_CONF], p[..., D_CONF:]
    u = a * jax.nn.sigmoid(b)
    half = (CONF_K - 1) // 2
    u = dwconv(u, dw_w, dw_b, half, half)
    u = jax.nn.silu(layernorm(u, ln_g, ln_b))
    return jnp.einsum('blc,ce->ble', u, pw_w) + pw_b


def linear_scan(a, b, reverse):
    def comb(e1, e2):
        a1, b1 = e1
        a2, b2 = e2
        return a1 * a2, a2 * b1 + b2
    return lax.associative_scan(comb, (a, b), reverse=reverse, axis=1)[1]


def rglru_direction(xr, wa, ba, wx, bx, lam, reverse):
    B, L, _ = xr.shape
    xb = xr.reshape(B, L, RG_HEADS, GROUP_WIDTH)
    gate_a = jnp.einsum('blhi,hij->blhj', xb, wa).reshape(B, L, D_RG) + ba
    gate_x = jnp.einsum('blhi,hij->blhj', xb, wx).reshape(B, L, D_RG) + bx
    r = jax.nn.sigmoid(gate_a.astype(jnp.float32))
    i = jax.nn.sigmoid(gate_x.astype(jnp.float32))
    log_a = -RG_C * r * jax.nn.softplus(-lam.astype(jnp.float32))
    a = jnp.exp(log_a)
    mult = jnp.sqrt(-jnp.expm1(2.0 * log_a))
    bterm = mult * (i * xr.astype(jnp.float32))
    return linear_scan(a, bterm, reverse)


def rglru_branch(p, conv_w, conv_b, wa, ba, wx, bx, lam):
    xr = dwconv(p, conv_w, conv_b, RG_CONV_K // 2, RG_CONV_K - 1 - RG_CONV_K // 2)
    h_f = rglru_direction(xr, wa[0], ba[0], wx[0], bx[0], lam[0], False)
    h_b = rglru_direction(xr, wa[1], ba[1], wx[1], bx[1], lam[1], True)
    return (h_f + h_b).astype(p.dtype)


def mixer_layer(x, norm_g, w_in, hy_conv_w, hy_conv_b, hy_w1, hy_b1, hy_w2, hy_b2, hy_w3, hy_b3,
                hy_w4, hy_freq, hy_d, cf_dw_w, cf_dw_b, cf_ln_g, cf_ln_b, cf_pw_w, cf_pw_b,
                rg_conv_w, rg_conv_b, rg_wa, rg_ba, rg_wx, rg_bx, rg_lam, grp_g, w_out):
    h = rmsnorm(x, norm_g)
    p = jnp.einsum('bld,de->ble', h, w_in)
    s1 = 3 * D_HYENA
    s2 = s1 + 2 * D_CONF
    s3 = s2 + D_RG
    p_hy, p_cf, p_rg, gate = p[..., :s1], p[..., s1:s2], p[..., s2:s3], p[..., s3:]
    y_hy = hyena_branch(p_hy, hy_conv_w, hy_conv_b, hy_w1, hy_b1, hy_w2, hy_b2, hy_w3, hy_b3,
                        hy_w4, hy_freq, hy_d)
    y_cf = conformer_branch(p_cf, cf_dw_w, cf_dw_b, cf_ln_g, cf_ln_b, cf_pw_w, cf_pw_b)
    y_rg = rglru_branch(p_rg, rg_conv_w, rg_conv_b, rg_wa, rg_ba, rg_wx, rg_bx, rg_lam)
    y = jnp.concatenate([
        rmsnorm(y_hy, grp_g[:D_HYENA]),
        rmsnorm(y_cf, grp_g[D_HYENA:D_HYENA + D_CONF]),
        rmsnorm(y_rg, grp_g[D_HYENA + D_CONF:])], axis=-1)
    y = y * jax.nn.silu(gate)
    return jnp.einsum('ble,ed->bld', y, w_out)


def trunk(x, layer_params, final_g):
    for l in range(DEPTH):
        x = x + mixer_layer(x, *[w[l] for w in layer_params])
    return rmsnorm(x, final_g)


def setup_inputs(seed: int = 0) -> dict:
    key = jax.random.key(seed)
    ks = jax.random.split(key, 32)
    f32 = jnp.float32

    def nrm(k, shape, scale):
        return jax.random.normal(k, shape, f32) * scale

    u = jax.random.uniform(ks[28], (DEPTH, 2, D_RG), f32, minval=0.9, maxval=0.999)
    a0 = u ** (1.0 / RG_C)
    rg_lam = jnp.log(a0) - jnp.log1p(-a0)
    return {
        "x_prompt": nrm(ks[0], (BATCH, SEQ, D_MODEL), 1.0),
        "x_sample": nrm(ks[1], (DEC_BATCH, DEC_SEQ, D_MODEL), 1.0),
        "norm_g": 1.0 + nrm(ks[2], (DEPTH, D_MODEL), 0.02),
        "w_in": nrm(ks[3], (DEPTH, D_MODEL, IN_COLS), D_MODEL ** -0.5),
        "hy_conv_w": nrm(ks[4], (DEPTH, HYENA_SHORT_K, 3 * D_HYENA), HYENA_SHORT_K ** -0.5),
        "hy_conv_b": nrm(ks[5], (DEPTH, 3 * D_HYENA), 0.02),
        "hy_w1": nrm(ks[6], (DEPTH, HYENA_EMB, HYENA_FILTER_ORDER), HYENA_EMB ** -0.5),
        "hy_b1": nrm(ks[7], (DEPTH, HYENA_FILTER_ORDER), 0.02),
        "hy_w2": nrm(ks[8], (DEPTH, HYENA_FILTER_ORDER, HYENA_FILTER_ORDER), HYENA_FILTER_ORDER ** -0.5),
        "hy_b2": nrm(ks[9], (DEPTH, HYENA_FILTER_ORDER), 0.02),
        "hy_w3": nrm(ks[10], (DEPTH, HYENA_FILTER_ORDER, HYENA_FILTER_ORDER), HYENA_FILTER_ORDER ** -0.5),
        "hy_b3": nrm(ks[11], (DEPTH, HYENA_FILTER_ORDER), 0.02),
        "hy_w4": nrm(ks[12], (DEPTH, HYENA_FILTER_ORDER, 2 * D_HYENA), HYENA_FILTER_ORDER ** -0.5),
        "hy_freq": 1.0 + nrm(ks[13], (DEPTH, HYENA_FILTER_ORDER), 0.02),
        "hy_d": nrm(ks[14], (DEPTH, D_HYENA), 0.1),
        "cf_dw_w": nrm(ks[15], (DEPTH, CONF_K, D_CONF), CONF_K ** -0.5),
        "cf_dw_b": nrm(ks[16], (DEPTH, D_CONF), 0.02),
        "cf_ln_g": 1.0 + nrm(ks[17], (DEPTH, D_CONF), 0.02),
        "cf_ln_b": nrm(ks[18], (DEPTH, D_CONF), 0.02),
        "cf_pw_w": nrm(ks[19], (DEPTH, D_CONF, D_CONF), D_CONF ** -0.5),
        "cf_pw_b": nrm(ks[20], (DEPTH, D_CONF), 0.02),
        "rg_conv_w": nrm(ks[21], (DEPTH, RG_CONV_K, D_RG), RG_CONV_K ** -0.5),
        "rg_conv_b": nrm(ks[22], (DEPTH, D_RG), 0.02),
        "rg_wa": nrm(ks[23], (DEPTH, 2, RG_HEADS, GROUP_WIDTH, GROUP_WIDTH), GROUP_WIDTH ** -0.5),
        "rg_ba": nrm(ks[24], (DEPTH, 2, D_RG), 0.02),
        "rg_wx": nrm(ks[25], (DEPTH, 2, RG_HEADS, GROUP_WIDTH, GROUP_WIDTH), GROUP_WIDTH ** -0.5),
        "rg_bx": nrm(ks[26], (DEPTH, 2, D_RG), 0.02),
        "rg_lam": rg_lam,
        "grp_g": 1.0 + nrm(ks[27], (DEPTH, D_MIX), 0.02),
        "w_out": nrm(ks[29], (DEPTH, D_MIX, D_MODEL), D_MIX ** -0.5),
        "final_g": 1.0 + nrm(ks[30], (D_MODEL,), 0.02),
    }


def reference(x_prompt, x_sample, norm_g, w_in, hy_conv_w, hy_conv_b, hy_w1, hy_b1, hy_w2, hy_b2,
              hy_w3, hy_b3, hy_w4, hy_freq, hy_d, cf_dw_w, cf_dw_b, cf_ln_g, cf_ln_b, cf_pw_w,
              cf_pw_b, rg_conv_w, rg_conv_b, rg_wa, rg_ba, rg_wx, rg_bx, rg_lam, grp_g, w_out,
              final_g):
    layer_params = (norm_g, w_in, hy_conv_w, hy_conv_b, hy_w1, hy_b1, hy_w2, hy_b2, hy_w3, hy_b3,
                    hy_w4, hy_freq, hy_d, cf_dw_w, cf_dw_b, cf_ln_g, cf_ln_b, cf_pw_w, cf_pw_b,
                    rg_conv_w, rg_conv_b, rg_wa, rg_ba, rg_wx, rg_bx, rg_lam, grp_g, w_out)
    y_prompt = trunk(x_prompt, layer_params, final_g)
    y_sample = trunk(x_sample, layer_params, final_g)
    return (y_prompt, y_sample)
```

```python
import math
import contextlib
import numpy as np
import ml_dtypes
import concourse.bass as bass
import concourse.mybir as mybir
from concourse.bass_utils import run_bass_kernel_spmd

F32 = mybir.dt.float32
BF16 = mybir.dt.bfloat16
F32R = mybir.dt.float32r
AF = mybir.ActivationFunctionType
ALU = mybir.AluOpType

L = 8192
D = 1024
DEPTH = 2
EPS = 1e-6
NCH = 32
ENGS = ["pe", "act", "dve", "pool", "sp"]

HCW, HCB, HD, CDW, CDB, CLG, CLB, CPB = 0, 27, 36, 39, 101, 103, 105, 107
RCW, RCB, RBA, RBX, RLAM, GG, NG, MLPB, DEL = 109, 121, 124, 130, 136, 142, 150, 158, 162
CL, CL2, HFR, QFR, HFB, QFB = 165, 171, 177, 178, 179, 182
NCP = 186


class Prog:
    def __init__(self, nc):
        self.nc = nc
        self.ops = []

    def op(self, eng, fn, reads=(), writes=(), dma=None):
        self.ops.append((eng, fn, tuple(reads), tuple(writes), dma))

    def barrier(self):
        self.ops.append(("barrier", None, (), (), None))

    def dma(self, out, in_, reads=(), writes=(), slot=None, eng="sp", **kw):
        def fn(e, out=out, in_=in_, kw=kw):
            return e.dma_start(out=out, in_=in_, **kw)
        self.op(eng, fn, reads, writes, dma=slot)

    def emit(self, stack):
        nc = self.nc
        esem = {e: stack.enter_context(nc.semaphore("s_" + e)) for e in ENGS}
        slots = []
        seen = set()
        for o in self.ops:
            if o[4] is not None and o[4] not in seen:
                seen.add(o[4])
                slots.append(o[4])
        dsem = {s: stack.enter_context(nc.semaphore("d_%d" % i)) for i, s in enumerate(slots)}
        ecount = {e: 0 for e in ENGS}
        dcount = {s: 0 for s in slots}
        last_w = {}
        readers = {}
        streams = {e: [] for e in ENGS}
        waited = {e: {} for e in ENGS}
        pending_barrier = {e: None for e in ENGS}
        for (eng, fn, reads, writes, dma) in self.ops:
            if eng == "barrier":
                snap = [("e", e2, ecount[e2], e2) for e2 in ENGS if ecount[e2] > 0]
                snap += [("d", s, dcount[s], None) for s in slots if dcount[s] > 0]
                for e in ENGS:
                    pending_barrier[e] = snap
                continue
            deps = []
            if pending_barrier[eng] is not None:
                deps.extend(pending_barrier[eng])
                pending_barrier[eng] = None
            for k in reads:
                if k in last_w:
                    deps.append(last_w[k])
            for k in writes:
                if k in last_w:
                    deps.append(last_w[k])
                for (kk, ss), (vv, pe_) in readers.get(k, {}).items():
                    deps.append((kk, ss, vv, pe_))
            if dma is not None and dcount[dma] > 0:
                deps.append(("d", dma, dcount[dma], None))
            waits = []
            w = waited[eng]
            for (kind, sid, val, peng) in deps:
                if kind == "e" and peng == eng and eng == "pe":
                    continue
                if w.get((kind, sid), 0) >= val:
                    continue
                w[(kind, sid)] = val
                waits.append((kind, sid, val))
            if dma is not None:
                dcount[dma] += 16
                tok = ("d", dma, dcount[dma], None)
            else:
                ecount[eng] += 1
                tok = ("e", eng, ecount[eng], eng)
            streams[eng].append((waits, fn, tok))
            for k in writes:
                last_w[k] = tok
                readers[k] = {}
            for k in reads:
                if k in writes:
                    continue
                rd = readers.setdefault(k, {})
                if rd.get((tok[0], tok[1]), (0, None))[0] < tok[2]:
                    rd[(tok[0], tok[1])] = (tok[2], tok[3])
        final_waits = [("d", s, dcount[s]) for s in slots if dcount[s] > 0]
        self.stats = {e: len(streams[e]) for e in ENGS}

        def semof(kind, sid):
            return esem[sid] if kind == "e" else dsem[sid]

        with nc.Block() as block:
            def make(e):
                def body(engine):
                    for (waits, fn, tok) in streams[e]:
                        for (kind, sid, val) in waits:
                            engine.wait_ge(semof(kind, sid), val)
                        ins = fn(engine)
                        if tok[0] == "e":
                            ins.then_inc(esem[e], 1)
                        else:
                            ins.then_inc(dsem[tok[1]], 16)
                    if e == "sp":
                        for (kind, sid, val) in final_waits:
                            engine.wait_ge(semof(kind, sid), val)
                        for e2 in ENGS:
                            if e2 != "sp" and ecount[e2] > 0:
                                engine.wait_ge(esem[e2], ecount[e2])
                return body
            block.tensor(make("pe"))
            block.scalar(make("act"))
            block.vector(make("dve"))
            block.gpsimd(make("pool"))
            block.sync(make("sp"))


class Arena:
    def __init__(self, ap32):
        self.ap = ap32
        self.W = ap32.shape[1]
        self.off = 0

    def reset(self, off):
        self.off = off

    def f32(self, cols):
        v = self.ap[:, self.off:self.off + cols]
        self.off += cols
        assert self.off <= self.W, (self.off, self.W)
        return v

    def bf16(self, cols):
        w = (cols + 1) // 2
        v = self.ap[:, self.off:self.off + w].bitcast(BF16)
        self.off += w
        assert self.off <= self.W, (self.off, self.W)
        return v


class Builder:
    def __init__(self, dbg=False):
        self.dbg = dbg
        self.nc = nc = bass.Bass("TRN2", target_bir_lowering=False)
        self.P = Prog(nc)
        self.uid = 0
        di = lambda n, s, dt=F32: nc.dram_tensor(n, list(s), dt, kind="ExternalInput").ap()
        do = lambda n, s, dt=F32: nc.dram_tensor(n, list(s), dt, kind="ExternalOutput").ap()
        sk = "ExternalOutput" if dbg else "Internal"
        ds = lambda n, s, dt=BF16: nc.dram_tensor(n, list(s), dt, kind=sk).ap()
        self.x_d = [di("xa", [L, D]), di("xb", [L, D])]
        self.y_d = [do("ya", [L, D]), do("yb", [L, D])]
        self.w_in_d = di("w_in", [DEPTH, 128, 8, 3072])
        self.w_out_d = di("w_out", [DEPTH, 128, 8, 1024])
        self.cp_d = di("cp", [DEPTH, 128, NCP])
        self.fg_d = di("fg", [128, D])
        self.w1_d = di("hw1", [DEPTH, 33, 64])
        self.w2_d = di("hw2", [DEPTH, 64, 64])
        self.w3_d = di("hw3", [DEPTH, 64, 64])
        self.w4_d = di("hw4", [DEPTH, 64, 768])
        self.zT_d = di("zT", [33, L])
        self.tt_d = di("tt", [128, L])
        self.rgw_d = di("rgw", [DEPTH, 3, 128, 4, 128])
        self.pww_d = di("pww", [DEPTH, 128, 2, 256])
        self.ident_d = di("ident", [128, 128])
        self.F1_d = di("F1", [128, 384], BF16)
        self.F1b_d = di("F1b", [64, 384], BF16)
        self.GT_d = di("GT", [128, 128 * 2 * 128], BF16)
        self.FIa_d = di("FIa", [128, 256], BF16)
        self.FIb_d = di("FIb", [128, 256], BF16)
        self.P_d = ds("P_s", [3072, L])
        self.U_d = ds("U_s", [384, L])
        self.X0_d = ds("X0_s", [384, L])
        self.YC_d = ds("YC_s", [384, L])
        self.YCF_d = ds("YCF_s", [256, L])
        self.YRG_d = ds("YRG_s", [384, L])
        self.YRGB_d = ds("YRGB_s", [384, L])
        self.GS_d = ds("GS_s", [1024, L])
        self.HT_d = ds("HT_s", [768, L])
        self.KF_d = ds("KF_s", [12, 128, 128 * 2 * NCH])
        self.X1_d = nc.dram_tensor("X1_s", [2, L, D], F32, kind=sk).ap()

    def ts(self, eng, out, in0, s1, op0, s2=None, op1=None, r=(), w=()):
        if eng == "pool" and op1 is None:
            s2, op1 = 0.0, ALU.add
        if op1 is None:
            fn = lambda e: e.tensor_scalar(out=out, in0=in0, scalar1=s1, scalar2=None, op0=op0)
        else:
            fn = lambda e: e.tensor_scalar(out=out, in0=in0, scalar1=s1, scalar2=s2, op0=op0, op1=op1)
        self.P.op(eng, fn, r, w)

    def stt(self, out, in0, scalar, in1, op0, op1, r=(), w=()):
        self.P.op("dve", lambda e: e.scalar_tensor_tensor(out=out, in0=in0, scalar=scalar, in1=in1,
                                                          op0=op0, op1=op1), r, w)

    def tt(self, eng, out, in0, in1, op, r=(), w=()):
        self.P.op(eng, lambda e: e.tensor_tensor(out=out, in0=in0, in1=in1, op=op), r, w)

    def act(self, out, in_, func, r=(), w=(), scale=None, bias=None, accum=None):
        kw = {}
        if scale is not None:
            kw["scale"] = scale
        if bias is not None:
            kw["bias"] = bias
        if accum is not None:
            kw["accum_out"] = accum
        self.P.op("act", lambda e: e.activation(out=out, in_=in_, func=func, **kw), r, w)

    def cpy(self, eng, out, in_, r=(), w=()):
        if eng == "act":
            self.P.op("act", lambda e: e.activation(out=out, in_=in_, func=AF.Copy), r, w)
        else:
            self.P.op(eng, lambda e: e.tensor_copy(out=out, in_=in_), r, w)

    def mm(self, out, lhsT, rhs, start, stop, r=(), w=()):
        self.P.op("pe", lambda e: e.matmul(out=out, lhsT=lhsT, rhs=rhs, start=start, stop=stop), r, w)

    def tr(self, out, in_, ident, r=(), w=()):
        self.P.op("pe", lambda e: e.transpose(out=out, in_=in_, identity=ident), r, w)

    def memset(self, eng, ap, val, r=(), w=()):
        self.P.op(eng, lambda e: e.memset(ap, val), r, w)

    def recip(self, out, in_, r=(), w=()):
        self.P.op("dve", lambda e: e.reciprocal(out=out, in_=in_), r, w)

    def k(self, name):
        self.uid += 1
        return "%s#%d" % (name, self.uid)

    def build(self):
        nc = self.nc
        with contextlib.ExitStack() as st:
            arena_t = st.enter_context(nc.sbuf_tensor("arena", [128, 50 * 1024], F32))
            psA = st.enter_context(nc.psum_tensor("psA", [128, 8, 512], F32))
            self.psA = psA
            self.pbv = [psA[:, 6 + i, :].bitcast(BF16) for i in range(2)]
            self.A = A = Arena(arena_t[:])
            self.ident_f = A.f32(128)
            self.ident_b = A.bf16(128)
            self.ones_f = A.f32(128)
            self.ones_b = A.bf16(128)
            self.negh = A.f32(2)
            self.cp = [A.f32(NCP), A.f32(NCP)]
            self.fg = A.f32(D)
            self.F1 = A.bf16(384)
            self.F1b = A.bf16(384)
            self.FIa = A.bf16(256)
            self.FIb = A.bf16(256)
            self.persist_end = A.off
            self.setup()
            stop = getattr(self, "stop_after", None)
            done = False
            for l in range(DEPTH):
                if done:
                    break
                self.P.barrier()
                self.filter_gen(l)
                self.P.barrier()
                self.filter_fft(l)
                for s in range(2):
                    xsrc = self.x_d[s] if l == 0 else self.X1_d[s]
                    for nm, fn in [("p1", lambda: self.phase1(l, xsrc)), ("conf", lambda: self.phase2_conf(l)),
                                   ("rg", lambda: self.phase2_rg(l)), ("fft", lambda: self.phase3_fft(l)),
                                   ("p4", lambda: self.phase4(l, s, xsrc))]:
                        self.P.barrier()
                        fn()
                        if stop == nm:
                            done = True
                            break
                    if done:
                        break
            self.P.emit(st)
        return nc

    def setup(self):
        P = self.P
        P.dma(self.ident_f, self.ident_d, writes=["ident_f"], slot="c0")
        P.dma(self.fg, self.fg_d, writes=["fg"], slot="c1")
        P.dma(self.F1, self.F1_d, writes=["F1"], slot="c2")
        P.dma(self.F1b[0:64, :], self.F1b_d, writes=["F1"], slot="c3")
        P.dma(self.FIa, self.FIa_d, writes=["FI"], slot="c4")
        P.dma(self.FIb, self.FIb_d, writes=["FI"], slot="c5")
        self.cpy("dve", self.ident_b, self.ident_f, r=["ident_f"], w=["ident_b"])
        self.memset("dve", self.ones_f, 1.0, w=["ones"])
        self.memset("dve", self.ones_b, 1.0, w=["ones"])
        self.memset("dve", self.negh, -0.5, w=["negh"])
        for l in range(DEPTH):
            cp = self.cp[l]
            ck = "cp%d" % l
            P.dma(cp[:, 0:DEL + 3], self.cp_d[l][:, 0:DEL + 3], writes=[ck], slot="c6")
            self.act(cp[:, CL:CL + 6], cp[:, RLAM:RLAM + 6], AF.Exp, r=[ck], w=[ck], scale=-1.0)
            self.act(cp[:, CL:CL + 6], cp[:, CL:CL + 6], AF.Ln, r=[ck], w=[ck], bias=1.0)
            self.ts("dve", cp[:, CL2:CL2 + 6], cp[:, CL:CL + 6], -16.0, ALU.mult, r=[ck], w=[ck])
            self.ts("dve", cp[:, CL:CL + 6], cp[:, CL:CL + 6], -8.0, ALU.mult, r=[ck], w=[ck])
            fr = cp[:, MLPB + 3:MLPB + 4]
            self.ts("dve", cp[:, HFR:HFR + 1], fr, 0.5, ALU.mult, r=[ck], w=[ck])
            self.ts("dve", cp[:, QFR:QFR + 1], fr, 0.25, ALU.mult, r=[ck], w=[ck])
            self.ts("dve", cp[:, HFB:HFB + 3], cp[:, MLPB:MLPB + 3], cp[:, HFR:HFR + 1], ALU.mult, r=[ck], w=[ck])
            self.ts("dve", cp[:, QFB:QFB + 3], cp[:, MLPB:MLPB + 3], cp[:, QFR:QFR + 1], ALU.mult, r=[ck], w=[ck])

    def filter_gen(self, l):
        P, A, cp = self.P, self.A, self.cp[l]
        ck = "cp%d" % l
        A.reset(self.persist_end)
        zT = A.f32(L)
        tt = A.f32(L)
        w1 = A.f32(64)
        w2 = A.f32(64)
        w3 = A.f32(64)
        w4 = A.f32(768)
        w4b = A.bf16(768)
        P.dma(zT[0:33, :], self.zT_d, writes=["zT"], slot="f0")
        P.dma(tt, self.tt_d, writes=["tt"], slot="f1")
        P.dma(w1[0:33, :], self.w1_d[l], writes=["fw"], slot="f2")
        P.dma(w2[0:64, :], self.w2_d[l], writes=["fw"], slot="f3")
        P.dma(w3[0:64, :], self.w3_d[l], writes=["fw"], slot="f4")
        P.dma(w4[0:64, :], self.w4_d[l], writes=["fw"], slot="f5")
        self.cpy("dve", w4b[0:64, :], w4[0:64, :], r=["fw"], w=["fw4b"])
        sets = []
        for i in range(2):
            sets.append(dict(s2=A.f32(512), s4=A.f32(512), hd=[A.f32(512), A.f32(512), A.bf16(512)],
                             dec=A.f32(3 * 512), stg=A.bf16(6 * 512)))
        HTv = self.HT_d.rearrange("(j p) t -> p j t", p=128)
        zcol = A.bf16(8).rearrange("p (j t) -> p j t", j=4)
        self.memset("dve", zcol, 0.0, w=["zcol"])
        P.dma(HTv[:, 3:6, 0:1], zcol[:, 0:3, 0:1], reads=["zcol"], writes=["HTz"], slot="fz",
              allow_slow_non_contiguous=True)
        wts = [w1[0:33, :], w2[0:64, :], w3[0:64, :]]
        for pair in range(8):
            tiles = [2 * pair, 2 * pair + 1]
            prev = {ti: zT[0:33, ti * 512:(ti + 1) * 512] for ti in tiles}
            for li in range(3):
                for ti in tiles:
                    S = sets[ti % 2]
                    sk = "fs%d" % (ti % 2)
                    ps = self.psA[0:64, ti % 2, :]
                    pk = "ps%d" % (ti % 2)
                    self.mm(ps, wts[li], prev[ti], True, True, r=["zT", "fw", sk + "hd"], w=[pk])
                    self.act(S["s2"][0:64, :], ps, AF.Sin, r=[pk, ck], w=[sk + "s2"],
                             scale=cp[0:64, HFR:HFR + 1], bias=cp[0:64, HFB + li:HFB + li + 1])
                    self.act(S["s4"][0:64, :], ps, AF.Sin, r=[pk, ck], w=[sk + "s4"],
                             scale=cp[0:64, QFR:QFR + 1], bias=cp[0:64, QFB + li:QFB + li + 1])
                    self.tt("dve", S["s4"][0:64, :], S["s4"][0:64, :], S["s4"][0:64, :], ALU.mult,
                            r=[sk + "s4"], w=[sk + "s4"])
                    self.ts("dve", S["s4"][0:64, :], S["s4"][0:64, :], -4.0, ALU.mult, 2.0, ALU.add,
                            r=[sk + "s4"], w=[sk + "s4"])
                    self.tt("dve", S["hd"][li][0:64, :], S["s4"][0:64, :], S["s2"][0:64, :], ALU.mult,
                            r=[sk + "s4", sk + "s2"], w=[sk + "hd"])
                    prev[ti] = S["hd"][li][0:64, :]
            for ti in tiles:
                S = sets[ti % 2]
                sk = "fs%d" % (ti % 2)
                t0 = ti * 512
                dec3 = S["dec"].rearrange("p (j t) -> p j t", j=3)
                for j in range(3):
                    self.act(dec3[:, j, :], tt[:, t0:t0 + 512], AF.Exp, r=["tt", ck], w=[sk + "dec"],
                             scale=cp[:, DEL + j:DEL + j + 1])
            for ti in tiles:
                S = sets[ti % 2]
                sk = "fs%d" % (ti % 2)
                t0 = ti * 512
                dec3 = S["dec"].rearrange("p (j t) -> p j t", j=3)
                stg = S["stg"].rearrange("p (j t) -> p j t", j=6)
                for j in range(6):
                    ps = self.psA[:, 2 + j % 4, :]
                    pk = "ps%d" % (2 + j % 4)
                    self.mm(ps, w4b[0:64, j * 128:(j + 1) * 128], prev[ti], True, True, r=["fw4b", sk + "hd"], w=[pk])
                    o_ = stg[:, j, :] if j < 3 else stg[:, j, ::-1]
                    self.tt("dve", o_, ps, dec3[:, j % 3, :], ALU.mult, r=[pk, sk + "dec"], w=[sk + "stg%d" % j])
                P.dma(HTv[:, 0:3, t0:t0 + 512], stg[:, 0:3, :], reads=[sk + "stg%d" % j for j in range(3)],
                      writes=["HTf%d" % ti], slot=sk + "o")
                J0 = 7681 - t0
                bk = [sk + "stg%d" % j for j in range(3, 6)]
                if ti == 0:
                    P.dma(HTv[:, 3:6, J0:J0 + 511], stg[:, 3:6, 0:511], reads=bk, writes=["HTb%d" % ti], slot=sk + "ob")
                else:
                    P.dma(HTv[:, 3:6, J0:J0 + 512], stg[:, 3:6, :], reads=bk, writes=["HTb%d" % ti] + (["HTz"] if ti == 15 else []),
                          slot=sk + "ob")

    def load_gt(self):
        GT = self.A.bf16(128 * 2 * 128)
        self.P.dma(GT, self.GT_d, writes=["GT"], slot="gt")
        return GT.rearrange("p (q pl r) -> p q pl r", pl=2, r=128)

    def fft_s1(self, Ux, F1, A4, akeys, ukey, tag, K=64):
        ukeys = ukey if isinstance(ukey, list) else [ukey]
        for c in range(NCH):
            b = c % 4
            ps = self.psA[:, b, 0:384]
            self.mm(ps, Ux[0:K, c, :], F1[0:K, :], True, True, r=ukeys + ["F1"], w=["ps%d" % b])
            eng = "dve" if c % 4 == 3 else "act"
            self.cpy(eng, A4[:, :, c, :], ps.rearrange("p (q pl) -> p q pl", pl=3),
                     r=["ps%d" % b], w=[akeys[c]])

    def filter_fft(self, l):
        P, A = self.P, self.A
        A.reset(self.persist_end)
        GT4 = self.load_gt()
        Af = [A.bf16(128 * 3 * NCH) for _ in range(2)]
        KFt = [A.bf16(128 * 2 * NCH) for _ in range(2)]
        Uf = [A.bf16(NCH * 128).rearrange("p (c l) -> p c l", c=NCH) for _ in range(2)]
        HTr = self.HT_d.rearrange("c (hi lo) -> hi c lo", lo=128)
        htk = ["HTf%d" % t for t in range(16)] + ["HTb%d" % t for t in range(16)] + ["HTz"]

        def loads(ch):
            c0 = ch * NCH
            P.dma(Uf[ch % 2][0:64], HTr[:, c0:c0 + NCH, :], reads=htk, writes=["Uf%da" % (ch % 2)], slot="Uf%da" % (ch % 2))
            P.dma(Uf[ch % 2][64:128], HTr[:, 384 + c0:384 + c0 + NCH, :], reads=htk, writes=["Uf%db" % (ch % 2)],
                  slot="Uf%db" % (ch % 2))

        loads(0)
        for ch in range(12):
            if ch + 1 < 12:
                loads(ch + 1)
            Af4 = Af[ch % 2].rearrange("p (q c pl) -> p q c pl", pl=3, c=NCH)
            afk = ["Af%d_%d" % (ch % 2, c) for c in range(NCH)]
            self.fft_s1(Uf[ch % 2], self.F1, Af4, afk, ["Uf%da" % (ch % 2), "Uf%db" % (ch % 2)], "f", K=128)
            kfk = "KFt%d" % (ch % 2)
            KF4 = KFt[ch % 2].rearrange("p (q c pl) -> p q c pl", pl=2, c=NCH)
            for kg in range(16):
                b = 4 + kg % 4
                pk = "ps%d" % b
                for q8 in range(8):
                    q = kg * 8 + q8
                    o = self.psA[:, b, q8 * 64:(q8 + 1) * 64]
                    self.mm(o, GT4[:, q, 0, :], Af4[:, q, :, 0:2], True, False, r=["GT"] + afk, w=[pk])
                    self.mm(o, GT4[:, q, 1, :], Af4[:, q, :, 1:3], False, True, r=["GT"] + afk, w=[pk])
                X4 = self.psA[:, b, :].rearrange("p (q c pl) -> p q c pl", pl=2, c=NCH)
                if kg % 2 == 0:
                    self.act(KF4[:, kg * 8:(kg + 1) * 8, :, :], X4, AF.Copy, r=[pk], w=[kfk + "_%d" % kg],
                             scale=1.0 / 16384.0)
                else:
                    self.ts("dve", KF4[:, kg * 8:(kg + 1) * 8, :, :], X4, 1.0 / 16384.0, ALU.mult,
                            r=[pk], w=[kfk + "_%d" % kg])
            P.dma(self.KF_d[ch], KFt[ch % 2], reads=[kfk + "_%d" % kg for kg in range(16)], writes=["KF"], slot=kfk)

    def phase1(self, l, xsrc):
        P, A, cp = self.P, self.A, self.cp[l]
        ck = "cp%d" % l
        A.reset(self.persist_end)
        wbf = A.bf16(8 * 3072).rearrange("p (k n) -> p k n", k=8)
        wst = [A.f32(8 * 384).rearrange("p (k n) -> p k n", k=8) for _ in range(2)]
        xb = [A.f32(D) for _ in range(3)]
        junk = A.bf16(D)
        ssq = A.f32(64)
        sd = A.f32(64)
        rs = A.f32(64)
        xs = [A.bf16(D) for _ in range(2)]
        hT = [A.bf16(8 * 512).rearrange("p (k t) -> p k t", k=8) for _ in range(2)]
        pst = [A.bf16(24 * 512).rearrange("p (j t) -> p j t", j=24) for _ in range(2)]
        for it0 in range(2):
            P.dma(xb[it0 % 3], xsrc[it0 * 128:(it0 + 1) * 128, :], reads=["X1"], writes=["x%d" % (it0 % 3)],
                  slot="x%d" % (it0 % 3))
        for pc in range(8):
            stg = wst[pc % 2]
            key = "wst%d" % (pc % 2)
            P.dma(stg, self.w_in_d[l][:, :, pc * 384:(pc + 1) * 384], writes=[key], slot=key)
            for k in range(8):
                eng = "dve" if k % 2 == 0 else "pool"
                self.ts(eng, wbf[:, k, pc * 384:(pc + 1) * 384], stg[:, k, :], cp[:, NG + k:NG + k + 1], ALU.mult,
                        r=[key, ck], w=["wbf%d_%d" % (pc, k)])
        Pv = self.P_d.rearrange("(j p) t -> p j t", p=128)

        def load(it):
            xk = "x%d" % (it % 3)
            P.dma(xb[it % 3], xsrc[it * 128:(it + 1) * 128, :], reads=["X1"], writes=[xk], slot=xk)

        def pre_sub(it):
            xt = xb[it % 3]
            xk = "x%d" % (it % 3)
            self.act(junk, xt, AF.Square, r=[xk], w=["ss%d" % it, "junk"], accum=ssq[:, it:it + 1])
            self.ts("pool", sd[:, it:it + 1], ssq[:, it:it + 1], 1.0 / D, ALU.mult, EPS, ALU.add,
                    r=["ss%d" % it], w=["sd%d" % it])
            self.tt("pool", rs[:, it:it + 1], sd[:, it:it + 1], self.negh[:, 0:1], ALU.pow,
                    r=["sd%d" % it, "negh"], w=["rs%d" % it])
            self.ts("dve", xs[it % 2], xt, rs[:, it:it + 1], ALU.mult, r=[xk, "rs%d" % it], w=["xs%d" % (it % 2)])
            if it + 2 < 64:
                load(it + 2)

        def tr_sub(it):
            g, sub = it // 4, it % 4
            pbk = "ps%d" % (6 + it % 2)
            for k in range(8):
                self.tr(self.pbv[it % 2][:, k * 128:(k + 1) * 128], xs[it % 2][:, k * 128:(k + 1) * 128],
                        self.ident_b, r=["xs%d" % (it % 2), "ident_b"], w=[pbk])
            eng = "act" if it % 2 == 0 else "dve"
            self.cpy(eng, hT[g % 2][:, :, sub * 128:(sub + 1) * 128],
                     self.pbv[it % 2].rearrange("p (k t) -> p k t", k=8), r=[pbk], w=["hT%d_%d" % (g % 2, sub)])

        for sub in range(4):
            pre_sub(sub)
            tr_sub(sub)
        evi = 0
        for g in range(16):
            hks = ["hT%d_%d" % (g % 2, s_) for s_ in range(4)]
            pks = []
            plain = [j for j in range(24) if j not in (11, 12) and j < 16]
            sigs, sils = [11, 12], list(range(16, 24))
            order = plain + (sigs + sils if g % 2 == 0 else sils + sigs)
            for part in range(4):
                nit = (g + 1) * 4 + part
                if g + 1 < 16:
                    pre_sub(nit)
                for j in order[part * 6:part * 6 + 6]:
                    b = evi % 4
                    evi += 1
                    pk = "ps%d" % b
                    for k in range(8):
                        self.mm(self.psA[:, b, :], wbf[:, k, j * 128:(j + 1) * 128], hT[g % 2][:, k, :],
                                k == 0, k == 7, r=hks + ["wbf%d_%d" % (j // 3, k)], w=[pk])
                    pko = "pst%d_%d" % (g % 2, j)
                    pks.append(pko)
                    if j in sigs:
                        self.act(pst[g % 2][:, j, :], self.psA[:, b, :], AF.Sigmoid, r=[pk], w=[pko])
                    elif j >= 16:
                        self.act(pst[g % 2][:, j, :], self.psA[:, b, :], AF.Silu, r=[pk], w=[pko])
                    else:
                        self.cpy("dve", pst[g % 2][:, j, :], self.psA[:, b, :], r=[pk], w=[pko])
                if g + 1 < 16:
                    tr_sub(nit)
            P.dma(Pv[:, :, g * 512:(g + 1) * 512], pst[g % 2], reads=pks, writes=["P"], slot="pst%d" % (g % 2))

    def phase2_hyena(self, l):
        P, A, cp = self.P, self.A, self.cp[l]
        ck = "cp%d" % l
        A.reset(self.persist_end)
        PW = L + 4
        pin = [A.bf16(PW) for _ in range(3)]
        ub = A.bf16(L)
        x0b = A.bf16(L)
        tmp = [[A.f32(2048) for _ in range(3)] for _ in range(2)]
        for i in range(3):
            self.memset("dve", pin[i][:, 0:2], 0.0, w=["hhalo"])
            self.memset("dve", pin[i][:, L + 2:L + 4], 0.0, w=["hhalo"])

        def conv3(out_last, acc, pinap, j, t0, S, pkey, akey, okey):
            c = HCW + j * 3
            self.act(acc, pinap[:, 2 + t0 - 1:2 + t0 - 1 + S], AF.Identity, r=[pkey, "hhalo", ck], w=[akey],
                     scale=cp[:, c:c + 1], bias=cp[:, HCB + j:HCB + j + 1])
            self.stt(acc, pinap[:, 2 + t0:2 + t0 + S], cp[:, c + 1:c + 2], acc, ALU.mult, ALU.add,
                     r=[pkey, akey, ck], w=[akey])
            self.stt(out_last, pinap[:, 2 + t0 + 1:2 + t0 + 1 + S], cp[:, c + 2:c + 3], acc, ALU.mult, ALU.add,
                     r=[pkey, "hhalo", akey, ck], w=[okey])

        for i in range(3):
            for gi, jj in enumerate([i, 3 + i, 6 + i]):
                P.dma(pin[gi][:, 2:2 + L], self.P_d[jj * 128:(jj + 1) * 128, :], reads=["P"],
                      writes=["hpin%d" % gi], slot="hpin%d" % gi)
            for sg in range(4):
                t0 = sg * 2048
                T = tmp[sg % 2]
                tk = ["ht%d_%d" % (sg % 2, q) for q in range(3)]
                conv3(T[0], T[0], pin[1], 3 + i, t0, 2048, "hpin1", tk[0], tk[0])
                conv3(T[1], T[1], pin[2], 6 + i, t0, 2048, "hpin2", tk[1], tk[1])
                self.tt("dve", ub[:, t0:t0 + 2048], T[0], T[1], ALU.mult, r=[tk[0], tk[1]], w=["ub%d" % sg])
                conv3(x0b[:, t0:t0 + 2048], T[2], pin[0], i, t0, 2048, "hpin0", tk[2], "x0b%d" % sg)
            P.dma(self.U_d[i * 128:(i + 1) * 128, :], ub, reads=["ub%d" % s_ for s_ in range(4)],
                  writes=["U"], slot="ub")
            P.dma(self.X0_d[i * 128:(i + 1) * 128, :], x0b, reads=["x0b%d" % s_ for s_ in range(4)],
                  writes=["X0"], slot="x0b")

    def phase2_conf(self, l):
        P, A, cp = self.P, self.A, self.cp[l]
        ck = "cp%d" % l
        A.reset(self.persist_end)
        UW = L + 32
        ug = [A.bf16(UW) for _ in range(2)]
        dg = A.bf16(2 * 31 * 128).rearrange("p (j k m) -> p j k m", j=2, k=31)
        pww = A.f32(2 * 256).rearrange("p (k n) -> p k n", k=2)
        ycf = [A.bf16(L) for _ in range(2)]
        ab = [[A.bf16(2048) for _ in range(2)] for _ in range(2)]
        T = []
        for i in range(2):
            T.append(dict(uc=A.f32(1024).rearrange("p (j t) -> p j t", j=2),
                          sqb=A.bf16(1024).rearrange("p (j t) -> p j t", j=2),
                          sub=A.bf16(1024).rearrange("p (j t) -> p j t", j=2),
                          mean=A.f32(512), m2=A.f32(512), var=A.f32(512), rstd=A.f32(512),
                          ln=A.f32(1024).rearrange("p (j t) -> p j t", j=2)))
        HW_ = 516
        hin = [A.bf16(9 * HW_).rearrange("p (j t) -> p j t", j=9) for _ in range(2)]
        hdg = A.bf16(27 * 128).rearrange("p (k m) -> p k m", k=27)
        x1c = [A.f32(3 * 512).rearrange("p (j t) -> p j t", j=3) for _ in range(2)]
        uo = [A.bf16(3 * 512).rearrange("p (j t) -> p j t", j=3) for _ in range(2)]
        x0o = [A.bf16(3 * 512).rearrange("p (j t) -> p j t", j=3) for _ in range(2)]
        for k in range(27):
            self.ts("dve" if k % 2 == 0 else "pool", hdg[:, k, :], self.ident_f, cp[:, HCW + k:HCW + k + 1], ALU.mult,
                    r=["ident_f", ck], w=["hdg%d" % k])
        hdgk = ["hdg%d" % k for k in range(27)]
        Phy = self.P_d[0:1152, :].rearrange("(j p) t -> p j t", p=128)
        Uv = self.U_d.rearrange("(j p) t -> p j t", p=128)
        X0v = self.X0_d.rearrange("(j p) t -> p j t", p=128)
        hbank = [0]

        def hy_load(tl):
            t0 = tl * 512
            hk = "hin%d" % (tl % 2)
            H = hin[tl % 2]
            if tl == 0:
                self.memset("dve", H[:, :, 0:2], 0.0, w=[hk])
                P.dma(H[:, :, 1:514], Phy[:, :, 0:513], reads=["P"], writes=[hk], slot=hk)
            elif tl == 15:
                P.dma(H[:, :, 0:513], Phy[:, :, t0 - 1:L], reads=["P"], writes=[hk], slot=hk)
                self.memset("dve", H[:, :, 513:514], 0.0, r=[hk], w=[hk + "h"])
            else:
                P.dma(H[:, :, 0:514], Phy[:, :, t0 - 1:t0 + 513], reads=["P"], writes=[hk], slot=hk)

        def hy_tile(tl):
            t0 = tl * 512
            hk = "hin%d" % (tl % 2)
            H = hin[tl % 2]
            st_ = tl % 2
            for i in range(3):
                for jj, kind in ((3 + i, "x1"), (6 + i, "v"), (i, "x0")):
                    b = 6 + hbank[0] % 2
                    hbank[0] += 1
                    pk = "ps%d" % b
                    for k in range(3):
                        self.mm(self.psA[:, b, :], hdg[:, jj * 3 + k, :], H[:, jj, k:k + 512], k == 0, k == 2,
                                r=[hk, hk + "h"] + hdgk, w=[pk])
                    bcol = cp[:, HCB + jj:HCB + jj + 1]
                    if kind == "x1":
                        self.act(x1c[st_][:, i, :], self.psA[:, b, :], AF.Identity, r=[pk, ck], w=["x1c%d_%d" % (st_, i)],
                                 bias=bcol)
                    elif kind == "v":
                        self.stt(uo[st_][:, i, :], self.psA[:, b, :], bcol, x1c[st_][:, i, :], ALU.add, ALU.mult,
                                 r=[pk, ck, "x1c%d_%d" % (st_, i)], w=["uo%d_%d" % (st_, i)])
                    else:
                        self.act(x0o[st_][:, i, :], self.psA[:, b, :], AF.Identity, r=[pk, ck], w=["x0o%d_%d" % (st_, i)],
                                 bias=bcol)
            P.dma(Uv[:, :, t0:t0 + 512], uo[st_], reads=["uo%d_%d" % (st_, i) for i in range(3)], writes=["U"],
                  slot="uo%d" % st_)
            P.dma(X0v[:, :, t0:t0 + 512], x0o[st_], reads=["x0o%d_%d" % (st_, i) for i in range(3)], writes=["X0"],
                  slot="x0o%d" % st_)

        P.dma(pww, self.pww_d[l], writes=["pww"], slot="pww")
        pwb16 = A.bf16(2 * 256).rearrange("p (k n) -> p k n", k=2)
        self.cpy("dve", pwb16, pww, r=["pww"], w=["pwb16"])
        for j in range(2):
            self.memset("dve", ug[j][:, 0:16], 0.0, w=["chalo"])
            self.memset("dve", ug[j][:, L + 16:L + 32], 0.0, w=["chalo"])
            for k in range(31):
                eng = "dve" if k % 2 == 0 else "pool"
                self.ts(eng, dg[:, j, k, :], self.ident_f, cp[:, CDW + j * 31 + k:CDW + j * 31 + k + 1], ALU.mult,
                        r=["ident_f", ck], w=["dg"])
        n = 0
        for sg in range(4):
            t0 = sg * 2048
            for j in range(2):
                S = ab[n % 2]
                sk = "cab%d" % (n % 2)
                P.dma(S[0], self.P_d[(9 + j) * 128:(10 + j) * 128, t0:t0 + 2048], reads=["P"], writes=[sk + "a"], slot=sk + "a")
                P.dma(S[1], self.P_d[(11 + j) * 128:(12 + j) * 128, t0:t0 + 2048], reads=["P"], writes=[sk + "b"], slot=sk + "b")
                self.tt("dve", ug[j][:, 16 + t0:16 + t0 + 2048], S[0], S[1], ALU.mult,
                        r=[sk + "a", sk + "b"], w=["ug%d_%d" % (j, sg)])
                n += 1
        ugk = [["ug%d_%d" % (j, sg) for sg in range(4)] + ["chalo"] for j in range(2)]

        def conv_mm(tl):
            t0 = tl * 512
            segs = sorted(set(min(3, max(0, tt_ // 2048)) for tt_ in (t0 - 15, t0 + 526)))
            for j in range(2):
                pk = "ps%d" % j
                rk = ["ug%d_%d" % (j, sg) for sg in segs] + ["chalo", "dg"]
                for k in range(31):
                    self.mm(self.psA[:, j, :], dg[:, j, k, :], ug[j][:, 16 + t0 + k - 15:16 + t0 + k - 15 + 512],
                            k == 0, k == 30, r=rk, w=[pk])

        def conv_ev(tl):
            S = T[tl % 2]
            sk = "ct%d" % (tl % 2)
            for j in range(2):
                pk = "ps%d" % j
                self.act(S["uc"][:, j, :], self.psA[:, j, :], AF.Identity, r=[pk, ck], w=[sk + "uc%d" % j],
                         bias=cp[:, CDB + j:CDB + j + 1])
                self.act(S["sqb"][:, j, :], self.psA[:, j, :], AF.Square, r=[pk, ck], w=[sk + "sq%d" % j],
                         bias=cp[:, CDB + j:CDB + j + 1])

        def stats(tl):
            S = T[tl % 2]
            sk = "ct%d" % (tl % 2)
            for j in range(2):
                self.mm(self.psA[:, 2, :], self.ones_f, S["uc"][:, j, :], j == 0, j == 1,
                        r=["ones", sk + "uc%d" % j], w=["ps2"])
            for j in range(2):
                self.mm(self.psA[:, 3, :], self.ones_b, S["sqb"][:, j, :], j == 0, j == 1,
                        r=["ones", sk + "sq%d" % j], w=["ps3"])

        def chain(tl):
            S = T[tl % 2]
            sk = "ct%d" % (tl % 2)
            self.act(S["mean"], self.psA[:, 2, :], AF.Copy, r=["ps2"], w=[sk + "mean"], scale=1.0 / 256)
            self.act(S["m2"], self.psA[:, 2, :], AF.Square, r=["ps2"], w=[sk + "m2"], scale=1.0 / 256)
            self.stt(S["var"], self.psA[:, 3, :], 1.0 / 256, S["m2"], ALU.mult, ALU.subtract,
                     r=["ps3", sk + "m2"], w=[sk + "var"])
            self.act(S["var"], S["var"], AF.Ln, r=[sk + "var"], w=[sk + "var"], bias=EPS)
            self.act(S["rstd"], S["var"], AF.Exp, r=[sk + "var"], w=[sk + "rstd"], scale=-0.5)
            for j in range(2):
                self.tt("dve", S["ln"][:, j, :], S["uc"][:, j, :], S["mean"], ALU.subtract,
                        r=[sk + "uc%d" % j, sk + "mean"], w=[sk + "ln%d" % j])
                self.tt("dve", S["ln"][:, j, :], S["ln"][:, j, :], S["rstd"], ALU.mult,
                        r=[sk + "ln%d" % j, sk + "rstd"], w=[sk + "ln%d" % j])
                self.act(S["sub"][:, j, :], S["ln"][:, j, :], AF.Silu, r=[sk + "ln%d" % j, ck], w=[sk + "su%d" % j],
                         scale=cp[:, CLG + j:CLG + j + 1], bias=cp[:, CLB + j:CLB + j + 1])

        def pw(tl):
            t0 = tl * 512
            S = T[tl % 2]
            sk = "ct%d" % (tl % 2)
            for e_ in range(2):
                pk = "ps%d" % (4 + e_)
                for j in range(2):
                    self.mm(self.psA[:, 4 + e_, :], pwb16[:, j, e_ * 128:(e_ + 1) * 128], S["sub"][:, j, :],
                            j == 0, j == 1, r=["pwb16", sk + "su%d" % j], w=[pk])
                self.act(ycf[e_][:, t0:t0 + 512], self.psA[:, 4 + e_, :], AF.Identity, r=[pk, ck],
                         w=["ycf%d_%d" % (e_, tl)], bias=cp[:, CPB + e_:CPB + e_ + 1])

        hy_load(0)
        hy_load(1)
        conv_mm(0)
        conv_ev(0)
        for tl in range(16):
            stats(tl)
            if tl + 1 < 16:
                conv_mm(tl + 1)
            chain(tl)
            if tl + 1 < 16:
                conv_ev(tl + 1)
            pw(tl)
            hy_tile(tl)
            if tl + 2 < 16:
                hy_load(tl + 2)
        for e_ in range(2):
            P.dma(self.YCF_d[e_ * 128:(e_ + 1) * 128, :], ycf[e_], reads=["ycf%d_%d" % (e_, tl) for tl in range(16)],
                  writes=["YCF"], slot="ycf%d" % e_)

    def phase2_rg(self, l):
        P, A, cp = self.P, self.A, self.cp[l]
        ck = "cp%d" % l
        A.reset(self.persist_end)
        pin = A.bf16(L + 4)
        xr = A.bf16(L)
        rb = A.bf16(2 * L).rearrange("p (d t) -> p d t", d=2)
        ib = A.bf16(2 * L).rearrange("p (d t) -> p d t", d=2)
        ob = A.bf16(L)
        obb = A.bf16(L)
        gw = A.f32(4 * 128).rearrange("p (g m) -> p g m", g=4)
        gwb = A.bf16(4 * 128).rearrange("p (g m) -> p g m", g=4)
        rdg = A.bf16(4 * 128).rearrange("p (k m) -> p k m", k=4)
        T = [dict(a=A.f32(2048), m=A.f32(2048)) for _ in range(2)]
        self.memset("dve", pin[:, 0:2], 0.0, w=["rhalo"])
        self.memset("dve", pin[:, L + 2:L + 4], 0.0, w=["rhalo"])
        nb = 0
        for j in range(3):
            P.dma(pin[:, 2:2 + L], self.P_d[(13 + j) * 128:(14 + j) * 128, :], reads=["P"], writes=["rpin"], slot="rpin")
            P.dma(gw, self.rgw_d[l][j], writes=["gw0"], slot="gw")
            self.cpy("dve", gwb, gw, r=["gw0"], w=["gw"])
            c = RCW + j * 4
            for k in range(4):
                self.ts("dve", rdg[:, k, :], self.ident_f, cp[:, c + k:c + k + 1], ALU.mult, r=["ident_f", ck], w=["rdg"])
            for tl in range(16):
                t0 = tl * 512
                b = 6 + tl % 2
                pk = "ps%d" % b
                for k in range(4):
                    self.mm(self.psA[:, b, :], rdg[:, k, :], pin[:, t0 + k:t0 + k + 512], k == 0, k == 3,
                            r=["rpin", "rhalo", "rdg"], w=[pk])
                self.act(xr[:, t0:t0 + 512], self.psA[:, b, :], AF.Identity, r=[pk, ck], w=["xr%d" % tl],
                         bias=cp[:, RCB + j:RCB + j + 1])
            for d in range(2):
                for tl in range(16):
                    t0 = tl * 512
                    for gi in range(2):
                        b = nb % 6
                        nb += 1
                        pk = "ps%d" % b
                        self.mm(self.psA[:, b, :], gwb[:, d * 2 + gi, :], xr[:, t0:t0 + 512],
                                True, True, r=["gw", "xr%d" % tl], w=[pk])
                        dst = rb if gi == 0 else ib
                        bc = (RBA if gi == 0 else RBX) + d * 3 + j
                        self.act(dst[:, d, t0:t0 + 512], self.psA[:, b, :], AF.Sigmoid, r=[pk, ck],
                                 w=[("rb" if gi == 0 else "ib") + "%d_%d" % (d, tl)], bias=cp[:, bc:bc + 1])
            n = 0
            for d in range(2):
                order = [0, 1, 2, 3] if d == 0 else [3, 2, 1, 0]
                cc = d * 3 + j
                for oi, sg in enumerate(order):
                    t0 = sg * 2048
                    S = T[n % 2]
                    sk = "rt%d" % (n % 2)
                    n += 1
                    rks = ["rb%d_%d" % (d, sg * 4 + q) for q in range(4)]
                    iks = ["ib%d_%d" % (d, sg * 4 + q) for q in range(4)]
                    xks = ["xr%d" % (sg * 4 + q) for q in range(4)]
                    rr = rb[:, d, t0:t0 + 2048]
                    self.act(S["a"], rr, AF.Exp, r=rks + [ck], w=[sk + "a"], scale=cp[:, CL + cc:CL + cc + 1])
                    self.act(S["m"], rr, AF.Exp, r=rks + [ck], w=[sk + "m"], scale=cp[:, CL2 + cc:CL2 + cc + 1])
                    self.act(S["m"], S["m"], AF.Ln, r=[sk + "m"], w=[sk + "m"], scale=-1.0, bias=1.0)
                    self.act(S["m"], S["m"], AF.Exp, r=[sk + "m"], w=[sk + "m"], scale=0.5)
                    self.tt("dve", S["m"], S["m"], ib[:, d, t0:t0 + 2048], ALU.mult, r=[sk + "m"] + iks, w=[sk + "m"])
                    self.tt("dve", S["m"], S["m"], xr[:, t0:t0 + 2048], ALU.mult, r=[sk + "m"] + xks, w=[sk + "m"])
                    if d == 0:
                        init = 0.0 if oi == 0 else ob[:, t0 - 1:t0]
                        rd = [sk + "a", sk + "m"] + (["ob%d" % (sg - 1)] if oi > 0 else [])
                        o_, a_, m_ = ob[:, t0:t0 + 2048], S["a"], S["m"]
                        P.op("dve", lambda e, o_=o_, a_=a_, m_=m_, init=init: e.tensor_tensor_scan(
                            out=o_, data0=a_, data1=m_, initial=init, op0=ALU.mult, op1=ALU.add), rd, ["ob%d" % sg])
                    else:
                        init = 0.0 if oi == 0 else obb[:, t0 + 2048:t0 + 2049]
                        rd = [sk + "a", sk + "m"] + (["obb%d" % (sg + 1)] if oi > 0 else [])
                        o_, a_, m_ = obb[:, t0:t0 + 2048][:, ::-1], S["a"][:, ::-1], S["m"][:, ::-1]
                        P.op("dve", lambda e, o_=o_, a_=a_, m_=m_, init=init: e.tensor_tensor_scan(
                            out=o_, data0=a_, data1=m_, initial=init, op0=ALU.mult, op1=ALU.add), rd, ["obb%d" % sg])
            P.dma(self.YRG_d[j * 128:(j + 1) * 128, :], ob, reads=["ob%d" % s_ for s_ in range(4)],
                  writes=["YRG"], slot="ob")
            P.dma(self.YRGB_d[j * 128:(j + 1) * 128, :], obb, reads=["obb%d" % s_ for s_ in range(4)],
                  writes=["YRG"], slot="obb")

    def phase2_gate(self, l):
        P, A = self.P, self.A
        A.reset(self.persist_end)
        gi_ = [A.bf16(L) for _ in range(2)]
        go_ = [A.bf16(L) for _ in range(2)]
        for j in range(8):
            ik, ok = "gi%d" % (j % 2), "go%d" % (j % 2)
            P.dma(gi_[j % 2], self.P_d[(16 + j) * 128:(17 + j) * 128, :], reads=["P"], writes=[ik], slot=ik)
            self.act(go_[j % 2], gi_[j % 2], AF.Silu, r=[ik], w=[ok])
            P.dma(self.GS_d[j * 128:(j + 1) * 128, :], go_[j % 2], reads=[ok], writes=["GS"], slot=ok)

    def phase3_fft(self, l):
        P, A = self.P, self.A
        A.reset(self.persist_end)
        GT4 = self.load_gt()
        Aab = [A.bf16(128 * 3 * NCH) for _ in range(2)]
        Bb = A.bf16(128 * 2 * NCH)
        B4 = Bb.rearrange("p (q c pl) -> p q c pl", pl=2, c=NCH)
        Yy = A.bf16(128 * 2 * NCH)
        Y4 = Yy.rearrange("p (q pl c) -> p q pl c", pl=2, c=NCH)
        KFt = A.bf16(128 * 2 * NCH)
        KF4 = KFt.rearrange("p (q c pl) -> p q c pl", pl=2, c=NCH)
        Ux = A.bf16(NCH * 128).rearrange("p (c l) -> p c l", c=NCH)
        yb = A.bf16(NCH * 128).rearrange("p (c l) -> p c l", c=NCH)
        ta = [A.f32(512).rearrange("p (q c pl) -> p q c pl", pl=2, c=NCH) for _ in range(2)]
        tb = [A.f32(512).rearrange("p (q c pl) -> p q c pl", pl=2, c=NCH) for _ in range(2)]
        Ur = self.U_d.rearrange("c (hi lo) -> hi c lo", lo=128)
        YCr = self.YC_d.rearrange("c (hi lo) -> hi c lo", lo=128)

        def load_U(ch):
            P.dma(Ux[0:64], Ur[:, ch * NCH:(ch + 1) * NCH, :], reads=["U"], writes=["Ux"], slot="Ux")

        def load_KF(ch):
            P.dma(KFt, self.KF_d[ch], reads=["KF"], writes=["KFt"], slot="kfi")

        def akeys(ch):
            return ["A%d_%d" % (ch % 2, c) for c in range(NCH)]

        def S1(ch):
            A4 = Aab[ch % 2].rearrange("p (q c pl) -> p q c pl", pl=3, c=NCH)
            self.fft_s1(Ux, self.F1, A4, akeys(ch), "Ux", "d")

        def S2(ch):
            A4 = Aab[ch % 2].rearrange("p (q c pl) -> p q c pl", pl=3, c=NCH)
            ak = akeys(ch)
            for kg in range(16):
                b = 4 + kg % 4
                pk = "ps%d" % b
                for q8 in range(8):
                    q = kg * 8 + q8
                    o = self.psA[:, b, q8 * 64:(q8 + 1) * 64]
                    self.mm(o, GT4[:, q, 0, :], A4[:, q, :, 0:2], True, False, r=["GT"] + ak, w=[pk])
                    self.mm(o, GT4[:, q, 1, :], A4[:, q, :, 1:3], False, True, r=["GT"] + ak, w=[pk])
                X4 = self.psA[:, b, :].rearrange("p (q c pl) -> p q c pl", pl=2, c=NCH)
                tak, tbk = "ta%d" % (kg % 2), "tb%d" % (kg % 2)
                qs = slice(kg * 8, (kg + 1) * 8)
                self.tt("dve", ta[kg % 2], X4, KF4[:, qs, :, :], ALU.mult, r=[pk, "KFt"], w=[tak])
                self.tt("dve", tb[kg % 2], X4, KF4[:, qs, :, ::-1], ALU.mult, r=[pk, "KFt"], w=[tbk])
                self.tt("pool", Y4[:, qs, 0, :], ta[kg % 2][:, :, :, 0], ta[kg % 2][:, :, :, 1], ALU.subtract,
                        r=[tak], w=["Y%dr" % kg])
                self.tt("pool", Y4[:, qs, 1, :], tb[kg % 2][:, :, :, 0], tb[kg % 2][:, :, :, 1], ALU.add,
                        r=[tbk], w=["Y%di" % kg])

        def I12(ch):
            c0 = ch * NCH
            yall = ["Y%dr" % kg for kg in range(16)] + ["Y%di" % kg for kg in range(16)]
            bk = ["B%d" % c for c in range(NCH)]
            for c in range(NCH):
                b = c % 4
                ps = self.psA[:, b, 0:256]
                pk = "ps%d" % b
                self.mm(ps, Y4[:, :, 0, c], self.FIa, True, False, r=yall + ["FI"], w=[pk])
                self.mm(ps, Y4[:, :, 1, c], self.FIb, False, True, r=yall + ["FI"], w=[pk])
                eng = "dve" if c % 4 == 3 else "act"
                self.cpy(eng, B4[:, :, c, :], ps.rearrange("p (q pl) -> p q pl", pl=2), r=[pk], w=[bk[c]])
            ybw = []
            for lg in range(8):
                b = 4 + lg % 4
                pk = "ps%d" % b
                pv = self.psA[0:64, b, :].rearrange("p (c l) -> p c l", l=16)
                for l16 in range(16):
                    q = lg * 16 + l16
                    o = pv[:, :, l16]
                    self.mm(o, GT4[:, q, 0, 0:64], B4[:, q, :, 0], True, False, r=["GT"] + bk, w=[pk])
                    self.mm(o, GT4[:, q, 1, 0:64], B4[:, q, :, 1], False, True, r=["GT"] + bk, w=[pk])
                wk = "yb_%d" % lg
                ybw.append(wk)
                self.cpy("act" if lg % 2 == 0 else "dve", yb[0:64, :, lg * 16:(lg + 1) * 16], pv, r=[pk], w=[wk])
            P.dma(YCr[:, c0:c0 + NCH, :], yb[0:64], reads=ybw, writes=["YC"], slot="yb")

        load_U(0)
        load_KF(0)
        S1(0)
        for ch in range(12):
            S2(ch)
            if ch + 1 < 12:
                load_U(ch + 1)
                load_KF(ch + 1)
                S1(ch + 1)
            I12(ch)

    def phase4(self, l, s, xsrc):
        P, A, cp = self.P, self.A, self.cp[l]
        ck = "cp%d" % l
        last = (l == DEPTH - 1)
        A.reset(self.persist_end)
        wo = A.bf16(8 * D).rearrange("p (k n) -> p k n", k=8)
        wst = [A.f32(D) for _ in range(2)]
        LD = []
        for i in range(2):
            LD.append(dict(x0=A.bf16(3 * 512).rearrange("p (j t) -> p j t", j=3),
                           u=A.bf16(3 * 512).rearrange("p (j t) -> p j t", j=3),
                           yc=A.bf16(3 * 512).rearrange("p (j t) -> p j t", j=3),
                           ycf=A.bf16(2 * 512).rearrange("p (j t) -> p j t", j=2),
                           yrg=A.bf16(3 * 512).rearrange("p (j t) -> p j t", j=3),
                           yrgb=A.bf16(3 * 512).rearrange("p (j t) -> p j t", j=3),
                           gs=A.bf16(8 * 512).rearrange("p (j t) -> p j t", j=8),
                           xr=A.f32(4 * D).rearrange("p (s d) -> p s d", s=4)))
        yraw = A.bf16(3 * 512).rearrange("p (j t) -> p j t", j=3)
        yr32 = A.f32(3 * 512).rearrange("p (j t) -> p j t", j=3)
        yrs = A.bf16(3 * 512).rearrange("p (j t) -> p j t", j=3)
        sqb = A.bf16(8 * 512).rearrange("p (j t) -> p j t", j=8)
        gsr = A.bf16(8 * 512).rearrange("p (j t) -> p j t", j=8)
        rstd = A.f32(3 * 512).rearrange("p (j t) -> p j t", j=3)
        ym = [A.bf16(8 * 512).rearrange("p (j t) -> p j t", j=8) for _ in range(2)]
        xn = [A.f32(D) for _ in range(2)]
        junk = A.bf16(D)
        fss = A.f32(64)
        fsd = A.f32(64)
        frs = A.f32(64)
        yo = [A.f32(D) for _ in range(2)]
        wok = ["wo%d" % k for k in range(8)]

        def load_wo():
            for k in range(8):
                key = "wst%d" % (k % 2)
                P.dma(wst[k % 2], self.w_out_d[l][:, k, :], writes=[key], slot=key)
                self.ts("dve" if k % 2 == 0 else "pool", wo[:, k, :], wst[k % 2], cp[:, GG + k:GG + k + 1], ALU.mult,
                        r=[key, ck], w=["wo%d" % k])
        ydst = self.y_d[s] if last else self.X1_d[s]
        grp = [0, 0, 0, 1, 1, 2, 2, 2]
        gw = [384.0, 256.0, 384.0]
        gr = [(0, 3), (3, 5), (5, 8)]
        nbc = [0]

        def loads(tl):
            t0 = tl * 512
            S = LD[tl % 2]
            sk = "p4_%d" % (tl % 2)
            for nm, dten, nj in [("x0", self.X0_d, 3), ("u", self.U_d, 3), ("yc", self.YC_d, 3),
                                 ("ycf", self.YCF_d, 2), ("yrg", self.YRG_d, 3), ("yrgb", self.YRGB_d, 3),
                                 ("gs", self.P_d[2048:3072, :], 8)]:
                P.dma(S[nm], dten.rearrange("(j p) t -> p j t", p=128)[:, :, t0:t0 + 512],
                      reads=["X0", "U", "YC", "YCF", "YRG", "P"], writes=[sk + nm], slot=sk + nm)
            P.dma(S["xr"], xsrc[t0:t0 + 512, :].rearrange("(s p) d -> p s d", p=128), reads=["X1"],
                  writes=[sk + "xr"], slot=sk + "xr")

        def ysrc_of(tl):
            S = LD[tl % 2]
            sk = "p4_%d" % (tl % 2)
            ysrc = [(yraw[:, i, :], "yraw%d" % i) for i in range(3)]
            ysrc += [(S["ycf"][:, j, :], sk + "ycf") for j in range(2)]
            ysrc += [(yrs[:, j, :], "yrs%d" % j) for j in range(3)]
            return ysrc

        def E1(tl):
            S = LD[tl % 2]
            sk = "p4_%d" % (tl % 2)
            for i in range(3):
                self.stt(yr32[:, i, :], S["u"][:, i, :], cp[:, HD + i:HD + i + 1], S["yc"][:, i, :], ALU.mult, ALU.add,
                         r=[sk + "u", sk + "yc", ck], w=["yr32_%d" % i])
                self.tt("pool", yraw[:, i, :], yr32[:, i, :], S["x0"][:, i, :], ALU.mult,
                        r=["yr32_%d" % i, sk + "x0"], w=["yraw%d" % i])
            for j in range(3):
                self.tt("dve", yrs[:, j, :], S["yrg"][:, j, :], S["yrgb"][:, j, :], ALU.add,
                        r=[sk + "yrg", sk + "yrgb"], w=["yrs%d" % j])
            ysrc = ysrc_of(tl)
            for k in range(8):
                self.act(sqb[:, k, :], ysrc[k][0], AF.Square, r=[ysrc[k][1]], w=["sq%d" % k])

        def E2a(tl):
            for g in range(3):
                lo, hi = gr[g]
                for k in range(lo, hi):
                    self.mm(self.psA[:, g, :], self.ones_b, sqb[:, k, :], k == lo, k == hi - 1,
                            r=["ones", "sq%d" % k], w=["ps%d" % g])
                self.act(rstd[:, g, :], self.psA[:, g, :], AF.Ln, r=["ps%d" % g], w=["rstd%d" % g],
                         scale=1.0 / gw[g], bias=EPS)
                self.act(rstd[:, g, :], rstd[:, g, :], AF.Exp, r=["rstd%d" % g], w=["rstd%d" % g], scale=-0.5)

        def E2b1(tl):
            S = LD[tl % 2]
            sk = "p4_%d" % (tl % 2)
            for k in range(8):
                self.tt("pool" if k in (3, 6) else "dve", gsr[:, k, :], S["gs"][:, k, :], rstd[:, grp[k], :], ALU.mult,
                        r=[sk + "gs", "rstd%d" % grp[k]], w=["gsr%d" % k])

        def E2b2(tl):
            ysrc = ysrc_of(tl)
            for k in range(8):
                self.tt("dve", ym[tl % 2][:, k, :], ysrc[k][0], gsr[:, k, :], ALU.mult,
                        r=[ysrc[k][1], "gsr%d" % k], w=["ym%d_%d" % (tl % 2, k)])

        mbank = {}

        def M_mm(tl, half):
            ymk = ["ym%d_%d" % (tl % 2, k) for k in range(8)]
            for s4 in (2 * half, 2 * half + 1):
                for nh in range(2):
                    b = 3 + nbc[0] % 5
                    nbc[0] += 1
                    mbank[(tl, s4, nh)] = b
                    pk = "ps%d" % b
                    for k in range(8):
                        self.mm(self.psA[:, b, :], ym[tl % 2][:, k, s4 * 128:(s4 + 1) * 128],
                                wo[:, k, nh * 512:(nh + 1) * 512], k == 0, k == 7, r=ymk + wok, w=[pk])

        def M_add(tl, half):
            t0 = tl * 512
            S = LD[tl % 2]
            sk = "p4_%d" % (tl % 2)
            for s4 in (2 * half, 2 * half + 1):
                it = tl * 4 + s4
                xk = "xn%d" % (it % 2)
                for nh in range(2):
                    b = mbank[(tl, s4, nh)]
                    pk = "ps%d" % b
                    self.tt("dve", xn[it % 2][:, nh * 512:(nh + 1) * 512], self.psA[:, b, :],
                            S["xr"][:, s4, nh * 512:(nh + 1) * 512], ALU.add, r=[pk, sk + "xr"], w=[xk + "_%d" % nh])
                xks = [xk + "_0", xk + "_1"]
                rows = slice(t0 + s4 * 128, t0 + (s4 + 1) * 128)
                if not last:
                    P.dma(ydst[rows, :], xn[it % 2], reads=xks, writes=["X1"], slot=xk)
                else:
                    c1 = slice(it % 64, it % 64 + 1)
                    self.act(junk, xn[it % 2], AF.Square, r=xks, w=["fss%d" % it, "junk"], accum=fss[:, c1])
                    self.ts("pool", fsd[:, c1], fss[:, c1], 1.0 / D, ALU.mult, EPS, ALU.add,
                            r=["fss%d" % it], w=["fsd%d" % it])
                    self.tt("pool", frs[:, c1], fsd[:, c1], self.negh[:, 0:1], ALU.pow,
                            r=["fsd%d" % it, "negh"], w=["frs%d" % it])
                    ok = "yo%d" % (it % 2)
                    self.stt(yo[it % 2], xn[it % 2], frs[:, c1], self.fg, ALU.mult, ALU.mult,
                             r=xks + ["frs%d" % it, "fg"], w=[ok])
                    P.dma(ydst[rows, :], yo[it % 2], reads=[ok], writes=["Y"], slot=ok)

        load_wo()
        loads(0)
        loads(1)
        E1(0)
        E2a(0)
        E2b1(0)
        E2b2(0)
        for tl in range(16):
            nx = tl + 1 < 16
            if nx:
                E1(tl + 1)
                E2a(tl + 1)
            M_mm(tl, 0)
            if nx:
                E2b1(tl + 1)
            M_add(tl, 0)
            M_mm(tl, 1)
            if nx:
                E2b2(tl + 1)
            M_add(tl, 1)
            if tl + 2 < 16:
                loads(tl + 2)


def _consts():
    bf = ml_dtypes.bfloat16
    N = 2 * L
    hi = np.arange(128)[:, None]
    q = np.arange(128)[None, :]
    phi = 2 * np.pi * ((hi * q) % 128) / 128.0
    F1 = np.stack([np.cos(phi), -np.sin(phi), -np.cos(phi)], axis=-1).reshape(128, 384)
    F1b = np.stack([-np.sin(phi), np.cos(phi), np.sin(phi)], axis=-1).reshape(128, 384)[0:64]
    p = np.arange(128)[:, None, None]
    qq = np.arange(128)[None, :, None]
    r = np.arange(128)[None, None, :]
    th = 2 * np.pi * ((p * (qq + 128 * r)) % N) / N
    GT = np.stack([np.cos(th), np.sin(th)], axis=2).reshape(128, 128 * 2 * 128)
    kh = np.arange(128)[:, None]
    lo = np.arange(128)[None, :]
    psi = 2 * np.pi * kh * lo / 128.0
    FIa = np.stack([np.cos(psi), -np.sin(psi)], axis=-1).reshape(128, 256)
    FIb = np.stack([-np.sin(psi), -np.cos(psi)], axis=-1).reshape(128, 256)
    t = np.linspace(0.0, 1.0, L, dtype=np.float32).astype(np.float64)
    w = (2.0 * np.pi / L) * np.arange(L, dtype=np.float64)
    f = np.linspace(1e-4, 15.0, 16, dtype=np.float32).astype(np.float64)
    zT = np.concatenate([t[None, :], np.cos(f[:, None] * w[None, :]), -np.sin(f[:, None] * w[None, :])], axis=0)
    max_decay = math.log(1e-2) / 0.3
    min_decay = math.log(1e-2) / 1.5
    deltas = np.abs(np.linspace(min_decay, max_decay, 384, dtype=np.float32)).astype(np.float32)
    return dict(F1=F1.astype(bf), F1b=F1b.astype(bf), GT=GT.astype(bf), FIa=FIa.astype(bf), FIb=FIb.astype(bf),
                zT=zT.astype(np.float32), tt=np.ascontiguousarray(np.broadcast_to(t.astype(np.float32), (128, L))),
                deltas=deltas, ident=np.eye(128, dtype=np.float32))


def _pack_cp(inp, C):
    cp = np.zeros((DEPTH, 128, NCP), np.float32)
    f = lambda a: np.asarray(a, np.float32)
    for l in range(DEPTH):
        def ch(v, n):
            return f(v).reshape(n, 128).T
        w = f(inp["hy_conv_w"][l])
        for j in range(9):
            for k in range(3):
                cp[l, :, HCW + j * 3 + k] = w[k, j * 128:(j + 1) * 128]
        cp[l, :, HCB:HCB + 9] = ch(inp["hy_conv_b"][l], 9)
        cp[l, :, HD:HD + 3] = ch(inp["hy_d"][l], 3)
        w = f(inp["cf_dw_w"][l])
        for j in range(2):
            for k in range(31):
                cp[l, :, CDW + j * 31 + k] = w[k, j * 128:(j + 1) * 128]
        cp[l, :, CDB:CDB + 2] = ch(inp["cf_dw_b"][l], 2)
        cp[l, :, CLG:CLG + 2] = ch(inp["cf_ln_g"][l], 2)
        cp[l, :, CLB:CLB + 2] = ch(inp["cf_ln_b"][l], 2)
        cp[l, :, CPB:CPB + 2] = ch(inp["cf_pw_b"][l], 2)
        w = f(inp["rg_conv_w"][l])
        for j in range(3):
            for k in range(4):
                cp[l, :, RCW + j * 4 + k] = w[k, j * 128:(j + 1) * 128]
        cp[l, :, RCB:RCB + 3] = ch(inp["rg_conv_b"][l], 3)
        for d in range(2):
            cp[l, :, RBA + d * 3:RBA + d * 3 + 3] = ch(inp["rg_ba"][l, d], 3)
            cp[l, :, RBX + d * 3:RBX + d * 3 + 3] = ch(inp["rg_bx"][l, d], 3)
            cp[l, :, RLAM + d * 3:RLAM + d * 3 + 3] = ch(inp["rg_lam"][l, d], 3)
        cp[l, :, GG:GG + 8] = ch(inp["grp_g"][l], 8)
        cp[l, :, NG:NG + 8] = ch(inp["norm_g"][l], 8)
        cp[l, 0:64, MLPB + 0] = f(inp["hy_b1"][l])
        cp[l, 0:64, MLPB + 1] = f(inp["hy_b2"][l])
        cp[l, 0:64, MLPB + 2] = f(inp["hy_b3"][l])
        cp[l, 0:64, MLPB + 3] = f(inp["hy_freq"][l])
        cp[l, :, DEL:DEL + 3] = -C["deltas"].reshape(3, 128).T
    return cp


_NC_CACHE = {}


def kernel(**inp):
    C = _consts()
    f = lambda a: np.ascontiguousarray(np.asarray(a, np.float32))
    w_in = f(inp["w_in"]).reshape(DEPTH, 8, 128, 3072).transpose(0, 2, 1, 3)
    w_out = f(inp["w_out"]).reshape(DEPTH, 8, 128, 1024).transpose(0, 2, 1, 3)
    cp = _pack_cp(inp, C)
    rgw = np.zeros((DEPTH, 3, 128, 4, 128), np.float32)
    wa, wx = f(inp["rg_wa"]), f(inp["rg_wx"])
    for l in range(DEPTH):
        for d in range(2):
            for h in range(6):
                j, o = h // 2, (h % 2) * 64
                rgw[l, j, o:o + 64, d * 2 + 0, o:o + 64] = wa[l, d, h]
                rgw[l, j, o:o + 64, d * 2 + 1, o:o + 64] = wx[l, d, h]
    pww = f(inp["cf_pw_w"]).reshape(DEPTH, 2, 128, 256).transpose(0, 2, 1, 3)
    common = {
        "w_in": np.ascontiguousarray(w_in), "w_out": np.ascontiguousarray(w_out), "cp": cp,
        "fg": np.ascontiguousarray(np.broadcast_to(f(inp["final_g"])[None, :], (128, D))),
        "hw1": f(inp["hy_w1"]), "hw2": f(inp["hy_w2"]), "hw3": f(inp["hy_w3"]), "hw4": f(inp["hy_w4"]),
        "zT": C["zT"], "tt": C["tt"], "rgw": rgw, "pww": np.ascontiguousarray(pww), "ident": C["ident"],
        "F1": C["F1"], "F1b": C["F1b"], "GT": C["GT"], "FIa": C["FIa"], "FIb": C["FIb"],
    }
    xp, xs = f(inp["x_prompt"]), f(inp["x_sample"])
    in_maps = []
    for i in range(8):
        m = dict(common)
        m["xa"] = xp[i]
        m["xb"] = xs[i % 4]
        in_maps.append(m)
    if "nc" not in _NC_CACHE:
        _NC_CACHE["nc"] = Builder().build()
    res = run_bass_kernel_spmd(_NC_CACHE["nc"], in_maps, core_ids=list(range(8)))
    yp = np.stack([np.asarray(res.results[i]["ya"], np.float32) for i in range(8)], axis=0)
    ys = np.stack([np.asarray(res.results[i]["yb"], np.float32) for i in range(4)], axis=0)
    return (yp, ys)
```

```python
import math
import contextlib
import numpy as np
import ml_dtypes
import concourse.bass as bass
import concourse.mybir as mybir
from concourse.bass_utils import run_bass_kernel_spmd

F32 = mybir.dt.float32
BF16 = mybir.dt.bfloat16
F32R = mybir.dt.float32r
AF = mybir.ActivationFunctionType
ALU = mybir.AluOpType

L = 8192
D = 1024
DEPTH = 2
EPS = 1e-6
NCH = 32
ENGS = ["pe", "act", "dve", "pool", "sp"]

HCW, HCB, HD, CDW, CDB, CLG, CLB, CPB = 0, 27, 36, 39, 101, 103, 105, 107
RCW, RCB, RBA, RBX, RLAM, GG, NG, MLPB, DEL = 109, 121, 124, 130, 136, 142, 150, 158, 162
CL, CL2, HFR, QFR, HFB, QFB = 165, 171, 177, 178, 179, 182
NCP = 186


class Prog:
    def __init__(self, nc):
        self.nc = nc
        self.ops = []

    def op(self, eng, fn, reads=(), writes=(), dma=None):
        self.ops.append((eng, fn, tuple(reads), tuple(writes), dma))

    def barrier(self):
        self.ops.append(("barrier", None, (), (), None))

    def dma(self, out, in_, reads=(), writes=(), slot=None, eng="sp", **kw):
        def fn(e, out=out, in_=in_, kw=kw):
            return e.dma_start(out=out, in_=in_, **kw)
        self.op(eng, fn, reads, writes, dma=slot)

    def emit(self, stack):
        nc = self.nc
        esem = {e: stack.enter_context(nc.semaphore("s_" + e)) for e in ENGS}
        slots = []
        seen = set()
        for o in self.ops:
            if o[4] is not None and o[4] not in seen:
                seen.add(o[4])
                slots.append(o[4])
        dsem = {s: stack.enter_context(nc.semaphore("d_%d" % i)) for i, s in enumerate(slots)}
        ecount = {e: 0 for e in ENGS}
        dcount = {s: 0 for s in slots}
        last_w = {}
        readers = {}
        streams = {e: [] for e in ENGS}
        waited = {e: {} for e in ENGS}
        pending_barrier = {e: None for e in ENGS}
        for (eng, fn, reads, writes, dma) in self.ops:
            if eng == "barrier":
                snap = [("e", e2, ecount[e2], e2) for e2 in ENGS if ecount[e2] > 0]
                snap += [("d", s, dcount[s], None) for s in slots if dcount[s] > 0]
                for e in ENGS:
                    pending_barrier[e] = snap
                continue
            deps = []
            if pending_barrier[eng] is not None:
                deps.extend(pending_barrier[eng])
                pending_barrier[eng] = None
            for k in reads:
                if k in last_w:
                    deps.append(last_w[k])
            for k in writes:
                if k in last_w:
                    deps.append(last_w[k])
                for (kk, ss), (vv, pe_) in readers.get(k, {}).items():
                    deps.append((kk, ss, vv, pe_))
            if dma is not None and dcount[dma] > 0:
                deps.append(("d", dma, dcount[dma], None))
            waits = []
            w = waited[eng]
            for (kind, sid, val, peng) in deps:
                if kind == "e" and peng == eng and eng == "pe":
                    continue
                if w.get((kind, sid), 0) >= val:
                    continue
                w[(kind, sid)] = val
                waits.append((kind, sid, val))
            if dma is not None:
                dcount[dma] += 16
                tok = ("d", dma, dcount[dma], None)
            else:
                ecount[eng] += 1
                tok = ("e", eng, ecount[eng], eng)
            streams[eng].append((waits, fn, tok))
            for k in writes:
                last_w[k] = tok
                readers[k] = {}
            for k in reads:
                if k in writes:
                    continue
                rd = readers.setdefault(k, {})
                if rd.get((tok[0], tok[1]), (0, None))[0] < tok[2]:
                    rd[(tok[0], tok[1])] = (tok[2], tok[3])
        final_waits = [("d", s, dcount[s]) for s in slots if dcount[s] > 0]
        self.stats = {e: len(streams[e]) for e in ENGS}

        def semof(kind, sid):
            return esem[sid] if kind == "e" else dsem[sid]

        with nc.Block() as block:
            def make(e):
                def body(engine):
                    for (waits, fn, tok) in streams[e]:
                        for (kind, sid, val) in waits:
                            engine.wait_ge(semof(kind, sid), val)
                        ins = fn(engine)
                        if tok[0] == "e":
                            ins.then_inc(esem[e], 1)
                        else:
                            ins.then_inc(dsem[tok[1]], 16)
                    if e == "sp":
                        for (kind, sid, val) in final_waits:
                            engine.wait_ge(semof(kind, sid), val)
                        for e2 in ENGS:
                            if e2 != "sp" and ecount[e2] > 0:
                                engine.wait_ge(esem[e2], ecount[e2])
                return body
            block.tensor(make("pe"))
            block.scalar(make("act"))
            block.vector(make("dve"))
            block.gpsimd(make("pool"))
            block.sync(make("sp"))


class Arena:
    def __init__(self, ap32):
        self.ap = ap32
        self.W = ap32.shape[1]
        self.off = 0

    def reset(self, off):
        self.off = off

    def f32(self, cols):
        v = self.ap[:, self.off:self.off + cols]
        self.off += cols
        assert self.off <= self.W, (self.off, self.W)
        return v

    def bf16(self, cols):
        w = (cols + 1) // 2
        v = self.ap[:, self.off:self.off + w].bitcast(BF16)
        self.off += w
        assert self.off <= self.W, (self.off, self.W)
        return v


class Builder:
    def __init__(self, dbg=False):
        self.dbg = dbg
        self.nc = nc = bass.Bass("TRN2", target_bir_lowering=False)
        self.P = Prog(nc)
        self.uid = 0
        di = lambda n, s, dt=F32: nc.dram_tensor(n, list(s), dt, kind="ExternalInput").ap()
        do = lambda n, s, dt=F32: nc.dram_tensor(n, list(s), dt, kind="ExternalOutput").ap()
        sk = "ExternalOutput" if dbg else "Internal"
        ds = lambda n, s, dt=BF16: nc.dram_tensor(n, list(s), dt, kind=sk).ap()
        self.x_d = [di("xa", [L, D]), di("xb", [L, D])]
        self.y_d = [do("ya", [L, D]), do("yb", [L, D])]
        self.w_in_d = di("w_in", [DEPTH, 128, 8, 3072])
        self.w_out_d = di("w_out", [DEPTH, 128, 8, 1024])
        self.cp_d = di("cp", [DEPTH, 128, NCP])
        self.fg_d = di("fg", [128, D])
        self.w1_d = di("hw1", [DEPTH, 33, 64])
        self.w2_d = di("hw2", [DEPTH, 64, 64])
        self.w3_d = di("hw3", [DEPTH, 64, 64])
        self.w4_d = di("hw4", [DEPTH, 64, 768])
        self.zT_d = di("zT", [33, L])
        self.tt_d = di("tt", [128, L])
        self.rgw_d = di("rgw", [DEPTH, 3, 128, 4, 128])
        self.pww_d = di("pww", [DEPTH, 128, 2, 256])
        self.ident_d = di("ident", [128, 128])
        self.F1_d = di("F1", [128, 384], BF16)
        self.F1b_d = di("F1b", [64, 384], BF16)
        self.GT_d = di("GT", [128, 128 * 2 * 128], BF16)
        self.FIa_d = di("FIa", [128, 256], BF16)
        self.FIb_d = di("FIb", [128, 256], BF16)
        self.P_d = ds("P_s", [3072, L])
        self.U_d = ds("U_s", [384, L])
        self.X0_d = ds("X0_s", [384, L])
        self.YC_d = ds("YC_s", [384, L])
        self.YCF_d = ds("YCF_s", [256, L])
        self.YRG_d = ds("YRG_s", [384, L])
        self.YRGB_d = ds("YRGB_s", [384, L])
        self.GS_d = ds("GS_s", [1024, L])
        self.HT_d = ds("HT_s", [768, L])
        self.KF_d = ds("KF_s", [12, 128, 128 * 2 * NCH])
        self.X1_d = nc.dram_tensor("X1_s", [2, L, D], F32, kind=sk).ap()

    def ts(self, eng, out, in0, s1, op0, s2=None, op1=None, r=(), w=()):
        if eng == "pool" and op1 is None:
            s2, op1 = 0.0, ALU.add
        if op1 is None:
            fn = lambda e: e.tensor_scalar(out=out, in0=in0, scalar1=s1, scalar2=None, op0=op0)
        else:
            fn = lambda e: e.tensor_scalar(out=out, in0=in0, scalar1=s1, scalar2=s2, op0=op0, op1=op1)
        self.P.op(eng, fn, r, w)

    def stt(self, out, in0, scalar, in1, op0, op1, r=(), w=()):
        self.P.op("dve", lambda e: e.scalar_tensor_tensor(out=out, in0=in0, scalar=scalar, in1=in1,
                                                          op0=op0, op1=op1), r, w)

    def tt(self, eng, out, in0, in1, op, r=(), w=()):
        self.P.op(eng, lambda e: e.tensor_tensor(out=out, in0=in0, in1=in1, op=op), r, w)

    def act(self, out, in_, func, r=(), w=(), scale=None, bias=None, accum=None):
        kw = {}
        if scale is not None:
            kw["scale"] = scale
        if bias is not None:
            kw["bias"] = bias
        if accum is not None:
            kw["accum_out"] = accum
        self.P.op("act", lambda e: e.activation(out=out, in_=in_, func=func, **kw), r, w)

    def cpy(self, eng, out, in_, r=(), w=()):
        if eng == "act":
            self.P.op("act", lambda e: e.activation(out=out, in_=in_, func=AF.Copy), r, w)
        else:
            self.P.op(eng, lambda e: e.tensor_copy(out=out, in_=in_), r, w)

    def mm(self, out, lhsT, rhs, start, stop, r=(), w=()):
        self.P.op("pe", lambda e: e.matmul(out=out, lhsT=lhsT, rhs=rhs, start=start, stop=stop), r, w)

    def tr(self, out, in_, ident, r=(), w=()):
        self.P.op("pe", lambda e: e.transpose(out=out, in_=in_, identity=ident), r, w)

    def memset(self, eng, ap, val, r=(), w=()):
        self.P.op(eng, lambda e: e.memset(ap, val), r, w)

    def recip(self, out, in_, r=(), w=()):
        self.P.op("dve", lambda e: e.reciprocal(out=out, in_=in_), r, w)

    def k(self, name):
        self.uid += 1
        return "%s#%d" % (name, self.uid)

    def build(self):
        nc = self.nc
        with contextlib.ExitStack() as st:
            arena_t = st.enter_context(nc.sbuf_tensor("arena", [128, 50 * 1024], F32))
            psA = st.enter_context(nc.psum_tensor("psA", [128, 8, 512], F32))
            self.psA = psA
            self.pbv = [psA[:, 6 + i, :].bitcast(BF16) for i in range(2)]
            self.A = A = Arena(arena_t[:])
            self.ident_f = A.f32(128)
            self.ident_b = A.bf16(128)
            self.ones_f = A.f32(128)
            self.ones_b = A.bf16(128)
            self.negh = A.f32(2)
            self.cp = [A.f32(NCP), A.f32(NCP)]
            self.fg = A.f32(D)
            self.F1 = A.bf16(384)
            self.F1b = A.bf16(384)
            self.FIa = A.bf16(256)
            self.FIb = A.bf16(256)
            self.persist_end = A.off
            self.setup()
            stop = getattr(self, "stop_after", None)
            done = False
            for l in range(DEPTH):
                if done:
                    break
                self.P.barrier()
                self.filter_gen(l)
                self.P.barrier()
                self.filter_fft(l)
                for s in range(2):
                    xsrc = self.x_d[s] if l == 0 else self.X1_d[s]
                    for nm, fn in [("p1", lambda: self.phase1(l, xsrc)), ("conf", lambda: self.phase2_conf(l)),
                                   ("rg", lambda: self.phase2_rg(l)), ("fft", lambda: self.phase3_fft(l)),
                                   ("p4", lambda: self.phase4(l, s, xsrc))]:
                        self.P.barrier()
                        fn()
                        if stop == nm:
                            done = True
                            break
                    if done:
                        break
            self.P.emit(st)
        return nc

    def setup(self):
        P = self.P
        P.dma(self.ident_f, self.ident_d, writes=["ident_f"], slot="c0")
        P.dma(self.fg, self.fg_d, writes=["fg"], slot="c1")
        P.dma(self.F1, self.F1_d, writes=["F1"], slot="c2")
        P.dma(self.F1b[0:64, :], self.F1b_d, writes=["F1"], slot="c3")
        P.dma(self.FIa, self.FIa_d, writes=["FI"], slot="c4")
        P.dma(self.FIb, self.FIb_d, writes=["FI"], slot="c5")
        self.cpy("dve", self.ident_b, self.ident_f, r=["ident_f"], w=["ident_b"])
        self.memset("dve", self.ones_f, 1.0, w=["ones"])
        self.memset("dve", self.ones_b, 1.0, w=["ones"])
        self.memset("dve", self.negh, -0.5, w=["negh"])
        for l in range(DEPTH):
            cp = self.cp[l]
            ck = "cp%d" % l
            P.dma(cp[:, 0:DEL + 3], self.cp_d[l][:, 0:DEL + 3], writes=[ck], slot="c6")
            self.act(cp[:, CL:CL + 6], cp[:, RLAM:RLAM + 6], AF.Exp, r=[ck], w=[ck], scale=-1.0)
            self.act(cp[:, CL:CL + 6], cp[:, CL:CL + 6], AF.Ln, r=[ck], w=[ck], bias=1.0)
            self.ts("dve", cp[:, CL2:CL2 + 6], cp[:, CL:CL + 6], -16.0, ALU.mult, r=[ck], w=[ck])
            self.ts("dve", cp[:, CL:CL + 6], cp[:, CL:CL + 6], -8.0, ALU.mult, r=[ck], w=[ck])
            fr = cp[:, MLPB + 3:MLPB + 4]
            self.ts("dve", cp[:, HFR:HFR + 1], fr, 0.5, ALU.mult, r=[ck], w=[ck])
            self.ts("dve", cp[:, QFR:QFR + 1], fr, 0.25, ALU.mult, r=[ck], w=[ck])
            self.ts("dve", cp[:, HFB:HFB + 3], cp[:, MLPB:MLPB + 3], cp[:, HFR:HFR + 1], ALU.mult, r=[ck], w=[ck])
            self.ts("dve", cp[:, QFB:QFB + 3], cp[:, MLPB:MLPB + 3], cp[:, QFR:QFR + 1], ALU.mult, r=[ck], w=[ck])

    def filter_gen(self, l):
        P, A, cp = self.P, self.A, self.cp[l]
        ck = "cp%d" % l
        A.reset(self.persist_end)
        zT = A.f32(L)
        tt = A.f32(L)
        w1 = A.f32(64)
        w2 = A.f32(64)
        w3 = A.f32(64)
        w4 = A.f32(768)
        w4b = A.bf16(768)
        P.dma(zT[0:33, :], self.zT_d, writes=["zT"], slot="f0")
        P.dma(tt, self.tt_d, writes=["tt"], slot="f1")
        P.dma(w1[0:33, :], self.w1_d[l], writes=["fw"], slot="f2")
        P.dma(w2[0:64, :], self.w2_d[l], writes=["fw"], slot="f3")
        P.dma(w3[0:64, :], self.w3_d[l], writes=["fw"], slot="f4")
        P.dma(w4[0:64, :], self.w4_d[l], writes=["fw"], slot="f5")
        self.cpy("dve", w4b[0:64, :], w4[0:64, :], r=["fw"], w=["fw4b"])
        sets = []
        for i in range(2):
            sets.append(dict(s2=A.f32(512), s4=A.f32(512), hd=[A.f32(512), A.f32(512), A.bf16(512)],
                             dec=A.f32(3 * 512), stg=A.bf16(6 * 512)))
        HTv = self.HT_d.rearrange("(j p) t -> p j t", p=128)
        zcol = A.bf16(8).rearrange("p (j t) -> p j t", j=4)
        self.memset("dve", zcol, 0.0, w=["zcol"])
        P.dma(HTv[:, 3:6, 0:1], zcol[:, 0:3, 0:1], reads=["zcol"], writes=["HTz"], slot="fz",
              allow_slow_non_contiguous=True)
        wts = [w1[0:33, :], w2[0:64, :], w3[0:64, :]]
        for pair in range(8):
            tiles = [2 * pair, 2 * pair + 1]
            prev = {ti: zT[0:33, ti * 512:(ti + 1) * 512] for ti in tiles}
            for li in range(3):
                for ti in tiles:
                    S = sets[ti % 2]
                    sk = "fs%d" % (ti % 2)
                    ps = self.psA[0:64, ti % 2, :]
                    pk = "ps%d" % (ti % 2)
                    self.mm(ps, wts[li], prev[ti], True, True, r=["zT", "fw", sk + "hd"], w=[pk])
                    self.act(S["s2"][0:64, :], ps, AF.Sin, r=[pk, ck], w=[sk + "s2"],
                             scale=cp[0:64, HFR:HFR + 1], bias=cp[0:64, HFB + li:HFB + li + 1])
                    self.act(S["s4"][0:64, :], ps, AF.Sin, r=[pk, ck], w=[sk + "s4"],
                             scale=cp[0:64, QFR:QFR + 1], bias=cp[0:64, QFB + li:QFB + li + 1])
                    self.tt("dve", S["s4"][0:64, :], S["s4"][0:64, :], S["s4"][0:64, :], ALU.mult,
                            r=[sk + "s4"], w=[sk + "s4"])
                    self.ts("dve", S["s4"][0:64, :], S["s4"][0:64, :], -4.0, ALU.mult, 2.0, ALU.add,
                            r=[sk + "s4"], w=[sk + "s4"])
                    self.tt("dve", S["hd"][li][0:64, :], S["s4"][0:64, :], S["s2"][0:64, :], ALU.mult,
                            r=[sk + "s4", sk + "s2"], w=[sk + "hd"])
                    prev[ti] = S["hd"][li][0:64, :]
            for ti in tiles:
                S = sets[ti % 2]
                sk = "fs%d" % (ti % 2)
                t0 = ti * 512
                dec3 = S["dec"].rearrange("p (j t) -> p j t", j=3)
                for j in range(3):
                    self.act(dec3[:, j, :], tt[:, t0:t0 + 512], AF.Exp, r=["tt", ck], w=[sk + "dec"],
                             scale=cp[:, DEL + j:DEL + j + 1])
            for ti in tiles:
                S = sets[ti % 2]
                sk = "fs%d" % (ti % 2)
                t0 = ti * 512
                dec3 = S["dec"].rearrange("p (j t) -> p j t", j=3)
                stg = S["stg"].rearrange("p (j t) -> p j t", j=6)
                for j in range(6):
                    ps = self.psA[:, 2 + j % 4, :]
                    pk = "ps%d" % (2 + j % 4)
                    self.mm(ps, w4b[0:64, j * 128:(j + 1) * 128], prev[ti], True, True, r=["fw4b", sk + "hd"], w=[pk])
                    o_ = stg[:, j, :] if j < 3 else stg[:, j, ::-1]
                    self.tt("dve", o_, ps, dec3[:, j % 3, :], ALU.mult, r=[pk, sk + "dec"], w=[sk + "stg%d" % j])
                P.dma(HTv[:, 0:3, t0:t0 + 512], stg[:, 0:3, :], reads=[sk + "stg%d" % j for j in range(3)],
                      writes=["HTf%d" % ti], slot=sk + "o")
                J0 = 7681 - t0
                bk = [sk + "stg%d" % j for j in range(3, 6)]
                if ti == 0:
                    P.dma(HTv[:, 3:6, J0:J0 + 511], stg[:, 3:6, 0:511], reads=bk, writes=["HTb%d" % ti], slot=sk + "ob")
                else:
                    P.dma(HTv[:, 3:6, J0:J0 + 512], stg[:, 3:6, :], reads=bk, writes=["HTb%d" % ti] + (["HTz"] if ti == 15 else []),
                          slot=sk + "ob")

    def load_gt(self):
        GT = self.A.bf16(128 * 2 * 128)
        self.P.dma(GT, self.GT_d, writes=["GT"], slot="gt")
        return GT.rearrange("p (q pl r) -> p q pl r", pl=2, r=128)

    def fft_s1(self, Ux, F1, A4, akeys, ukey, tag, K=64):
        ukeys = ukey if isinstance(ukey, list) else [ukey]
        for c in range(NCH):
            b = c % 4
            ps = self.psA[:, b, 0:384]
            self.mm(ps, Ux[0:K, c, :], F1[0:K, :], True, True, r=ukeys + ["F1"], w=["ps%d" % b])
            eng = "dve" if c % 4 == 3 else "act"
            self.cpy(eng, A4[:, :, c, :], ps.rearrange("p (q pl) -> p q pl", pl=3),
                     r=["ps%d" % b], w=[akeys[c]])

    def filter_fft(self, l):
        P, A = self.P, self.A
        A.reset(self.persist_end)
        GT4 = self.load_gt()
        Af = [A.bf16(128 * 3 * NCH) for _ in range(2)]
        KFt = [A.bf16(128 * 2 * NCH) for _ in range(2)]
        Uf = [A.bf16(NCH * 128).rearrange("p (c l) -> p c l", c=NCH) for _ in range(2)]
        HTr = self.HT_d.rearrange("c (hi lo) -> hi c lo", lo=128)
        htk = ["HTf%d" % t for t in range(16)] + ["HTb%d" % t for t in range(16)] + ["HTz"]

        def loads(ch):
            c0 = ch * NCH
            P.dma(Uf[ch % 2][0:64], HTr[:, c0:c0 + NCH, :], reads=htk, writes=["Uf%da" % (ch % 2)], slot="Uf%da" % (ch % 2))
            P.dma(Uf[ch % 2][64:128], HTr[:, 384 + c0:384 + c0 + NCH, :], reads=htk, writes=["Uf%db" % (ch % 2)],
                  slot="Uf%db" % (ch % 2))

        loads(0)
        for ch in range(12):
            if ch + 1 < 12:
                loads(ch + 1)
            Af4 = Af[ch % 2].rearrange("p (q c pl) -> p q c pl", pl=3, c=NCH)
            afk = ["Af%d_%d" % (ch % 2, c) for c in range(NCH)]
            self.fft_s1(Uf[ch % 2], self.F1, Af4, afk, ["Uf%da" % (ch % 2), "Uf%db" % (ch % 2)], "f", K=128)
            kfk = "KFt%d" % (ch % 2)
            KF4 = KFt[ch % 2].rearrange("p (q c pl) -> p q c pl", pl=2, c=NCH)
            for kg in range(16):
                b = 4 + kg % 4
                pk = "ps%d" % b
                for q8 in range(8):
                    q = kg * 8 + q8
                    o = self.psA[:, b, q8 * 64:(q8 + 1) * 64]
                    self.mm(o, GT4[:, q, 0, :], Af4[:, q, :, 0:2], True, False, r=["GT"] + afk, w=[pk])
                    self.mm(o, GT4[:, q, 1, :], Af4[:, q, :, 1:3], False, True, r=["GT"] + afk, w=[pk])
                X4 = self.psA[:, b, :].rearrange("p (q c pl) -> p q c pl", pl=2, c=NCH)
                if kg % 2 == 0:
                    self.act(KF4[:, kg * 8:(kg + 1) * 8, :, :], X4, AF.Copy, r=[pk], w=[kfk + "_%d" % kg],
                             scale=1.0 / 16384.0)
                else:
                    self.ts("dve", KF4[:, kg * 8:(kg + 1) * 8, :, :], X4, 1.0 / 16384.0, ALU.mult,
                            r=[pk], w=[kfk + "_%d" % kg])
            P.dma(self.KF_d[ch], KFt[ch % 2], reads=[kfk + "_%d" % kg for kg in range(16)], writes=["KF"], slot=kfk)

    def phase1(self, l, xsrc):
        P, A, cp = self.P, self.A, self.cp[l]
        ck = "cp%d" % l
        A.reset(self.persist_end)
        wbf = A.bf16(8 * 3072).rearrange("p (k n) -> p k n", k=8)
        wst = [A.f32(8 * 384).rearrange("p (k n) -> p k n", k=8) for _ in range(2)]
        xb = [A.f32(D) for _ in range(3)]
        junk = A.bf16(D)
        ssq = A.f32(64)
        sd = A.f32(64)
        rs = A.f32(64)
        xs = [A.bf16(D) for _ in range(2)]
        hT = [A.bf16(8 * 512).rearrange("p (k t) -> p k t", k=8) for _ in range(2)]
        pst = [A.bf16(24 * 512).rearrange("p (j t) -> p j t", j=24) for _ in range(2)]
        for it0 in range(2):
            P.dma(xb[it0 % 3], xsrc[it0 * 128:(it0 + 1) * 128, :], reads=["X1"], writes=["x%d" % (it0 % 3)],
                  slot="x%d" % (it0 % 3))
        for pc in range(8):
            stg = wst[pc % 2]
            key = "wst%d" % (pc % 2)
            P.dma(stg, self.w_in_d[l][:, :, pc * 384:(pc + 1) * 384], writes=[key], slot=key)
            for k in range(8):
                eng = "dve" if k % 2 == 0 else "pool"
                self.ts(eng, wbf[:, k, pc * 384:(pc + 1) * 384], stg[:, k, :], cp[:, NG + k:NG + k + 1], ALU.mult,
                        r=[key, ck], w=["wbf%d_%d" % (pc, k)])
        Pv = self.P_d.rearrange("(j p) t -> p j t", p=128)

        def load(it):
            xk = "x%d" % (it % 3)
            P.dma(xb[it % 3], xsrc[it * 128:(it + 1) * 128, :], reads=["X1"], writes=[xk], slot=xk)

        def pre_sub(it):
            xt = xb[it % 3]
            xk = "x%d" % (it % 3)
            self.act(junk, xt, AF.Square, r=[xk], w=["ss%d" % it, "junk"], accum=ssq[:, it:it + 1])
            self.ts("pool", sd[:, it:it + 1], ssq[:, it:it + 1], 1.0 / D, ALU.mult, EPS, ALU.add,
                    r=["ss%d" % it], w=["sd%d" % it])
            self.tt("pool", rs[:, it:it + 1], sd[:, it:it + 1], self.negh[:, 0:1], ALU.pow,
                    r=["sd%d" % it, "negh"], w=["rs%d" % it])
            self.ts("dve", xs[it % 2], xt, rs[:, it:it + 1], ALU.mult, r=[xk, "rs%d" % it], w=["xs%d" % (it % 2)])
            if it + 2 < 64:
                load(it + 2)

        def tr_sub(it):
            g, sub = it // 4, it % 4
            pbk = "ps%d" % (6 + it % 2)
            for k in range(8):
                self.tr(self.pbv[it % 2][:, k * 128:(k + 1) * 128], xs[it % 2][:, k * 128:(k + 1) * 128],
                        self.ident_b, r=["xs%d" % (it % 2), "ident_b"], w=[pbk])
            eng = "act" if it % 2 == 0 else "dve"
            self.cpy(eng, hT[g % 2][:, :, sub * 128:(sub + 1) * 128],
                     self.pbv[it % 2].rearrange("p (k t) -> p k t", k=8), r=[pbk], w=["hT%d_%d" % (g % 2, sub)])

        for sub in range(4):
            pre_sub(sub)
            tr_sub(sub)
        evi = 0
        for g in range(16):
            hks = ["hT%d_%d" % (g % 2, s_) for s_ in range(4)]
            pks = []
            plain = [j for j in range(24) if j not in (11, 12) and j < 16]
            sigs, sils = [11, 12], list(range(16, 24))
            order = plain + (sigs + sils if g % 2 == 0 else sils + sigs)
            for part in range(4):
                nit = (g + 1) * 4 + part
                if g + 1 < 16:
                    pre_sub(nit)
                for j in order[part * 6:part * 6 + 6]:
                    b = evi % 4
                    evi += 1
                    pk = "ps%d" % b
                    for k in range(8):
                        self.mm(self.psA[:, b, :], wbf[:, k, j * 128:(j + 1) * 128], hT[g % 2][:, k, :],
                                k == 0, k == 7, r=hks + ["wbf%d_%d" % (j // 3, k)], w=[pk])
                    pko = "pst%d_%d" % (g % 2, j)
                    pks.append(pko)
                    if j in sigs:
                        self.act(pst[g % 2][:, j, :], self.psA[:, b, :], AF.Sigmoid, r=[pk], w=[pko])
                    elif j >= 16:
                        self.act(pst[g % 2][:, j, :], self.psA[:, b, :], AF.Silu, r=[pk], w=[pko])
                    else:
                        self.cpy("dve", pst[g % 2][:, j, :], self.psA[:, b, :], r=[pk], w=[pko])
                if g + 1 < 16:
                    tr_sub(nit)
            P.dma(Pv[:, :, g * 512:(g + 1) * 512], pst[g % 2], reads=pks, writes=["P"], slot="pst%d" % (g % 2))

    def phase2_hyena(self, l):
        P, A, cp = self.P, self.A, self.cp[l]
        ck = "cp%d" % l
        A.reset(self.persist_end)
        PW = L + 4
        pin = [A.bf16(PW) for _ in range(3)]
        ub = A.bf16(L)
        x0b = A.bf16(L)
        tmp = [[A.f32(2048) for _ in range(3)] for _ in range(2)]
        for i in range(3):
            self.memset("dve", pin[i][:, 0:2], 0.0, w=["hhalo"])
            self.memset("dve", pin[i][:, L + 2:L + 4], 0.0, w=["hhalo"])

        def conv3(out_last, acc, pinap, j, t0, S, pkey, akey, okey):
            c = HCW + j * 3
            self.act(acc, pinap[:, 2 + t0 - 1:2 + t0 - 1 + S], AF.Identity, r=[pkey, "hhalo", ck], w=[akey],
                     scale=cp[:, c:c + 1], bias=cp[:, HCB + j:HCB + j + 1])
            self.stt(acc, pinap[:, 2 + t0:2 + t0 + S], cp[:, c + 1:c + 2], acc, ALU.mult, ALU.add,
                     r=[pkey, akey, ck], w=[akey])
            self.stt(out_last, pinap[:, 2 + t0 + 1:2 + t0 + 1 + S], cp[:, c + 2:c + 3], acc, ALU.mult, ALU.add,
                     r=[pkey, "hhalo", akey, ck], w=[okey])

        for i in range(3):
            for gi, jj in enumerate([i, 3 + i, 6 + i]):
                P.dma(pin[gi][:, 2:2 + L], self.P_d[jj * 128:(jj + 1) * 128, :], reads=["P"],
                      writes=["hpin%d" % gi], slot="hpin%d" % gi)
            for sg in range(4):
                t0 = sg * 2048
                T = tmp[sg % 2]
                tk = ["ht%d_%d" % (sg % 2, q) for q in range(3)]
                conv3(T[0], T[0], pin[1], 3 + i, t0, 2048, "hpin1", tk[0], tk[0])
                conv3(T[1], T[1], pin[2], 6 + i, t0, 2048, "hpin2", tk[1], tk[1])
                self.tt("dve", ub[:, t0:t0 + 2048], T[0], T[1], ALU.mult, r=[tk[0], tk[1]], w=["ub%d" % sg])
                conv3(x0b[:, t0:t0 + 2048], T[2], pin[0], i, t0, 2048, "hpin0", tk[2], "x0b%d" % sg)
            P.dma(self.U_d[i * 128:(i + 1) * 128, :], ub, reads=["ub%d" % s_ for s_ in range(4)],
                  writes=["U"], slot="ub")
            P.dma(self.X0_d[i * 128:(i + 1) * 128, :], x0b, reads=["x0b%d" % s_ for s_ in range(4)],
                  writes=["X0"], slot="x0b")

    def phase2_conf(self, l):
        P, A, cp = self.P, self.A, self.cp[l]
        ck = "cp%d" % l
        A.reset(self.persist_end)
        UW = L + 32
        ug = [A.bf16(UW) for _ in range(2)]
        dg = A.bf16(2 * 31 * 128).rearrange("p (j k m) -> p j k m", j=2, k=31)
        pww = A.f32(2 * 256).rearrange("p (k n) -> p k n", k=2)
        ycf = [A.bf16(L) for _ in range(2)]
        ab = [[A.bf16(2048) for _ in range(2)] for _ in range(2)]
        T = []
        for i in range(2):
            T.append(dict(uc=A.f32(1024).rearrange("p (j t) -> p j t", j=2),
                          sqb=A.bf16(1024).rearrange("p (j t) -> p j t", j=2),
                          sub=A.bf16(1024).rearrange("p (j t) -> p j t", j=2),
                          mean=A.f32(512), m2=A.f32(512), var=A.f32(512), rstd=A.f32(512),
                          ln=A.f32(1024).rearrange("p (j t) -> p j t", j=2)))
        HW_ = 516
        hin = [A.bf16(9 * HW_).rearrange("p (j t) -> p j t", j=9) for _ in range(2)]
        hdg = A.bf16(27 * 128).rearrange("p (k m) -> p k m", k=27)
        x1c = [A.f32(3 * 512).rearrange("p (j t) -> p j t", j=3) for _ in range(2)]
        uo = [A.bf16(3 * 512).rearrange("p (j t) -> p j t", j=3) for _ in range(2)]
        x0o = [A.bf16(3 * 512).rearrange("p (j t) -> p j t", j=3) for _ in range(2)]
        for k in range(27):
            self.ts("dve" if k % 2 == 0 else "pool", hdg[:, k, :], self.ident_f, cp[:, HCW + k:HCW + k + 1], ALU.mult,
                    r=["ident_f", ck], w=["hdg%d" % k])
        hdgk = ["hdg%d" % k for k in range(27)]
        Phy = self.P_d[0:1152, :].rearrange("(j p) t -> p j t", p=128)
        Uv = self.U_d.rearrange("(j p) t -> p j t", p=128)
        X0v = self.X0_d.rearrange("(j p) t -> p j t", p=128)
        hbank = [0]

        def hy_load(tl):
            t0 = tl * 512
            hk = "hin%d" % (tl % 2)
            H = hin[tl % 2]
            if tl == 0:
                self.memset("dve", H[:, :, 0:2], 0.0, w=[hk])
                P.dma(H[:, :, 1:514], Phy[:, :, 0:513], reads=["P"], writes=[hk], slot=hk)
            elif tl == 15:
                P.dma(H[:, :, 0:513], Phy[:, :, t0 - 1:L], reads=["P"], writes=[hk], slot=hk)
                self.memset("dve", H[:, :, 513:514], 0.0, r=[hk], w=[hk + "h"])
            else:
                P.dma(H[:, :, 0:514], Phy[:, :, t0 - 1:t0 + 513], reads=["P"], writes=[hk], slot=hk)

        def hy_tile(tl):
            t0 = tl * 512
            hk = "hin%d" % (tl % 2)
            H = hin[tl % 2]
            st_ = tl % 2
            for i in range(3):
                for jj, kind in ((3 + i, "x1"), (6 + i, "v"), (i, "x0")):
                    b = 6 + hbank[0] % 2
                    hbank[0] += 1
                    pk = "ps%d" % b
                    for k in range(3):
                        self.mm(self.psA[:, b, :], hdg[:, jj * 3 + k, :], H[:, jj, k:k + 512], k == 0, k == 2,
                                r=[hk, hk + "h"] + hdgk, w=[pk])
                    bcol = cp[:, HCB + jj:HCB + jj + 1]
                    if kind == "x1":
                        self.act(x1c[st_][:, i, :], self.psA[:, b, :], AF.Identity, r=[pk, ck], w=["x1c%d_%d" % (st_, i)],
                                 bias=bcol)
                    elif kind == "v":
                        self.stt(uo[st_][:, i, :], self.psA[:, b, :], bcol, x1c[st_][:, i, :], ALU.add, ALU.mult,
                                 r=[pk, ck, "x1c%d_%d" % (st_, i)], w=["uo%d_%d" % (st_, i)])
                    else:
                        self.act(x0o[st_][:, i, :], self.psA[:, b, :], AF.Identity, r=[pk, ck], w=["x0o%d_%d" % (st_, i)],
                                 bias=bcol)
            P.dma(Uv[:, :, t0:t0 + 512], uo[st_], reads=["uo%d_%d" % (st_, i) for i in range(3)], writes=["U"],
                  slot="uo%d" % st_)
            P.dma(X0v[:, :, t0:t0 + 512], x0o[st_], reads=["x0o%d_%d" % (st_, i) for i in range(3)], writes=["X0"],
                  slot="x0o%d" % st_)

        P.dma(pww, self.pww_d[l], writes=["pww"], slot="pww")
        pwb16 = A.bf16(2 * 256).rearrange("p (k n) -> p k n", k=2)
        self.cpy("dve", pwb16, pww, r=["pww"], w=["pwb16"])
        for j in range(2):
            self.memset("dve", ug[j][:, 0:16], 0.0, w=["chalo"])
            self.memset("dve", ug[j][:, L + 16:L + 32], 0.0, w=["chalo"])
            for k in range(31):
                eng = "dve" if k % 2 == 0 else "pool"
                self.ts(eng, dg[:, j, k, :], self.ident_f, cp[:, CDW + j * 31 + k:CDW + j * 31 + k + 1], ALU.mult,
                        r=["ident_f", ck], w=["dg"])
        n = 0
        for sg in range(4):
            t0 = sg * 2048
            for j in range(2):
                S = ab[n % 2]
                sk = "cab%d" % (n % 2)
                P.dma(S[0], self.P_d[(9 + j) * 128:(10 + j) * 128, t0:t0 + 2048], reads=["P"], writes=[sk + "a"], slot=sk + "a")
                P.dma(S[1], self.P_d[(11 + j) * 128:(12 + j) * 128, t0:t0 + 2048], reads=["P"], writes=[sk + "b"], slot=sk + "b")
                self.tt("dve", ug[j][:, 16 + t0:16 + t0 + 2048], S[0], S[1], ALU.mult,
                        r=[sk + "a", sk + "b"], w=["ug%d_%d" % (j, sg)])
                n += 1
        ugk = [["ug%d_%d" % (j, sg) for sg in range(4)] + ["chalo"] for j in range(2)]

        def conv_mm(tl):
            t0 = tl * 512
            segs = sorted(set(min(3, max(0, tt_ // 2048)) for tt_ in (t0 - 15, t0 + 526)))
            for j in range(2):
                pk = "ps%d" % j
                rk = ["ug%d_%d" % (j, sg) for sg in segs] + ["chalo", "dg"]
                for k in range(31):
                    self.mm(self.psA[:, j, :], dg[:, j, k, :], ug[j][:, 16 + t0 + k - 15:16 + t0 + k - 15 + 512],
                            k == 0, k == 30, r=rk, w=[pk])

        def conv_ev(tl):
            S = T[tl % 2]
            sk = "ct%d" % (tl % 2)
            for j in range(2):
                pk = "ps%d" % j
                self.act(S["uc"][:, j, :], self.psA[:, j, :], AF.Identity, r=[pk, ck], w=[sk + "uc%d" % j],
                         bias=cp[:, CDB + j:CDB + j + 1])
                self.act(S["sqb"][:, j, :], self.psA[:, j, :], AF.Square, r=[pk, ck], w=[sk + "sq%d" % j],
                         bias=cp[:, CDB + j:CDB + j + 1])

        def stats(tl):
            S = T[tl % 2]
            sk = "ct%d" % (tl % 2)
            for j in range(2):
                self.mm(self.psA[:, 2, :], self.ones_f, S["uc"][:, j, :], j == 0, j == 1,
                        r=["ones", sk + "uc%d" % j], w=["ps2"])
            for j in range(2):
                self.mm(self.psA[:, 3, :], self.ones_b, S["sqb"][:, j, :], j == 0, j == 1,
                        r=["ones", sk + "sq%d" % j], w=["ps3"])

        def chain(tl):
            S = T[tl % 2]
            sk = "ct%d" % (tl % 2)
            self.act(S["mean"], self.psA[:, 2, :], AF.Copy, r=["ps2"], w=[sk + "mean"], scale=1.0 / 256)
            self.act(S["m2"], self.psA[:, 2, :], AF.Square, r=["ps2"], w=[sk + "m2"], scale=1.0 / 256)
            self.stt(S["var"], self.psA[:, 3, :], 1.0 / 256, S["m2"], ALU.mult, ALU.subtract,
                     r=["ps3", sk + "m2"], w=[sk + "var"])
            self.act(S["var"], S["var"], AF.Ln, r=[sk + "var"], w=[sk + "var"], bias=EPS)
            self.act(S["rstd"], S["var"], AF.Exp, r=[sk + "var"], w=[sk + "rstd"], scale=-0.5)
            for j in range(2):
                self.tt("dve", S["ln"][:, j, :], S["uc"][:, j, :], S["mean"], ALU.subtract,
                        r=[sk + "uc%d" % j, sk + "mean"], w=[sk + "ln%d" % j])
                self.tt("dve", S["ln"][:, j, :], S["ln"][:, j, :], S["rstd"], ALU.mult,
                        r=[sk + "ln%d" % j, sk + "rstd"], w=[sk + "ln%d" % j])
                self.act(S["sub"][:, j, :], S["ln"][:, j, :], AF.Silu, r=[sk + "ln%d" % j, ck], w=[sk + "su%d" % j],
                         scale=cp[:, CLG + j:CLG + j + 1], bias=cp[:, CLB + j:CLB + j + 1])

        def pw(tl):
            t0 = tl * 512
            S = T[tl % 2]
            sk = "ct%d" % (tl % 2)
            for e_ in range(2):
                pk = "ps%d" % (4 + e_)
                for j in range(2):
                    self.mm(self.psA[:, 4 + e_, :], pwb16[:, j, e_ * 128:(e_ + 1) * 128], S["sub"][:, j, :],
                            j == 0, j == 1, r=["pwb16", sk + "su%d" % j], w=[pk])
                self.act(ycf[e_][:, t0:t0 + 512], self.psA[:, 4 + e_, :], AF.Identity, r=[pk, ck],
                         w=["ycf%d_%d" % (e_, tl)], bias=cp[:, CPB + e_:CPB + e_ + 1])

        hy_load(0)
        hy_load(1)
        conv_mm(0)
        conv_ev(0)
        for tl in range(16):
            stats(tl)
            if tl + 1 < 16:
                conv_mm(tl + 1)
            chain(tl)
            if tl + 1 < 16:
                conv_ev(tl + 1)
            pw(tl)
            hy_tile(tl)
            if tl + 2 < 16:
                hy_load(tl + 2)
        for e_ in range(2):
            P.dma(self.YCF_d[e_ * 128:(e_ + 1) * 128, :], ycf[e_], reads=["ycf%d_%d" % (e_, tl) for tl in range(16)],
                  writes=["YCF"], slot="ycf%d" % e_)

    def phase2_rg(self, l):
        P, A, cp = self.P, self.A, self.cp[l]
        ck = "cp%d" % l
        A.reset(self.persist_end)
        pin = A.bf16(L + 4)
        xr = A.bf16(L)
        rb = A.bf16(2 * L).rearrange("p (d t) -> p d t", d=2)
        ib = A.bf16(2 * L).rearrange("p (d t) -> p d t", d=2)
        ob = A.bf16(L)
        obb = A.bf16(L)
        gw = A.f32(4 * 128).rearrange("p (g m) -> p g m", g=4)
        gwb = A.bf16(4 * 128).rearrange("p (g m) -> p g m", g=4)
        rdg = A.bf16(4 * 128).rearrange("p (k m) -> p k m", k=4)
        T = [dict(a=A.f32(2048), m=A.f32(2048)) for _ in range(2)]
        self.memset("dve", pin[:, 0:2], 0.0, w=["rhalo"])
        self.memset("dve", pin[:, L + 2:L + 4], 0.0, w=["rhalo"])
        nb = 0
        for j in range(3):
            P.dma(pin[:, 2:2 + L], self.P_d[(13 + j) * 128:(14 + j) * 128, :], reads=["P"], writes=["rpin"], slot="rpin")
            P.dma(gw, self.rgw_d[l][j], writes=["gw0"], slot="gw")
            self.cpy("dve", gwb, gw, r=["gw0"], w=["gw"])
            c = RCW + j * 4
            for k in range(4):
                self.ts("dve", rdg[:, k, :], self.ident_f, cp[:, c + k:c + k + 1], ALU.mult, r=["ident_f", ck], w=["rdg"])
            for tl in range(16):
                t0 = tl * 512
                b = 6 + tl % 2
                pk = "ps%d" % b
                for k in range(4):
                    self.mm(self.psA[:, b, :], rdg[:, k, :], pin[:, t0 + k:t0 + k + 512], k == 0, k == 3,
                            r=["rpin", "rhalo", "rdg"], w=[pk])
                self.act(xr[:, t0:t0 + 512], self.psA[:, b, :], AF.Identity, r=[pk, ck], w=["xr%d" % tl],
                         bias=cp[:, RCB + j:RCB + j + 1])
            for d in range(2):
                for tl in range(16):
                    t0 = tl * 512
                    for gi in range(2):
                        b = nb % 6
                        nb += 1
                        pk = "ps%d" % b
                        self.mm(self.psA[:, b, :], gwb[:, d * 2 + gi, :], xr[:, t0:t0 + 512],
                                True, True, r=["gw", "xr%d" % tl], w=[pk])
                        dst = rb if gi == 0 else ib
                        bc = (RBA if gi == 0 else RBX) + d * 3 + j
                        self.act(dst[:, d, t0:t0 + 512], self.psA[:, b, :], AF.Sigmoid, r=[pk, ck],
                                 w=[("rb" if gi == 0 else "ib") + "%d_%d" % (d, tl)], bias=cp[:, bc:bc + 1])
            n = 0
            for d in range(2):
                order = [0, 1, 2, 3] if d == 0 else [3, 2, 1, 0]
                cc = d * 3 + j
                for oi, sg in enumerate(order):
                    t0 = sg * 2048
                    S = T[n % 2]
                    sk = "rt%d" % (n % 2)
                    n += 1
                    rks = ["rb%d_%d" % (d, sg * 4 + q) for q in range(4)]
                    iks = ["ib%d_%d" % (d, sg * 4 + q) for q in range(4)]
                    xks = ["xr%d" % (sg * 4 + q) for q in range(4)]
                    rr = rb[:, d, t0:t0 + 2048]
                    self.act(S["a"], rr, AF.Exp, r=rks + [ck], w=[sk + "a"], scale=cp[:, CL + cc:CL + cc + 1])
                    self.act(S["m"], rr, AF.Exp, r=rks + [ck], w=[sk + "m"], scale=cp[:, CL2 + cc:CL2 + cc + 1])
                    self.act(S["m"], S["m"], AF.Ln, r=[sk + "m"], w=[sk + "m"], scale=-1.0, bias=1.0)
                    self.act(S["m"], S["m"], AF.Exp, r=[sk + "m"], w=[sk + "m"], scale=0.5)
                    self.tt("dve", S["m"], S["m"], ib[:, d, t0:t0 + 2048], ALU.mult, r=[sk + "m"] + iks, w=[sk + "m"])
                    self.tt("dve", S["m"], S["m"], xr[:, t0:t0 + 2048], ALU.mult, r=[sk + "m"] + xks, w=[sk + "m"])
                    if d == 0:
                        init = 0.0 if oi == 0 else ob[:, t0 - 1:t0]
                        rd = [sk + "a", sk + "m"] + (["ob%d" % (sg - 1)] if oi > 0 else [])
                        o_, a_, m_ = ob[:, t0:t0 + 2048], S["a"], S["m"]
                        P.op("dve", lambda e, o_=o_, a_=a_, m_=m_, init=init: e.tensor_tensor_scan(
                            out=o_, data0=a_, data1=m_, initial=init, op0=ALU.mult, op1=ALU.add), rd, ["ob%d" % sg])
                    else:
                        init = 0.0 if oi == 0 else obb[:, t0 + 2048:t0 + 2049]
                        rd = [sk + "a", sk + "m"] + (["obb%d" % (sg + 1)] if oi > 0 else [])
                        o_, a_, m_ = obb[:, t0:t0 + 2048][:, ::-1], S["a"][:, ::-1], S["m"][:, ::-1]
                        P.op("dve", lambda e, o_=o_, a_=a_, m_=m_, init=init: e.tensor_tensor_scan(
                            out=o_, data0=a_, data1=m_, initial=init, op0=ALU.mult, op1=ALU.add), rd, ["obb%d" % sg])
            P.dma(self.YRG_d[j * 128:(j + 1) * 128, :], ob, reads=["ob%d" % s_ for s_ in range(4)],
                  writes=["YRG"], slot="ob")
            P.dma(self.YRGB_d[j * 128:(j + 1) * 128, :], obb, reads=["obb%d" % s_ for s_ in range(4)],
                  writes=["YRG"], slot="obb")

    def phase2_gate(self, l):
        P, A = self.P, self.A
        A.reset(self.persist_end)
        gi_ = [A.bf16(L) for _ in range(2)]
        go_ = [A.bf16(L) for _ in range(2)]
        for j in range(8):
            ik, ok = "gi%d" % (j % 2), "go%d" % (j % 2)
            P.dma(gi_[j % 2], self.P_d[(16 + j) * 128:(17 + j) * 128, :], reads=["P"], writes=[ik], slot=ik)
            self.act(go_[j % 2], gi_[j % 2], AF.Silu, r=[ik], w=[ok])
            P.dma(self.GS_d[j * 128:(j + 1) * 128, :], go_[j % 2], reads=[ok], writes=["GS"], slot=ok)

    def phase3_fft(self, l):
        P, A = self.P, self.A
        A.reset(self.persist_end)
        GT4 = self.load_gt()
        Aab = [A.bf16(128 * 3 * NCH) for _ in range(2)]
        Bb = A.bf16(128 * 2 * NCH)
        B4 = Bb.rearrange("p (q c pl) -> p q c pl", pl=2, c=NCH)
        Yy = A.bf16(128 * 2 * NCH)
        Y4 = Yy.rearrange("p (q pl c) -> p q pl c", pl=2, c=NCH)
        KFt = A.bf16(128 * 2 * NCH)
        KF4 = KFt.rearrange("p (q c pl) -> p q c pl", pl=2, c=NCH)
        Ux = A.bf16(NCH * 128).rearrange("p (c l) -> p c l", c=NCH)
        yb = A.bf16(NCH * 128).rearrange("p (c l) -> p c l", c=NCH)
        ta = [A.f32(512).rearrange("p (q c pl) -> p q c pl", pl=2, c=NCH) for _ in range(2)]
        tb = [A.f32(512).rearrange("p (q c pl) -> p q c pl", pl=2, c=NCH) for _ in range(2)]
        Ur = self.U_d.rearrange("c (hi lo) -> hi c lo", lo=128)
        YCr = self.YC_d.rearrange("c (hi lo) -> hi c lo", lo=128)

        def load_U(ch):
            P.dma(Ux[0:64], Ur[:, ch * NCH:(ch + 1) * NCH, :], reads=["U"], writes=["Ux"], slot="Ux")

        def load_KF(ch):
            P.dma(KFt, self.KF_d[ch], reads=["KF"], writes=["KFt"], slot="kfi")

        def akeys(ch):
            return ["A%d_%d" % (ch % 2, c) for c in range(NCH)]

        def S1(ch):
            A4 = Aab[ch % 2].rearrange("p (q c pl) -> p q c pl", pl=3, c=NCH)
            self.fft_s1(Ux, self.F1, A4, akeys(ch), "Ux", "d")

        def S2(ch):
            A4 = Aab[ch % 2].rearrange("p (q c pl) -> p q c pl", pl=3, c=NCH)
            ak = akeys(ch)
            for kg in range(16):
                b = 4 + kg % 4
                pk = "ps%d" % b
                for q8 in range(8):
                    q = kg * 8 + q8
                    o = self.psA[:, b, q8 * 64:(q8 + 1) * 64]
                    self.mm(o, GT4[:, q, 0, :], A4[:, q, :, 0:2], True, False, r=["GT"] + ak, w=[pk])
                    self.mm(o, GT4[:, q, 1, :], A4[:, q, :, 1:3], False, True, r=["GT"] + ak, w=[pk])
                X4 = self.psA[:, b, :].rearrange("p (q c pl) -> p q c pl", pl=2, c=NCH)
                tak, tbk = "ta%d" % (kg % 2), "tb%d" % (kg % 2)
                qs = slice(kg * 8, (kg + 1) * 8)
                self.tt("dve", ta[kg % 2], X4, KF4[:, qs, :, :], ALU.mult, r=[pk, "KFt"], w=[tak])
                self.tt("dve", tb[kg % 2], X4, KF4[:, qs, :, ::-1], ALU.mult, r=[pk, "KFt"], w=[tbk])
                self.tt("pool", Y4[:, qs, 0, :], ta[kg % 2][:, :, :, 0], ta[kg % 2][:, :, :, 1], ALU.subtract,
                        r=[tak], w=["Y%dr" % kg])
                self.tt("pool", Y4[:, qs, 1, :], tb[kg % 2][:, :, :, 0], tb[kg % 2][:, :, :, 1], ALU.add,
                        r=[tbk], w=["Y%di" % kg])

        def I12(ch):
            c0 = ch * NCH
            yall = ["Y%dr" % kg for kg in range(16)] + ["Y%di" % kg for kg in range(16)]
            bk = ["B%d" % c for c in range(NCH)]
            for c in range(NCH):
                b = c % 4
                ps = self.psA[:, b, 0:256]
                pk = "ps%d" % b
                self.mm(ps, Y4[:, :, 0, c], self.FIa, True, False, r=yall + ["FI"], w=[pk])
                self.mm(ps, Y4[:, :, 1, c], self.FIb, False, True, r=yall + ["FI"], w=[pk])
                eng = "dve" if c % 4 == 3 else "act"
                self.cpy(eng, B4[:, :, c, :], ps.rearrange("p (q pl) -> p q pl", pl=2), r=[pk], w=[bk[c]])
            ybw = []
            for lg in range(8):
                b = 4 + lg % 4
                pk = "ps%d" % b
                pv = self.psA[0:64, b, :].rearrange("p (c l) -> p c l", l=16)
                for l16 in range(16):
                    q = lg * 16 + l16
                    o = pv[:, :, l16]
                    self.mm(o, GT4[:, q, 0, 0:64], B4[:, q, :, 0], True, False, r=["GT"] + bk, w=[pk])
                    self.mm(o, GT4[:, q, 1, 0:64], B4[:, q, :, 1], False, True, r=["GT"] + bk, w=[pk])
                wk = "yb_%d" % lg
                ybw.append(wk)
                self.cpy("act" if lg % 2 == 0 else "dve", yb[0:64, :, lg * 16:(lg + 1) * 16], pv, r=[pk], w=[wk])
            P.dma(YCr[:, c0:c0 + NCH, :], yb[0:64], reads=ybw, writes=["YC"], slot="yb")

        load_U(0)
        load_KF(0)
        S1(0)
        for ch in range(12):
            S2(ch)
            if ch + 1 < 12:
                load_U(ch + 1)
                load_KF(ch + 1)
                S1(ch + 1)
            I12(ch)

    def phase4(self, l, s, xsrc):
        P, A, cp = self.P, self.A, self.cp[l]
        ck = "cp%d" % l
        last = (l == DEPTH - 1)
        A.reset(self.persist_end)
        wo = A.bf16(8 * D).rearrange("p (k n) -> p k n", k=8)
        wst = [A.f32(D) for _ in range(2)]
        LD = []
        for i in range(2):
            LD.append(dict(x0=A.bf16(3 * 512).rearrange("p (j t) -> p j t", j=3),
                           u=A.bf16(3 * 512).rearrange("p (j t) -> p j t", j=3),
                           yc=A.bf16(3 * 512).rearrange("p (j t) -> p j t", j=3),
                           ycf=A.bf16(2 * 512).rearrange("p (j t) -> p j t", j=2),
                           yrg=A.bf16(3 * 512).rearrange("p (j t) -> p j t", j=3),
                           yrgb=A.bf16(3 * 512).rearrange("p (j t) -> p j t", j=3),
                           gs=A.bf16(8 * 512).rearrange("p (j t) -> p j t", j=8),
                           xr=A.f32(4 * D).rearrange("p (s d) -> p s d", s=4)))
        yraw = A.bf16(3 * 512).rearrange("p (j t) -> p j t", j=3)
        yr32 = A.f32(3 * 512).rearrange("p (j t) -> p j t", j=3)
        yrs = A.bf16(3 * 512).rearrange("p (j t) -> p j t", j=3)
        sqb = A.bf16(8 * 512).rearrange("p (j t) -> p j t", j=8)
        gsr = A.bf16(8 * 512).rearrange("p (j t) -> p j t", j=8)
        rstd = A.f32(3 * 512).rearrange("p (j t) -> p j t", j=3)
        ym = [A.bf16(8 * 512).rearrange("p (j t) -> p j t", j=8) for _ in range(2)]
        xn = [A.f32(D) for _ in range(2)]
        junk = A.bf16(D)
        fss = A.f32(64)
        fsd = A.f32(64)
        frs = A.f32(64)
        yo = [A.f32(D) for _ in range(2)]
        wok = ["wo%d" % k for k in range(8)]

        def load_wo():
            for k in range(8):
                key = "wst%d" % (k % 2)
                P.dma(wst[k % 2], self.w_out_d[l][:, k, :], writes=[key], slot=key)
                self.ts("dve" if k % 2 == 0 else "pool", wo[:, k, :], wst[k % 2], cp[:, GG + k:GG + k + 1], ALU.mult,
                        r=[key, ck], w=["wo%d" % k])
        ydst = self.y_d[s] if last else self.X1_d[s]
        grp = [0, 0, 0, 1, 1, 2, 2, 2]
        gw = [384.0, 256.0, 384.0]
        gr = [(0, 3), (3, 5), (5, 8)]
        nbc = [0]

        def loads_bf(tl):
            t0 = tl * 512
            S = LD[tl % 2]
            sk = "p4_%d" % (tl % 2)
            for nm, dten, nj in [("x0", self.X0_d, 3), ("u", self.U_d, 3), ("yc", self.YC_d, 3),
                                 ("yrg", self.YRG_d, 3), ("yrgb", self.YRGB_d, 3), ("ycf", self.YCF_d, 2),
                                 ("gs", self.P_d[2048:3072, :], 8)]:
                P.dma(S[nm], dten.rearrange("(j p) t -> p j t", p=128)[:, :, t0:t0 + 512],
                      reads=["X0", "U", "YC", "YCF", "YRG", "P"], writes=[sk + nm], slot=sk + nm)

        def load_xr(tl):
            t0 = tl * 512
            S = LD[tl % 2]
            sk = "p4_%d" % (tl % 2)
            P.dma(S["xr"], xsrc[t0:t0 + 512, :].rearrange("(s p) d -> p s d", p=128), reads=["X1"],
                  writes=[sk + "xr"], slot=sk + "xr")

        def ysrc_of(tl):
            S = LD[tl % 2]
            sk = "p4_%d" % (tl % 2)
            ysrc = [(yraw[:, i, :], "yraw%d" % i) for i in range(3)]
            ysrc += [(S["ycf"][:, j, :], sk + "ycf") for j in range(2)]
            ysrc += [(yrs[:, j, :], "yrs%d" % j) for j in range(3)]
            return ysrc

        def E1(tl):
            S = LD[tl % 2]
            sk = "p4_%d" % (tl % 2)
            for i in range(3):
                self.stt(yr32[:, i, :], S["u"][:, i, :], cp[:, HD + i:HD + i + 1], S["yc"][:, i, :], ALU.mult, ALU.add,
                         r=[sk + "u", sk + "yc", ck], w=["yr32_%d" % i])
                self.tt("pool", yraw[:, i, :], yr32[:, i, :], S["x0"][:, i, :], ALU.mult,
                        r=["yr32_%d" % i, sk + "x0"], w=["yraw%d" % i])
            for j in range(3):
                self.tt("dve", yrs[:, j, :], S["yrg"][:, j, :], S["yrgb"][:, j, :], ALU.add,
                        r=[sk + "yrg", sk + "yrgb"], w=["yrs%d" % j])
            ysrc = ysrc_of(tl)
            for k in range(8):
                self.act(sqb[:, k, :], ysrc[k][0], AF.Square, r=[ysrc[k][1]], w=["sq%d" % k])

        def E2a(tl):
            for g in range(3):
                lo, hi = gr[g]
                for k in range(lo, hi):
                    self.mm(self.psA[:, g, :], self.ones_b, sqb[:, k, :], k == lo, k == hi - 1,
                            r=["ones", "sq%d" % k], w=["ps%d" % g])
                self.act(rstd[:, g, :], self.psA[:, g, :], AF.Ln, r=["ps%d" % g], w=["rstd%d" % g],
                         scale=1.0 / gw[g], bias=EPS)
                self.act(rstd[:, g, :], rstd[:, g, :], AF.Exp, r=["rstd%d" % g], w=["rstd%d" % g], scale=-0.5)

        def E2b1(tl):
            S = LD[tl % 2]
            sk = "p4_%d" % (tl % 2)
            for k in range(8):
                self.tt("pool" if k in (3, 6) else "dve", gsr[:, k, :], S["gs"][:, k, :], rstd[:, grp[k], :], ALU.mult,
                        r=[sk + "gs", "rstd%d" % grp[k]], w=["gsr%d" % k])

        def E2b2(tl):
            ysrc = ysrc_of(tl)
            for k in range(8):
                self.tt("dve", ym[tl % 2][:, k, :], ysrc[k][0], gsr[:, k, :], ALU.mult,
                        r=[ysrc[k][1], "gsr%d" % k], w=["ym%d_%d" % (tl % 2, k)])

        mbank = {}

        def M_mm(tl, half):
            ymk = ["ym%d_%d" % (tl % 2, k) for k in range(8)]
            for s4 in (2 * half, 2 * half + 1):
                for nh in range(2):
                    b = 3 + nbc[0] % 5
                    nbc[0] += 1
                    mbank[(tl, s4, nh)] = b
                    pk = "ps%d" % b
                    for k in range(8):
                        self.mm(self.psA[:, b, :], ym[tl % 2][:, k, s4 * 128:(s4 + 1) * 128],
                                wo[:, k, nh * 512:(nh + 1) * 512], k == 0, k == 7, r=ymk + wok, w=[pk])

        def M_add(tl, half):
            t0 = tl * 512
            S = LD[tl % 2]
            sk = "p4_%d" % (tl % 2)
            for s4 in (2 * half, 2 * half + 1):
                it = tl * 4 + s4
                xk = "xn%d" % (it % 2)
                for nh in range(2):
                    b = mbank[(tl, s4, nh)]
                    pk = "ps%d" % b
                    self.tt("dve", xn[it % 2][:, nh * 512:(nh + 1) * 512], self.psA[:, b, :],
                            S["xr"][:, s4, nh * 512:(nh + 1) * 512], ALU.add, r=[pk, sk + "xr"], w=[xk + "_%d" % nh])
                xks = [xk + "_0", xk + "_1"]
                rows = slice(t0 + s4 * 128, t0 + (s4 + 1) * 128)
                if not last:
                    P.dma(ydst[rows, :], xn[it % 2], reads=xks, writes=["X1"], slot=xk)
                else:
                    c1 = slice(it % 64, it % 64 + 1)
                    self.act(junk, xn[it % 2], AF.Square, r=xks, w=["fss%d" % it, "junk"], accum=fss[:, c1])
                    self.ts("pool", fsd[:, c1], fss[:, c1], 1.0 / D, ALU.mult, EPS, ALU.add,
                            r=["fss%d" % it], w=["fsd%d" % it])
                    self.tt("pool", frs[:, c1], fsd[:, c1], self.negh[:, 0:1], ALU.pow,
                            r=["fsd%d" % it, "negh"], w=["frs%d" % it])
                    ok = "yo%d" % (it % 2)
                    self.stt(yo[it % 2], xn[it % 2], frs[:, c1], self.fg, ALU.mult, ALU.mult,
                             r=xks + ["frs%d" % it, "fg"], w=[ok])
                    P.dma(ydst[rows, :], yo[it % 2], reads=[ok], writes=["Y"], slot=ok)

        load_wo()
        loads_bf(0)
        load_xr(0)
        loads_bf(1)
        load_xr(1)
        E1(0)
        E2a(0)
        E2b1(0)
        E2b2(0)
        for tl in range(16):
            nx = tl + 1 < 16
            if tl + 2 < 16:
                loads_bf(tl + 2)
            if nx:
                E1(tl + 1)
                E2a(tl + 1)
            M_mm(tl, 0)
            if nx:
                E2b1(tl + 1)
            M_add(tl, 0)
            M_mm(tl, 1)
            if nx:
                E2b2(tl + 1)
            M_add(tl, 1)
            if tl + 2 < 16:
                load_xr(tl + 2)


def _consts():
    bf = ml_dtypes.bfloat16
    N = 2 * L
    hi = np.arange(128)[:, None]
    q = np.arange(128)[None, :]
    phi = 2 * np.pi * ((hi * q) % 128) / 128.0
    F1 = np.stack([np.cos(phi), -np.sin(phi), -np.cos(phi)], axis=-1).reshape(128, 384)
    F1b = np.stack([-np.sin(phi), np.cos(phi), np.sin(phi)], axis=-1).reshape(128, 384)[0:64]
    p = np.arange(128)[:, None, None]
    qq = np.arange(128)[None, :, None]
    r = np.arange(128)[None, None, :]
    th = 2 * np.pi * ((p * (qq + 128 * r)) % N) / N
    GT = np.stack([np.cos(th), np.sin(th)], axis=2).reshape(128, 128 * 2 * 128)
    kh = np.arange(128)[:, None]
    lo = np.arange(128)[None, :]
    psi = 2 * np.pi * kh * lo / 128.0
    FIa = np.stack([np.cos(psi), -np.sin(psi)], axis=-1).reshape(128, 256)
    FIb = np.stack([-np.sin(psi), -np.cos(psi)], axis=-1).reshape(128, 256)
    t = np.linspace(0.0, 1.0, L, dtype=np.float32).astype(np.float64)
    w = (2.0 * np.pi / L) * np.arange(L, dtype=np.float64)
    f = np.linspace(1e-4, 15.0, 16, dtype=np.float32).astype(np.float64)
    zT = np.concatenate([t[None, :], np.cos(f[:, None] * w[None, :]), -np.sin(f[:, None] * w[None, :])], axis=0)
    max_decay = math.log(1e-2) / 0.3
    min_decay = math.log(1e-2) / 1.5
    deltas = np.abs(np.linspace(min_decay, max_decay, 384, dtype=np.float32)).astype(np.float32)
    return dict(F1=F1.astype(bf), F1b=F1b.astype(bf), GT=GT.astype(bf), FIa=FIa.astype(bf), FIb=FIb.astype(bf),
                zT=zT.astype(np.float32), tt=np.ascontiguousarray(np.broadcast_to(t.astype(np.float32), (128, L))),
                deltas=deltas, ident=np.eye(128, dtype=np.float32))


def _pack_cp(inp, C):
    cp = np.zeros((DEPTH, 128, NCP), np.float32)
    f = lambda a: np.asarray(a, np.float32)
    for l in range(DEPTH):
        def ch(v, n):
            return f(v).reshape(n, 128).T
        w = f(inp["hy_conv_w"][l])
        for j in range(9):
            for k in range(3):
                cp[l, :, HCW + j * 3 + k] = w[k, j * 128:(j + 1) * 128]
        cp[l, :, HCB:HCB + 9] = ch(inp["hy_conv_b"][l], 9)
        cp[l, :, HD:HD + 3] = ch(inp["hy_d"][l], 3)
        w = f(inp["cf_dw_w"][l])
        for j in range(2):
            for k in range(31):
                cp[l, :, CDW + j * 31 + k] = w[k, j * 128:(j + 1) * 128]
        cp[l, :, CDB:CDB + 2] = ch(inp["cf_dw_b"][l], 2)
        cp[l, :, CLG:CLG + 2] = ch(inp["cf_ln_g"][l], 2)
        cp[l, :, CLB:CLB + 2] = ch(inp["cf_ln_b"][l], 2)
        cp[l, :, CPB:CPB + 2] = ch(inp["cf_pw_b"][l], 2)
        w = f(inp["rg_conv_w"][l])
        for j in range(3):
            for k in range(4):
                cp[l, :, RCW + j * 4 + k] = w[k, j * 128:(j + 1) * 128]
        cp[l, :, RCB:RCB + 3] = ch(inp["rg_conv_b"][l], 3)
        for d in range(2):
            cp[l, :, RBA + d * 3:RBA + d * 3 + 3] = ch(inp["rg_ba"][l, d], 3)
            cp[l, :, RBX + d * 3:RBX + d * 3 + 3] = ch(inp["rg_bx"][l, d], 3)
            cp[l, :, RLAM + d * 3:RLAM + d * 3 + 3] = ch(inp["rg_lam"][l, d], 3)
        cp[l, :, GG:GG + 8] = ch(inp["grp_g"][l], 8)
        cp[l, :, NG:NG + 8] = ch(inp["norm_g"][l], 8)
        cp[l, 0:64, MLPB + 0] = f(inp["hy_b1"][l])
        cp[l, 0:64, MLPB + 1] = f(inp["hy_b2"][l])
        cp[l, 0:64, MLPB + 2] = f(inp["hy_b3"][l])
        cp[l, 0:64, MLPB + 3] = f(inp["hy_freq"][l])
        cp[l, :, DEL:DEL + 3] = -C["deltas"].reshape(3, 128).T
    return cp


_NC_CACHE = {}


def kernel(**inp):
    C = _consts()
    f = lambda a: np.ascontiguousarray(np.asarray(a, np.float32))
    w_in = f(inp["w_in"]).reshape(DEPTH, 8, 128, 3072).transpose(0, 2, 1, 3)
    w_out = f(inp["w_out"]).reshape(DEPTH, 8, 128, 1024).transpose(0, 2, 1, 3)
    cp = _pack_cp(inp, C)
    rgw = np.zeros((DEPTH, 3, 128, 4, 128), np.float32)
    wa, wx = f(inp["rg_wa"]), f(inp["rg_wx"])
    for l in range(DEPTH):
        for d in range(2):
            for h in range(6):
                j, o = h // 2, (h % 2) * 64
                rgw[l, j, o:o + 64, d * 2 + 0, o:o + 64] = wa[l, d, h]
                rgw[l, j, o:o + 64, d * 2 + 1, o:o + 64] = wx[l, d, h]
    pww = f(inp["cf_pw_w"]).reshape(DEPTH, 2, 128, 256).transpose(0, 2, 1, 3)
    common = {
        "w_in": np.ascontiguousarray(w_in), "w_out": np.ascontiguousarray(w_out), "cp": cp,
        "fg": np.ascontiguousarray(np.broadcast_to(f(inp["final_g"])[None, :], (128, D))),
        "hw1": f(inp["hy_w1"]), "hw2": f(inp["hy_w2"]), "hw3": f(inp["hy_w3"]), "hw4": f(inp["hy_w4"]),
        "zT": C["zT"], "tt": C["tt"], "rgw": rgw, "pww": np.ascontiguousarray(pww), "ident": C["ident"],
        "F1": C["F1"], "F1b": C["F1b"], "GT": C["GT"], "FIa": C["FIa"], "FIb": C["FIb"],
    }
    xp, xs = f(inp["x_prompt"]), f(inp["x_sample"])
    in_maps = []
    for i in range(8):
        m = dict(common)
        m["xa"] = xp[i]
        m["xb"] = xs[i % 4]
        in_maps.append(m)
    if "nc" not in _NC_CACHE:
        _NC_CACHE["nc"] = Builder().build()
    res = run_bass_kernel_spmd(_NC_CACHE["nc"], in_maps, core_ids=list(range(8)))
    yp = np.stack([np.asarray(res.results[i]["ya"], np.float32) for i in range(8)], axis=0)
    ys = np.stack([np.asarray(res.results[i]["yb"], np.float32) for i in range(4)], axis=0)
    return (yp, ys)
```

```python
import math
import contextlib
import numpy as np
import ml_dtypes
import concourse.bass as bass
import concourse.mybir as mybir
from concourse.bass_utils import run_bass_kernel_spmd

F32 = mybir.dt.float32
BF16 = mybir.dt.bfloat16
F32R = mybir.dt.float32r
AF = mybir.ActivationFunctionType
ALU = mybir.AluOpType

L = 8192
D = 1024
DEPTH = 2
EPS = 1e-6
NCH = 32
ENGS = ["pe", "act", "dve", "pool", "sp"]

HCW, HCB, HD, CDW, CDB, CLG, CLB, CPB = 0, 27, 36, 39, 101, 103, 105, 107
RCW, RCB, RBA, RBX, RLAM, GG, NG, MLPB, DEL = 109, 121, 124, 130, 136, 142, 150, 158, 162
CL, CL2, HFR, QFR, HFB, QFB = 165, 171, 177, 178, 179, 182
NCP = 186


class Prog:
    def __init__(self, nc):
        self.nc = nc
        self.ops = []

    def op(self, eng, fn, reads=(), writes=(), dma=None):
        self.ops.append((eng, fn, tuple(reads), tuple(writes), dma))

    def barrier(self):
        self.ops.append(("barrier", None, (), (), None))

    def dma(self, out, in_, reads=(), writes=(), slot=None, eng="sp", **kw):
        def fn(e, out=out, in_=in_, kw=kw):
            return e.dma_start(out=out, in_=in_, **kw)
        self.op(eng, fn, reads, writes, dma=slot)

    def emit(self, stack):
        nc = self.nc
        esem = {e: stack.enter_context(nc.semaphore("s_" + e)) for e in ENGS}
        slots = []
        seen = set()
        for o in self.ops:
            if o[4] is not None and o[4] not in seen:
                seen.add(o[4])
                slots.append(o[4])
        dsem = {s: stack.enter_context(nc.semaphore("d_%d" % i)) for i, s in enumerate(slots)}
        ecount = {e: 0 for e in ENGS}
        dcount = {s: 0 for s in slots}
        last_w = {}
        readers = {}
        streams = {e: [] for e in ENGS}
        waited = {e: {} for e in ENGS}
        pending_barrier = {e: None for e in ENGS}
        for (eng, fn, reads, writes, dma) in self.ops:
            if eng == "barrier":
                snap = [("e", e2, ecount[e2], e2) for e2 in ENGS if ecount[e2] > 0]
                snap += [("d", s, dcount[s], None) for s in slots if dcount[s] > 0]
                for e in ENGS:
                    pending_barrier[e] = snap
                continue
            deps = []
            if pending_barrier[eng] is not None:
                deps.extend(pending_barrier[eng])
                pending_barrier[eng] = None
            for k in reads:
                if k in last_w:
                    deps.append(last_w[k])
            for k in writes:
                if k in last_w:
                    deps.append(last_w[k])
                for (kk, ss), (vv, pe_) in readers.get(k, {}).items():
                    deps.append((kk, ss, vv, pe_))
            if dma is not None and dcount[dma] > 0:
                deps.append(("d", dma, dcount[dma], None))
            waits = []
            w = waited[eng]
            for (kind, sid, val, peng) in deps:
                if kind == "e" and peng == eng and eng == "pe":
                    continue
                if w.get((kind, sid), 0) >= val:
                    continue
                w[(kind, sid)] = val
                waits.append((kind, sid, val))
            if dma is not None:
                dcount[dma] += 16
                tok = ("d", dma, dcount[dma], None)
            else:
                ecount[eng] += 1
                tok = ("e", eng, ecount[eng], eng)
            streams[eng].append((waits, fn, tok))
            for k in writes:
                last_w[k] = tok
                readers[k] = {}
            for k in reads:
                if k in writes:
                    continue
                rd = readers.setdefault(k, {})
                if rd.get((tok[0], tok[1]), (0, None))[0] < tok[2]:
                    rd[(tok[0], tok[1])] = (tok[2], tok[3])
        final_waits = [("d", s, dcount[s]) for s in slots if dcount[s] > 0]
        self.stats = {e: len(streams[e]) for e in ENGS}

        def semof(kind, sid):
            return esem[sid] if kind == "e" else dsem[sid]

        with nc.Block() as block:
            def make(e):
                def body(engine):
                    for (waits, fn, tok) in streams[e]:
                        for (kind, sid, val) in waits:
                            engine.wait_ge(semof(kind, sid), val)
                        ins = fn(engine)
                        if tok[0] == "e":
                            ins.then_inc(esem[e], 1)
                        else:
                            ins.then_inc(dsem[tok[1]], 16)
                    if e == "sp":
                        for (kind, sid, val) in final_waits:
                            engine.wait_ge(semof(kind, sid), val)
                        for e2 in ENGS:
                            if e2 != "sp" and ecount[e2] > 0:
                                engine.wait_ge(esem[e2], ecount[e2])
                return body
            block.tensor(make("pe"))
            block.scalar(make("act"))
            block.vector(make("dve"))
            block.gpsimd(make("pool"))
            block.sync(make("sp"))


class Arena:
    def __init__(self, ap32):
        self.ap = ap32
        self.W = ap32.shape[1]
        self.off = 0

    def reset(self, off):
        self.off = off

    def f32(self, cols):
        v = self.ap[:, self.off:self.off + cols]
        self.off += cols
        assert self.off <= self.W, (self.off, self.W)
        return v

    def bf16(self, cols):
        w = (cols + 1) // 2
        v = self.ap[:, self.off:self.off + w].bitcast(BF16)
        self.off += w
        assert self.off <= self.W, (self.off, self.W)
        return v


class Builder:
    def __init__(self, dbg=False):
        self.dbg = dbg
        self.nc = nc = bass.Bass("TRN2", target_bir_lowering=False)
        self.P = Prog(nc)
        self.uid = 0
        di = lambda n, s, dt=F32: nc.dram_tensor(n, list(s), dt, kind="ExternalInput").ap()
        do = lambda n, s, dt=F32: nc.dram_tensor(n, list(s), dt, kind="ExternalOutput").ap()
        sk = "ExternalOutput" if dbg else "Internal"
        ds = lambda n, s, dt=BF16: nc.dram_tensor(n, list(s), dt, kind=sk).ap()
        self.x_d = [di("xa", [L, D]), di("xb", [L, D])]
        self.y_d = [do("ya", [L, D]), do("yb", [L, D])]
        self.w_in_d = di("w_in", [DEPTH, 128, 8, 3072])
        self.w_out_d = di("w_out", [DEPTH, 128, 8, 1024])
        self.cp_d = di("cp", [DEPTH, 128, NCP])
        self.fg_d = di("fg", [128, D])
        self.w1_d = di("hw1", [DEPTH, 33, 64])
        self.w2_d = di("hw2", [DEPTH, 64, 64])
        self.w3_d = di("hw3", [DEPTH, 64, 64])
        self.w4_d = di("hw4", [DEPTH, 64, 768])
        self.zT_d = di("zT", [33, L])
        self.tt_d = di("tt", [128, L])
        self.rgw_d = di("rgw", [DEPTH, 3, 128, 4, 128])
        self.pww_d = di("pww", [DEPTH, 128, 2, 256])
        self.ident_d = di("ident", [128, 128])
        self.F1_d = di("F1", [128, 384], BF16)
        self.F1b_d = di("F1b", [64, 384], BF16)
        self.GT_d = di("GT", [128, 128 * 2 * 128], BF16)
        self.FIa_d = di("FIa", [128, 256], BF16)
        self.FIb_d = di("FIb", [128, 256], BF16)
        self.P_d = ds("P_s", [3072, L])
        self.U_d = ds("U_s", [384, L])
        self.X0_d = ds("X0_s", [384, L])
        self.YC_d = ds("YC_s", [384, L])
        self.YCF_d = ds("YCF_s", [256, L])
        self.YRG_d = ds("YRG_s", [384, L])
        self.YRGB_d = ds("YRGB_s", [384, L])
        self.GS_d = ds("GS_s", [1024, L])
        self.HT_d = ds("HT_s", [768, L])
        self.KF_d = ds("KF_s", [12, 128, 128 * 2 * NCH])
        self.X1_d = nc.dram_tensor("X1_s", [2, L, D], F32, kind=sk).ap()

    def ts(self, eng, out, in0, s1, op0, s2=None, op1=None, r=(), w=()):
        if eng == "pool" and op1 is None:
            s2, op1 = 0.0, ALU.add
        if op1 is None:
            fn = lambda e: e.tensor_scalar(out=out, in0=in0, scalar1=s1, scalar2=None, op0=op0)
        else:
            fn = lambda e: e.tensor_scalar(out=out, in0=in0, scalar1=s1, scalar2=s2, op0=op0, op1=op1)
        self.P.op(eng, fn, r, w)

    def stt(self, out, in0, scalar, in1, op0, op1, r=(), w=()):
        self.P.op("dve", lambda e: e.scalar_tensor_tensor(out=out, in0=in0, scalar=scalar, in1=in1,
                                                          op0=op0, op1=op1), r, w)

    def tt(self, eng, out, in0, in1, op, r=(), w=()):
        self.P.op(eng, lambda e: e.tensor_tensor(out=out, in0=in0, in1=in1, op=op), r, w)

    def act(self, out, in_, func, r=(), w=(), scale=None, bias=None, accum=None):
        kw = {}
        if scale is not None:
            kw["scale"] = scale
        if bias is not None:
            kw["bias"] = bias
        if accum is not None:
            kw["accum_out"] = accum
        self.P.op("act", lambda e: e.activation(out=out, in_=in_, func=func, **kw), r, w)

    def cpy(self, eng, out, in_, r=(), w=()):
        if eng == "act":
            self.P.op("act", lambda e: e.activation(out=out, in_=in_, func=AF.Copy), r, w)
        else:
            self.P.op(eng, lambda e: e.tensor_copy(out=out, in_=in_), r, w)

    def mm(self, out, lhsT, rhs, start, stop, r=(), w=()):
        self.P.op("pe", lambda e: e.matmul(out=out, lhsT=lhsT, rhs=rhs, start=start, stop=stop), r, w)

    def tr(self, out, in_, ident, r=(), w=()):
        self.P.op("pe", lambda e: e.transpose(out=out, in_=in_, identity=ident), r, w)

    def memset(self, eng, ap, val, r=(), w=()):
        self.P.op(eng, lambda e: e.memset(ap, val), r, w)

    def recip(self, out, in_, r=(), w=()):
        self.P.op("dve", lambda e: e.reciprocal(out=out, in_=in_), r, w)

    def k(self, name):
        self.uid += 1
        return "%s#%d" % (name, self.uid)

    def build(self):
        nc = self.nc
        with contextlib.ExitStack() as st:
            arena_t = st.enter_context(nc.sbuf_tensor("arena", [128, 50 * 1024], F32))
            psA = st.enter_context(nc.psum_tensor("psA", [128, 8, 512], F32))
            self.psA = psA
            self.pbv = [psA[:, 6 + i, :].bitcast(BF16) for i in range(2)]
            self.A = A = Arena(arena_t[:])
            self.ident_f = A.f32(128)
            self.ident_b = A.bf16(128)
            self.ones_f = A.f32(128)
            self.ones_b = A.bf16(128)
            self.negh = A.f32(2)
            self.cp = [A.f32(NCP), A.f32(NCP)]
            self.fg = A.f32(D)
            self.F1 = A.bf16(384)
            self.F1b = A.bf16(384)
            self.FIa = A.bf16(256)
            self.FIb = A.bf16(256)
            self.persist_end = A.off
            self.setup()
            stop = getattr(self, "stop_after", None)
            done = False
            for l in range(DEPTH):
                if done:
                    break
                self.P.barrier()
                self.filter_gen(l)
                self.P.barrier()
                self.filter_fft(l)
                for s in range(2):
                    xsrc = self.x_d[s] if l == 0 else self.X1_d[s]
                    for nm, fn in [("p1", lambda: self.phase1(l, xsrc)), ("conf", lambda: self.phase2_conf(l)),
                                   ("rg", lambda: self.phase2_rg(l)), ("fft", lambda: self.phase3_fft(l)),
                                   ("p4", lambda: self.phase4(l, s, xsrc))]:
                        self.P.barrier()
                        fn()
                        if stop == nm:
                            done = True
                            break
                    if done:
                        break
            self.P.emit(st)
        return nc

    def setup(self):
        P = self.P
        P.dma(self.ident_f, self.ident_d, writes=["ident_f"], slot="c0")
        P.dma(self.fg, self.fg_d, writes=["fg"], slot="c1")
        P.dma(self.F1, self.F1_d, writes=["F1"], slot="c2")
        P.dma(self.F1b[0:64, :], self.F1b_d, writes=["F1"], slot="c3")
        P.dma(self.FIa, self.FIa_d, writes=["FI"], slot="c4")
        P.dma(self.FIb, self.FIb_d, writes=["FI"], slot="c5")
        self.cpy("dve", self.ident_b, self.ident_f, r=["ident_f"], w=["ident_b"])
        self.memset("dve", self.ones_f, 1.0, w=["ones"])
        self.memset("dve", self.ones_b, 1.0, w=["ones"])
        self.memset("dve", self.negh, -0.5, w=["negh"])
        for l in range(DEPTH):
            cp = self.cp[l]
            ck = "cp%d" % l
            P.dma(cp[:, 0:DEL + 3], self.cp_d[l][:, 0:DEL + 3], writes=[ck], slot="c6")
            self.act(cp[:, CL:CL + 6], cp[:, RLAM:RLAM + 6], AF.Exp, r=[ck], w=[ck], scale=-1.0)
            self.act(cp[:, CL:CL + 6], cp[:, CL:CL + 6], AF.Ln, r=[ck], w=[ck], bias=1.0)
            self.ts("dve", cp[:, CL2:CL2 + 6], cp[:, CL:CL + 6], -16.0, ALU.mult, r=[ck], w=[ck])
            self.ts("dve", cp[:, CL:CL + 6], cp[:, CL:CL + 6], -8.0, ALU.mult, r=[ck], w=[ck])
            fr = cp[:, MLPB + 3:MLPB + 4]
            self.ts("dve", cp[:, HFR:HFR + 1], fr, 0.5, ALU.mult, r=[ck], w=[ck])
            self.ts("dve", cp[:, QFR:QFR + 1], fr, 0.25, ALU.mult, r=[ck], w=[ck])
            self.ts("dve", cp[:, HFB:HFB + 3], cp[:, MLPB:MLPB + 3], cp[:, HFR:HFR + 1], ALU.mult, r=[ck], w=[ck])
            self.ts("dve", cp[:, QFB:QFB + 3], cp[:, MLPB:MLPB + 3], cp[:, QFR:QFR + 1], ALU.mult, r=[ck], w=[ck])

    def filter_gen(self, l):
        P, A, cp = self.P, self.A, self.cp[l]
        ck = "cp%d" % l
        A.reset(self.persist_end)
        zT = A.f32(L)
        tt = A.f32(L)
        w1 = A.f32(64)
        w2 = A.f32(64)
        w3 = A.f32(64)
        w4 = A.f32(768)
        w4b = A.bf16(768)
        P.dma(zT[0:33, :], self.zT_d, writes=["zT"], slot="f0")
        P.dma(tt, self.tt_d, writes=["tt"], slot="f1")
        P.dma(w1[0:33, :], self.w1_d[l], writes=["fw"], slot="f2")
        P.dma(w2[0:64, :], self.w2_d[l], writes=["fw"], slot="f3")
        P.dma(w3[0:64, :], self.w3_d[l], writes=["fw"], slot="f4")
        P.dma(w4[0:64, :], self.w4_d[l], writes=["fw"], slot="f5")
        self.cpy("dve", w4b[0:64, :], w4[0:64, :], r=["fw"], w=["fw4b"])
        sets = []
        for i in range(2):
            sets.append(dict(s2=A.f32(512), s4=A.f32(512), hd=[A.f32(512), A.f32(512), A.bf16(512)],
                             dec=A.f32(3 * 512), stg=A.bf16(6 * 512)))
        HTv = self.HT_d.rearrange("(j p) t -> p j t", p=128)
        zcol = A.bf16(8).rearrange("p (j t) -> p j t", j=4)
        self.memset("dve", zcol, 0.0, w=["zcol"])
        P.dma(HTv[:, 3:6, 0:1], zcol[:, 0:3, 0:1], reads=["zcol"], writes=["HTz"], slot="fz",
              allow_slow_non_contiguous=True)
        wts = [w1[0:33, :], w2[0:64, :], w3[0:64, :]]
        for pair in range(8):
            tiles = [2 * pair, 2 * pair + 1]
            prev = {ti: zT[0:33, ti * 512:(ti + 1) * 512] for ti in tiles}
            for li in range(3):
                for ti in tiles:
                    S = sets[ti % 2]
                    sk = "fs%d" % (ti % 2)
                    ps = self.psA[0:64, ti % 2, :]
                    pk = "ps%d" % (ti % 2)
                    self.mm(ps, wts[li], prev[ti], True, True, r=["zT", "fw", sk + "hd"], w=[pk])
                    self.act(S["s2"][0:64, :], ps, AF.Sin, r=[pk, ck], w=[sk + "s2"],
                             scale=cp[0:64, HFR:HFR + 1], bias=cp[0:64, HFB + li:HFB + li + 1])
                    self.act(S["s4"][0:64, :], ps, AF.Sin, r=[pk, ck], w=[sk + "s4"],
                             scale=cp[0:64, QFR:QFR + 1], bias=cp[0:64, QFB + li:QFB + li + 1])
                    self.tt("dve", S["s4"][0:64, :], S["s4"][0:64, :], S["s4"][0:64, :], ALU.mult,
                            r=[sk + "s4"], w=[sk + "s4"])
                    self.ts("dve", S["s4"][0:64, :], S["s4"][0:64, :], -4.0, ALU.mult, 2.0, ALU.add,
                            r=[sk + "s4"], w=[sk + "s4"])
                    self.tt("dve", S["hd"][li][0:64, :], S["s4"][0:64, :], S["s2"][0:64, :], ALU.mult,
                            r=[sk + "s4", sk + "s2"], w=[sk + "hd"])
                    prev[ti] = S["hd"][li][0:64, :]
            for ti in tiles:
                S = sets[ti % 2]
                sk = "fs%d" % (ti % 2)
                t0 = ti * 512
                dec3 = S["dec"].rearrange("p (j t) -> p j t", j=3)
                for j in range(3):
                    self.act(dec3[:, j, :], tt[:, t0:t0 + 512], AF.Exp, r=["tt", ck], w=[sk + "dec"],
                             scale=cp[:, DEL + j:DEL + j + 1])
            for ti in tiles:
                S = sets[ti % 2]
                sk = "fs%d" % (ti % 2)
                t0 = ti * 512
                dec3 = S["dec"].rearrange("p (j t) -> p j t", j=3)
                stg = S["stg"].rearrange("p (j t) -> p j t", j=6)
                for j in range(6):
                    ps = self.psA[:, 2 + j % 4, :]
                    pk = "ps%d" % (2 + j % 4)
                    self.mm(ps, w4b[0:64, j * 128:(j + 1) * 128], prev[ti], True, True, r=["fw4b", sk + "hd"], w=[pk])
                    o_ = stg[:, j, :] if j < 3 else stg[:, j, ::-1]
                    self.tt("dve", o_, ps, dec3[:, j % 3, :], ALU.mult, r=[pk, sk + "dec"], w=[sk + "stg%d" % j])
                P.dma(HTv[:, 0:3, t0:t0 + 512], stg[:, 0:3, :], reads=[sk + "stg%d" % j for j in range(3)],
                      writes=["HTf%d" % ti], slot=sk + "o")
                J0 = 7681 - t0
                bk = [sk + "stg%d" % j for j in range(3, 6)]
                if ti == 0:
                    P.dma(HTv[:, 3:6, J0:J0 + 511], stg[:, 3:6, 0:511], reads=bk, writes=["HTb%d" % ti], slot=sk + "ob")
                else:
                    P.dma(HTv[:, 3:6, J0:J0 + 512], stg[:, 3:6, :], reads=bk, writes=["HTb%d" % ti] + (["HTz"] if ti == 15 else []),
                          slot=sk + "ob")

    def load_gt(self):
        GT = self.A.bf16(128 * 2 * 128)
        self.P.dma(GT, self.GT_d, writes=["GT"], slot="gt")
        return GT.rearrange("p (q pl r) -> p q pl r", pl=2, r=128)

    def fft_s1(self, Ux, F1, A4, akeys, ukey, tag, K=64):
        ukeys = ukey if isinstance(ukey, list) else [ukey]
        for c in range(NCH):
            b = c % 4
            ps = self.psA[:, b, 0:384]
            self.mm(ps, Ux[0:K, c, :], F1[0:K, :], True, True, r=ukeys + ["F1"], w=["ps%d" % b])
            eng = "dve" if c % 4 == 3 else "act"
            self.cpy(eng, A4[:, :, c, :], ps.rearrange("p (q pl) -> p q pl", pl=3),
                     r=["ps%d" % b], w=[akeys[c]])

    def filter_fft(self, l):
        P, A = self.P, self.A
        A.reset(self.persist_end)
        GT4 = self.load_gt()
        Af = [A.bf16(128 * 3 * NCH) for _ in range(2)]
        KFt = [A.bf16(128 * 2 * NCH) for _ in range(2)]
        Uf = [A.bf16(NCH * 128).rearrange("p (c l) -> p c l", c=NCH) for _ in range(2)]
        HTr = self.HT_d.rearrange("c (hi lo) -> hi c lo", lo=128)
        htk = ["HTf%d" % t for t in range(16)] + ["HTb%d" % t for t in range(16)] + ["HTz"]

        def loads(ch):
            c0 = ch * NCH
            P.dma(Uf[ch % 2][0:64], HTr[:, c0:c0 + NCH, :], reads=htk, writes=["Uf%da" % (ch % 2)], slot="Uf%da" % (ch % 2))
            P.dma(Uf[ch % 2][64:128], HTr[:, 384 + c0:384 + c0 + NCH, :], reads=htk, writes=["Uf%db" % (ch % 2)],
                  slot="Uf%db" % (ch % 2))

        def s1(ch):
            Af4_ = Af[ch % 2].rearrange("p (q c pl) -> p q c pl", pl=3, c=NCH)
            afk_ = ["Af%d_%d" % (ch % 2, c) for c in range(NCH)]
            self.fft_s1(Uf[ch % 2], self.F1, Af4_, afk_, ["Uf%da" % (ch % 2), "Uf%db" % (ch % 2)], "f", K=128)

        loads(0)
        loads(1)
        s1(0)
        for ch in range(12):
            if ch + 1 < 12:
                s1(ch + 1)
            if ch + 2 < 12:
                loads(ch + 2)
            Af4 = Af[ch % 2].rearrange("p (q c pl) -> p q c pl", pl=3, c=NCH)
            afk = ["Af%d_%d" % (ch % 2, c) for c in range(NCH)]
            kfk = "KFt%d" % (ch % 2)
            KF4 = KFt[ch % 2].rearrange("p (q c pl) -> p q c pl", pl=2, c=NCH)
            for kg in range(16):
                b = 4 + kg % 4
                pk = "ps%d" % b
                for q8 in range(8):
                    q = kg * 8 + q8
                    o = self.psA[:, b, q8 * 64:(q8 + 1) * 64]
                    self.mm(o, GT4[:, q, 0, :], Af4[:, q, :, 0:2], True, False, r=["GT"] + afk, w=[pk])
                    self.mm(o, GT4[:, q, 1, :], Af4[:, q, :, 1:3], False, True, r=["GT"] + afk, w=[pk])
                X4 = self.psA[:, b, :].rearrange("p (q c pl) -> p q c pl", pl=2, c=NCH)
                if kg % 2 == 0:
                    self.act(KF4[:, kg * 8:(kg + 1) * 8, :, :], X4, AF.Copy, r=[pk], w=[kfk + "_%d" % kg],
                             scale=1.0 / 16384.0)
                else:
                    self.ts("dve", KF4[:, kg * 8:(kg + 1) * 8, :, :], X4, 1.0 / 16384.0, ALU.mult,
                            r=[pk], w=[kfk + "_%d" % kg])
            P.dma(self.KF_d[ch], KFt[ch % 2], reads=[kfk + "_%d" % kg for kg in range(16)], writes=["KF"], slot=kfk)

    def phase1(self, l, xsrc):
        P, A, cp = self.P, self.A, self.cp[l]
        ck = "cp%d" % l
        A.reset(self.persist_end)
        wbf = A.bf16(8 * 3072).rearrange("p (k n) -> p k n", k=8)
        wst = [A.f32(8 * 384).rearrange("p (k n) -> p k n", k=8) for _ in range(2)]
        xb = [A.f32(D) for _ in range(3)]
        junk = A.bf16(D)
        ssq = A.f32(64)
        sd = A.f32(64)
        rs = A.f32(64)
        xs = [A.bf16(D) for _ in range(2)]
        hT = [A.bf16(8 * 512).rearrange("p (k t) -> p k t", k=8) for _ in range(2)]
        pst = [A.bf16(24 * 512).rearrange("p (j t) -> p j t", j=24) for _ in range(2)]
        for it0 in range(2):
            P.dma(xb[it0 % 3], xsrc[it0 * 128:(it0 + 1) * 128, :], reads=["X1"], writes=["x%d" % (it0 % 3)],
                  slot="x%d" % (it0 % 3))
        for pc in range(8):
            stg = wst[pc % 2]
            key = "wst%d" % (pc % 2)
            P.dma(stg, self.w_in_d[l][:, :, pc * 384:(pc + 1) * 384], writes=[key], slot=key)
            for k in range(8):
                eng = "dve" if k % 2 == 0 else "pool"
                self.ts(eng, wbf[:, k, pc * 384:(pc + 1) * 384], stg[:, k, :], cp[:, NG + k:NG + k + 1], ALU.mult,
                        r=[key, ck], w=["wbf%d_%d" % (pc, k)])
        Pv = self.P_d.rearrange("(j p) t -> p j t", p=128)

        def load(it):
            xk = "x%d" % (it % 3)
            P.dma(xb[it % 3], xsrc[it * 128:(it + 1) * 128, :], reads=["X1"], writes=[xk], slot=xk)

        def pre_sub(it):
            xt = xb[it % 3]
            xk = "x%d" % (it % 3)
            self.act(junk, xt, AF.Square, r=[xk], w=["ss%d" % it, "junk"], accum=ssq[:, it:it + 1])
            self.ts("pool", sd[:, it:it + 1], ssq[:, it:it + 1], 1.0 / D, ALU.mult, EPS, ALU.add,
                    r=["ss%d" % it], w=["sd%d" % it])
            self.tt("pool", rs[:, it:it + 1], sd[:, it:it + 1], self.negh[:, 0:1], ALU.pow,
                    r=["sd%d" % it, "negh"], w=["rs%d" % it])
            self.ts("dve", xs[it % 2], xt, rs[:, it:it + 1], ALU.mult, r=[xk, "rs%d" % it], w=["xs%d" % (it % 2)])
            if it + 2 < 64:
                load(it + 2)

        def tr_sub(it):
            g, sub = it // 4, it % 4
            pbk = "ps%d" % (6 + it % 2)
            for k in range(8):
                self.tr(self.pbv[it % 2][:, k * 128:(k + 1) * 128], xs[it % 2][:, k * 128:(k + 1) * 128],
                        self.ident_b, r=["xs%d" % (it % 2), "ident_b"], w=[pbk])
            eng = "act" if it % 2 == 0 else "dve"
            self.cpy(eng, hT[g % 2][:, :, sub * 128:(sub + 1) * 128],
                     self.pbv[it % 2].rearrange("p (k t) -> p k t", k=8), r=[pbk], w=["hT%d_%d" % (g % 2, sub)])

        for sub in range(4):
            pre_sub(sub)
            tr_sub(sub)
        evi = 0
        for g in range(16):
            hks = ["hT%d_%d" % (g % 2, s_) for s_ in range(4)]
            pks = []
            plain = [j for j in range(24) if j not in (11, 12) and j < 16]
            sigs, sils = [11, 12], list(range(16, 24))
            order = plain + (sigs + sils if g % 2 == 0 else sils + sigs)
            for part in range(4):
                nit = (g + 1) * 4 + part
                if g + 1 < 16:
                    pre_sub(nit)
                for j in order[part * 6:part * 6 + 6]:
                    b = evi % 4
                    evi += 1
                    pk = "ps%d" % b
                    for k in range(8):
                        self.mm(self.psA[:, b, :], wbf[:, k, j * 128:(j + 1) * 128], hT[g % 2][:, k, :],
                                k == 0, k == 7, r=hks + ["wbf%d_%d" % (j // 3, k)], w=[pk])
                    pko = "pst%d_%d" % (g % 2, j)
                    pks.append(pko)
                    if j in sigs:
                        self.act(pst[g % 2][:, j, :], self.psA[:, b, :], AF.Sigmoid, r=[pk], w=[pko])
                    elif j >= 16:
                        self.act(pst[g % 2][:, j, :], self.psA[:, b, :], AF.Silu, r=[pk], w=[pko])
                    else:
                        self.cpy("dve", pst[g % 2][:, j, :], self.psA[:, b, :], r=[pk], w=[pko])
                if g + 1 < 16:
                    tr_sub(nit)
            P.dma(Pv[:, :, g * 512:(g + 1) * 512], pst[g % 2], reads=pks, writes=["P"], slot="pst%d" % (g % 2))

    def phase2_hyena(self, l):
        P, A, cp = self.P, self.A, self.cp[l]
        ck = "cp%d" % l
        A.reset(self.persist_end)
        PW = L + 4
        pin = [A.bf16(PW) for _ in range(3)]
        ub = A.bf16(L)
        x0b = A.bf16(L)
        tmp = [[A.f32(2048) for _ in range(3)] for _ in range(2)]
        for i in range(3):
            self.memset("dve", pin[i][:, 0:2], 0.0, w=["hhalo"])
            self.memset("dve", pin[i][:, L + 2:L + 4], 0.0, w=["hhalo"])

        def conv3(out_last, acc, pinap, j, t0, S, pkey, akey, okey):
            c = HCW + j * 3
            self.act(acc, pinap[:, 2 + t0 - 1:2 + t0 - 1 + S], AF.Identity, r=[pkey, "hhalo", ck], w=[akey],
                     scale=cp[:, c:c + 1], bias=cp[:, HCB + j:HCB + j + 1])
            self.stt(acc, pinap[:, 2 + t0:2 + t0 + S], cp[:, c + 1:c + 2], acc, ALU.mult, ALU.add,
                     r=[pkey, akey, ck], w=[akey])
            self.stt(out_last, pinap[:, 2 + t0 + 1:2 + t0 + 1 + S], cp[:, c + 2:c + 3], acc, ALU.mult, ALU.add,
                     r=[pkey, "hhalo", akey, ck], w=[okey])

        for i in range(3):
            for gi, jj in enumerate([i, 3 + i, 6 + i]):
                P.dma(pin[gi][:, 2:2 + L], self.P_d[jj * 128:(jj + 1) * 128, :], reads=["P"],
                      writes=["hpin%d" % gi], slot="hpin%d" % gi)
            for sg in range(4):
                t0 = sg * 2048
                T = tmp[sg % 2]
                tk = ["ht%d_%d" % (sg % 2, q) for q in range(3)]
                conv3(T[0], T[0], pin[1], 3 + i, t0, 2048, "hpin1", tk[0], tk[0])
                conv3(T[1], T[1], pin[2], 6 + i, t0, 2048, "hpin2", tk[1], tk[1])
                self.tt("dve", ub[:, t0:t0 + 2048], T[0], T[1], ALU.mult, r=[tk[0], tk[1]], w=["ub%d" % sg])
                conv3(x0b[:, t0:t0 + 2048], T[2], pin[0], i, t0, 2048, "hpin0", tk[2], "x0b%d" % sg)
            P.dma(self.U_d[i * 128:(i + 1) * 128, :], ub, reads=["ub%d" % s_ for s_ in range(4)],
                  writes=["U"], slot="ub")
            P.dma(self.X0_d[i * 128:(i + 1) * 128, :], x0b, reads=["x0b%d" % s_ for s_ in range(4)],
                  writes=["X0"], slot="x0b")

    def phase2_conf(self, l):
        P, A, cp = self.P, self.A, self.cp[l]
        ck = "cp%d" % l
        A.reset(self.persist_end)
        UW = L + 32
        ug = [A.bf16(UW) for _ in range(2)]
        dg = A.bf16(2 * 31 * 128).rearrange("p (j k m) -> p j k m", j=2, k=31)
        pww = A.f32(2 * 256).rearrange("p (k n) -> p k n", k=2)
        ycf = [A.bf16(L) for _ in range(2)]
        ab = [[A.bf16(2048) for _ in range(2)] for _ in range(2)]
        T = []
        for i in range(2):
            T.append(dict(uc=A.f32(1024).rearrange("p (j t) -> p j t", j=2),
                          sqb=A.bf16(1024).rearrange("p (j t) -> p j t", j=2),
                          sub=A.bf16(1024).rearrange("p (j t) -> p j t", j=2),
                          mean=A.f32(512), m2=A.f32(512), var=A.f32(512), rstd=A.f32(512),
                          ln=A.f32(1024).rearrange("p (j t) -> p j t", j=2)))
        HW_ = 516
        hin = [A.bf16(9 * HW_).rearrange("p (j t) -> p j t", j=9) for _ in range(2)]
        hdg = A.bf16(27 * 128).rearrange("p (k m) -> p k m", k=27)
        x1c = [A.f32(3 * 512).rearrange("p (j t) -> p j t", j=3) for _ in range(2)]
        uo = [A.bf16(3 * 512).rearrange("p (j t) -> p j t", j=3) for _ in range(2)]
        x0o = [A.bf16(3 * 512).rearrange("p (j t) -> p j t", j=3) for _ in range(2)]
        for k in range(27):
            self.ts("dve" if k % 2 == 0 else "pool", hdg[:, k, :], self.ident_f, cp[:, HCW + k:HCW + k + 1], ALU.mult,
                    r=["ident_f", ck], w=["hdg%d" % k])
        hdgk = ["hdg%d" % k for k in range(27)]
        Phy = self.P_d[0:1152, :].rearrange("(j p) t -> p j t", p=128)
        Uv = self.U_d.rearrange("(j p) t -> p j t", p=128)
        X0v = self.X0_d.rearrange("(j p) t -> p j t", p=128)
        hbank = [0]

        def hy_load(tl):
            t0 = tl * 512
            hk = "hin%d" % (tl % 2)
            H = hin[tl % 2]
            if tl == 0:
                self.memset("dve", H[:, :, 0:2], 0.0, w=[hk])
                P.dma(H[:, :, 1:514], Phy[:, :, 0:513], reads=["P"], writes=[hk], slot=hk)
            elif tl == 15:
                P.dma(H[:, :, 0:513], Phy[:, :, t0 - 1:L], reads=["P"], writes=[hk], slot=hk)
                self.memset("dve", H[:, :, 513:514], 0.0, r=[hk], w=[hk + "h"])
            else:
                P.dma(H[:, :, 0:514], Phy[:, :, t0 - 1:t0 + 513], reads=["P"], writes=[hk], slot=hk)

        def hy_tile(tl):
            t0 = tl * 512
            hk = "hin%d" % (tl % 2)
            H = hin[tl % 2]
            st_ = tl % 2
            for i in range(3):
                for jj, kind in ((3 + i, "x1"), (6 + i, "v"), (i, "x0")):
                    b = 6 + hbank[0] % 2
                    hbank[0] += 1
                    pk = "ps%d" % b
                    for k in range(3):
                        self.mm(self.psA[:, b, :], hdg[:, jj * 3 + k, :], H[:, jj, k:k + 512], k == 0, k == 2,
                                r=[hk, hk + "h"] + hdgk, w=[pk])
                    bcol = cp[:, HCB + jj:HCB + jj + 1]
                    if kind == "x1":
                        self.act(x1c[st_][:, i, :], self.psA[:, b, :], AF.Identity, r=[pk, ck], w=["x1c%d_%d" % (st_, i)],
                                 bias=bcol)
                    elif kind == "v":
                        self.stt(uo[st_][:, i, :], self.psA[:, b, :], bcol, x1c[st_][:, i, :], ALU.add, ALU.mult,
                                 r=[pk, ck, "x1c%d_%d" % (st_, i)], w=["uo%d_%d" % (st_, i)])
                    else:
                        self.act(x0o[st_][:, i, :], self.psA[:, b, :], AF.Identity, r=[pk, ck], w=["x0o%d_%d" % (st_, i)],
                                 bias=bcol)
            P.dma(Uv[:, :, t0:t0 + 512], uo[st_], reads=["uo%d_%d" % (st_, i) for i in range(3)], writes=["U"],
                  slot="uo%d" % st_)
            P.dma(X0v[:, :, t0:t0 + 512], x0o[st_], reads=["x0o%d_%d" % (st_, i) for i in range(3)], writes=["X0"],
                  slot="x0o%d" % st_)

        P.dma(pww, self.pww_d[l], writes=["pww"], slot="pww")
        pwb16 = A.bf16(2 * 256).rearrange("p (k n) -> p k n", k=2)
        self.cpy("dve", pwb16, pww, r=["pww"], w=["pwb16"])
        for j in range(2):
            self.memset("dve", ug[j][:, 0:16], 0.0, w=["chalo"])
            self.memset("dve", ug[j][:, L + 16:L + 32], 0.0, w=["chalo"])
            for k in range(31):
                eng = "dve" if k % 2 == 0 else "pool"
                self.ts(eng, dg[:, j, k, :], self.ident_f, cp[:, CDW + j * 31 + k:CDW + j * 31 + k + 1], ALU.mult,
                        r=["ident_f", ck], w=["dg%d_%d" % (j, k)])
        n = 0
        for sg in range(4):
            t0 = sg * 2048
            for j in range(2):
                S = ab[n % 2]
                sk = "cab%d" % (n % 2)
                P.dma(S[0], self.P_d[(9 + j) * 128:(10 + j) * 128, t0:t0 + 2048], reads=["P"], writes=[sk + "a"], slot=sk + "a")
                P.dma(S[1], self.P_d[(11 + j) * 128:(12 + j) * 128, t0:t0 + 2048], reads=["P"], writes=[sk + "b"], slot=sk + "b")
                self.tt("dve", ug[j][:, 16 + t0:16 + t0 + 2048], S[0], S[1], ALU.mult,
                        r=[sk + "a", sk + "b"], w=["ug%d_%d" % (j, sg)])
                n += 1
        ugk = [["ug%d_%d" % (j, sg) for sg in range(4)] + ["chalo"] for j in range(2)]

        def conv_mm(tl):
            t0 = tl * 512
            segs = sorted(set(min(3, max(0, tt_ // 2048)) for tt_ in (t0 - 15, t0 + 526)))
            for j in range(2):
                pk = "ps%d" % j
                rk = ["ug%d_%d" % (j, sg) for sg in segs] + ["chalo"] + ["dg%d_%d" % (j, k_) for k_ in range(31)]
                for k in range(31):
                    self.mm(self.psA[:, j, :], dg[:, j, k, :], ug[j][:, 16 + t0 + k - 15:16 + t0 + k - 15 + 512],
                            k == 0, k == 30, r=rk, w=[pk])

        def conv_ev(tl):
            S = T[tl % 2]
            sk = "ct%d" % (tl % 2)
            for j in range(2):
                pk = "ps%d" % j
                self.act(S["uc"][:, j, :], self.psA[:, j, :], AF.Identity, r=[pk, ck], w=[sk + "uc%d" % j],
                         bias=cp[:, CDB + j:CDB + j + 1])
                self.act(S["sqb"][:, j, :], self.psA[:, j, :], AF.Square, r=[pk, ck], w=[sk + "sq%d" % j],
                         bias=cp[:, CDB + j:CDB + j + 1])

        def stats(tl):
            S = T[tl % 2]
            sk = "ct%d" % (tl % 2)
            for j in range(2):
                self.mm(self.psA[:, 2, :], self.ones_f, S["uc"][:, j, :], j == 0, j == 1,
                        r=["ones", sk + "uc%d" % j], w=["ps2"])
            for j in range(2):
                self.mm(self.psA[:, 3, :], self.ones_b, S["sqb"][:, j, :], j == 0, j == 1,
                        r=["ones", sk + "sq%d" % j], w=["ps3"])

        def chain(tl):
            S = T[tl % 2]
            sk = "ct%d" % (tl % 2)
            self.act(S["mean"], self.psA[:, 2, :], AF.Copy, r=["ps2"], w=[sk + "mean"], scale=1.0 / 256)
            self.act(S["m2"], self.psA[:, 2, :], AF.Square, r=["ps2"], w=[sk + "m2"], scale=1.0 / 256)
            self.stt(S["var"], self.psA[:, 3, :], 1.0 / 256, S["m2"], ALU.mult, ALU.subtract,
                     r=["ps3", sk + "m2"], w=[sk + "var"])
            self.act(S["var"], S["var"], AF.Ln, r=[sk + "var"], w=[sk + "var"], bias=EPS)
            self.act(S["rstd"], S["var"], AF.Exp, r=[sk + "var"], w=[sk + "rstd"], scale=-0.5)
            for j in range(2):
                self.tt("dve", S["ln"][:, j, :], S["uc"][:, j, :], S["mean"], ALU.subtract,
                        r=[sk + "uc%d" % j, sk + "mean"], w=[sk + "ln%d" % j])
                self.tt("dve", S["ln"][:, j, :], S["ln"][:, j, :], S["rstd"], ALU.mult,
                        r=[sk + "ln%d" % j, sk + "rstd"], w=[sk + "ln%d" % j])
                self.act(S["sub"][:, j, :], S["ln"][:, j, :], AF.Silu, r=[sk + "ln%d" % j, ck], w=[sk + "su%d" % j],
                         scale=cp[:, CLG + j:CLG + j + 1], bias=cp[:, CLB + j:CLB + j + 1])

        def pw(tl):
            t0 = tl * 512
            S = T[tl % 2]
            sk = "ct%d" % (tl % 2)
            for e_ in range(2):
                pk = "ps%d" % (4 + e_)
                for j in range(2):
                    self.mm(self.psA[:, 4 + e_, :], pwb16[:, j, e_ * 128:(e_ + 1) * 128], S["sub"][:, j, :],
                            j == 0, j == 1, r=["pwb16", sk + "su%d" % j], w=[pk])
                self.act(ycf[e_][:, t0:t0 + 512], self.psA[:, 4 + e_, :], AF.Identity, r=[pk, ck],
                         w=["ycf%d_%d" % (e_, tl)], bias=cp[:, CPB + e_:CPB + e_ + 1])

        hy_load(0)
        hy_load(1)
        conv_mm(0)
        conv_ev(0)
        for tl in range(16):
            stats(tl)
            if tl + 1 < 16:
                conv_mm(tl + 1)
            chain(tl)
            if tl + 1 < 16:
                conv_ev(tl + 1)
            pw(tl)
            hy_tile(tl)
            if tl + 2 < 16:
                hy_load(tl + 2)
        for e_ in range(2):
            P.dma(self.YCF_d[e_ * 128:(e_ + 1) * 128, :], ycf[e_], reads=["ycf%d_%d" % (e_, tl) for tl in range(16)],
                  writes=["YCF"], slot="ycf%d" % e_)

    def phase2_rg(self, l):
        P, A, cp = self.P, self.A, self.cp[l]
        ck = "cp%d" % l
        A.reset(self.persist_end)
        pin = A.bf16(L + 4)
        xr = A.bf16(L)
        rb = A.bf16(2 * L).rearrange("p (d t) -> p d t", d=2)
        ib = A.bf16(2 * L).rearrange("p (d t) -> p d t", d=2)
        ob = A.bf16(L)
        obb = A.bf16(L)
        gw = A.f32(4 * 128).rearrange("p (g m) -> p g m", g=4)
        gwb = A.bf16(4 * 128).rearrange("p (g m) -> p g m", g=4)
        rdg = A.bf16(4 * 128).rearrange("p (k m) -> p k m", k=4)
        T = [dict(a=A.f32(2048), m=A.f32(2048)) for _ in range(2)]
        self.memset("dve", pin[:, 0:2], 0.0, w=["rhalo"])
        self.memset("dve", pin[:, L + 2:L + 4], 0.0, w=["rhalo"])
        nb = 0
        for j in range(3):
            P.dma(pin[:, 2:2 + L], self.P_d[(13 + j) * 128:(14 + j) * 128, :], reads=["P"], writes=["rpin"], slot="rpin")
            P.dma(gw, self.rgw_d[l][j], writes=["gw0"], slot="gw")
            self.cpy("dve", gwb, gw, r=["gw0"], w=["gw"])
            c = RCW + j * 4
            for k in range(4):
                self.ts("dve", rdg[:, k, :], self.ident_f, cp[:, c + k:c + k + 1], ALU.mult, r=["ident_f", ck], w=["rdg"])
            for tl in range(16):
                t0 = tl * 512
                b = 6 + tl % 2
                pk = "ps%d" % b
                for k in range(4):
                    self.mm(self.psA[:, b, :], rdg[:, k, :], pin[:, t0 + k:t0 + k + 512], k == 0, k == 3,
                            r=["rpin", "rhalo", "rdg"], w=[pk])
                self.act(xr[:, t0:t0 + 512], self.psA[:, b, :], AF.Identity, r=[pk, ck], w=["xr%d" % tl],
                         bias=cp[:, RCB + j:RCB + j + 1])
            for d in range(2):
                for tl in range(16):
                    t0 = tl * 512
                    for gi in range(2):
                        b = nb % 6
                        nb += 1
                        pk = "ps%d" % b
                        self.mm(self.psA[:, b, :], gwb[:, d * 2 + gi, :], xr[:, t0:t0 + 512],
                                True, True, r=["gw", "xr%d" % tl], w=[pk])
                        dst = rb if gi == 0 else ib
                        bc = (RBA if gi == 0 else RBX) + d * 3 + j
                        self.act(dst[:, d, t0:t0 + 512], self.psA[:, b, :], AF.Sigmoid, r=[pk, ck],
                                 w=[("rb" if gi == 0 else "ib") + "%d_%d" % (d, tl)], bias=cp[:, bc:bc + 1])
            n = 0
            for d in range(2):
                order = [0, 1, 2, 3] if d == 0 else [3, 2, 1, 0]
                cc = d * 3 + j
                for oi, sg in enumerate(order):
                    t0 = sg * 2048
                    S = T[n % 2]
                    sk = "rt%d" % (n % 2)
                    n += 1
                    rks = ["rb%d_%d" % (d, sg * 4 + q) for q in range(4)]
                    iks = ["ib%d_%d" % (d, sg * 4 + q) for q in range(4)]
                    xks = ["xr%d" % (sg * 4 + q) for q in range(4)]
                    rr = rb[:, d, t0:t0 + 2048]
                    self.act(S["a"], rr, AF.Exp, r=rks + [ck], w=[sk + "a"], scale=cp[:, CL + cc:CL + cc + 1])
                    self.act(S["m"], rr, AF.Exp, r=rks + [ck], w=[sk + "m"], scale=cp[:, CL2 + cc:CL2 + cc + 1])
                    self.act(S["m"], S["m"], AF.Ln, r=[sk + "m"], w=[sk + "m"], scale=-1.0, bias=1.0)
                    self.act(S["m"], S["m"], AF.Exp, r=[sk + "m"], w=[sk + "m"], scale=0.5)
                    self.tt("dve", S["m"], S["m"], ib[:, d, t0:t0 + 2048], ALU.mult, r=[sk + "m"] + iks, w=[sk + "m"])
                    self.tt("dve", S["m"], S["m"], xr[:, t0:t0 + 2048], ALU.mult, r=[sk + "m"] + xks, w=[sk + "m"])
                    if d == 0:
                        init = 0.0 if oi == 0 else ob[:, t0 - 1:t0]
                        rd = [sk + "a", sk + "m"] + (["ob%d" % (sg - 1)] if oi > 0 else [])
                        o_, a_, m_ = ob[:, t0:t0 + 2048], S["a"], S["m"]
                        P.op("dve", lambda e, o_=o_, a_=a_, m_=m_, init=init: e.tensor_tensor_scan(
                            out=o_, data0=a_, data1=m_, initial=init, op0=ALU.mult, op1=ALU.add), rd, ["ob%d" % sg])
                    else:
                        init = 0.0 if oi == 0 else obb[:, t0 + 2048:t0 + 2049]
                        rd = [sk + "a", sk + "m"] + (["obb%d" % (sg + 1)] if oi > 0 else [])
                        o_, a_, m_ = obb[:, t0:t0 + 2048][:, ::-1], S["a"][:, ::-1], S["m"][:, ::-1]
                        P.op("dve", lambda e, o_=o_, a_=a_, m_=m_, init=init: e.tensor_tensor_scan(
                            out=o_, data0=a_, data1=m_, initial=init, op0=ALU.mult, op1=ALU.add), rd, ["obb%d" % sg])
            P.dma(self.YRG_d[j * 128:(j + 1) * 128, :], ob, reads=["ob%d" % s_ for s_ in range(4)],
                  writes=["YRG"], slot="ob")
            P.dma(self.YRGB_d[j * 128:(j + 1) * 128, :], obb, reads=["obb%d" % s_ for s_ in range(4)],
                  writes=["YRG"], slot="obb")

    def phase2_gate(self, l):
        P, A = self.P, self.A
        A.reset(self.persist_end)
        gi_ = [A.bf16(L) for _ in range(2)]
        go_ = [A.bf16(L) for _ in range(2)]
        for j in range(8):
            ik, ok = "gi%d" % (j % 2), "go%d" % (j % 2)
            P.dma(gi_[j % 2], self.P_d[(16 + j) * 128:(17 + j) * 128, :], reads=["P"], writes=[ik], slot=ik)
            self.act(go_[j % 2], gi_[j % 2], AF.Silu, r=[ik], w=[ok])
            P.dma(self.GS_d[j * 128:(j + 1) * 128, :], go_[j % 2], reads=[ok], writes=["GS"], slot=ok)

    def phase3_fft(self, l):
        P, A = self.P, self.A
        A.reset(self.persist_end)
        GT4 = self.load_gt()
        Aab = [A.bf16(128 * 3 * NCH) for _ in range(2)]
        Bb = A.bf16(128 * 2 * NCH)
        B4 = Bb.rearrange("p (q c pl) -> p q c pl", pl=2, c=NCH)
        Yy = A.bf16(128 * 2 * NCH)
        Y4 = Yy.rearrange("p (q pl c) -> p q pl c", pl=2, c=NCH)
        KFt = A.bf16(128 * 2 * NCH)
        KF4 = KFt.rearrange("p (q c pl) -> p q c pl", pl=2, c=NCH)
        Ux = A.bf16(NCH * 128).rearrange("p (c l) -> p c l", c=NCH)
        yb = A.bf16(NCH * 128).rearrange("p (c l) -> p c l", c=NCH)
        ta = [A.f32(512).rearrange("p (q c pl) -> p q c pl", pl=2, c=NCH) for _ in range(2)]
        tb = [A.f32(512).rearrange("p (q c pl) -> p q c pl", pl=2, c=NCH) for _ in range(2)]
        Ur = self.U_d.rearrange("c (hi lo) -> hi c lo", lo=128)
        YCr = self.YC_d.rearrange("c (hi lo) -> hi c lo", lo=128)

        def load_U(ch):
            P.dma(Ux[0:64], Ur[:, ch * NCH:(ch + 1) * NCH, :], reads=["U"], writes=["Ux"], slot="Ux")

        def load_KF(ch):
            P.dma(KFt, self.KF_d[ch], reads=["KF"], writes=["KFt"], slot="kfi")

        def akeys(ch):
            return ["A%d_%d" % (ch % 2, c) for c in range(NCH)]

        def S1(ch):
            A4 = Aab[ch % 2].rearrange("p (q c pl) -> p q c pl", pl=3, c=NCH)
            self.fft_s1(Ux, self.F1, A4, akeys(ch), "Ux", "d")

        def S2(ch):
            A4 = Aab[ch % 2].rearrange("p (q c pl) -> p q c pl", pl=3, c=NCH)
            ak = akeys(ch)
            for kg in range(16):
                b = 4 + kg % 4
                pk = "ps%d" % b
                for q8 in range(8):
                    q = kg * 8 + q8
                    o = self.psA[:, b, q8 * 64:(q8 + 1) * 64]
                    self.mm(o, GT4[:, q, 0, :], A4[:, q, :, 0:2], True, False, r=["GT"] + ak, w=[pk])
                    self.mm(o, GT4[:, q, 1, :], A4[:, q, :, 1:3], False, True, r=["GT"] + ak, w=[pk])
                X4 = self.psA[:, b, :].rearrange("p (q c pl) -> p q c pl", pl=2, c=NCH)
                tak, tbk = "ta%d" % (kg % 2), "tb%d" % (kg % 2)
                qs = slice(kg * 8, (kg + 1) * 8)
                self.tt("dve", ta[kg % 2], X4, KF4[:, qs, :, :], ALU.mult, r=[pk, "KFt"], w=[tak])
                self.tt("dve", tb[kg % 2], X4, KF4[:, qs, :, ::-1], ALU.mult, r=[pk, "KFt"], w=[tbk])
                self.tt("pool", Y4[:, qs, 0, :], ta[kg % 2][:, :, :, 0], ta[kg % 2][:, :, :, 1], ALU.subtract,
                        r=[tak], w=["Y%dr" % kg])
                self.tt("pool", Y4[:, qs, 1, :], tb[kg % 2][:, :, :, 0], tb[kg % 2][:, :, :, 1], ALU.add,
                        r=[tbk], w=["Y%di" % kg])

        def I12(ch):
            c0 = ch * NCH
            yall = ["Y%dr" % kg for kg in range(16)] + ["Y%di" % kg for kg in range(16)]
            bk = ["B%d" % c for c in range(NCH)]
            for c in range(NCH):
                b = c % 4
                ps = self.psA[:, b, 0:256]
                pk = "ps%d" % b
                self.mm(ps, Y4[:, :, 0, c], self.FIa, True, False, r=yall + ["FI"], w=[pk])
                self.mm(ps, Y4[:, :, 1, c], self.FIb, False, True, r=yall + ["FI"], w=[pk])
                eng = "dve" if c % 4 == 3 else "act"
                self.cpy(eng, B4[:, :, c, :], ps.rearrange("p (q pl) -> p q pl", pl=2), r=[pk], w=[bk[c]])
            ybw = []
            for lg in range(8):
                b = 4 + lg % 4
                pk = "ps%d" % b
                pv = self.psA[0:64, b, :].rearrange("p (c l) -> p c l", l=16)
                for l16 in range(16):
                    q = lg * 16 + l16
                    o = pv[:, :, l16]
                    self.mm(o, GT4[:, q, 0, 0:64], B4[:, q, :, 0], True, False, r=["GT"] + bk, w=[pk])
                    self.mm(o, GT4[:, q, 1, 0:64], B4[:, q, :, 1], False, True, r=["GT"] + bk, w=[pk])
                wk = "yb_%d" % lg
                ybw.append(wk)
                self.cpy("act" if lg % 2 == 0 else "dve", yb[0:64, :, lg * 16:(lg + 1) * 16], pv, r=[pk], w=[wk])
            P.dma(YCr[:, c0:c0 + NCH, :], yb[0:64], reads=ybw, writes=["YC"], slot="yb")

        load_U(0)
        load_KF(0)
        S1(0)
        for ch in range(12):
            S2(ch)
            if ch + 1 < 12:
                load_U(ch + 1)
                load_KF(ch + 1)
                S1(ch + 1)
            I12(ch)

    def phase4(self, l, s, xsrc):
        P, A, cp = self.P, self.A, self.cp[l]
        ck = "cp%d" % l
        last = (l == DEPTH - 1)
        A.reset(self.persist_end)
        wo = A.bf16(8 * D).rearrange("p (k n) -> p k n", k=8)
        wst = [A.f32(D) for _ in range(2)]
        LD = []
        for i in range(2):
            LD.append(dict(x0=A.bf16(3 * 512).rearrange("p (j t) -> p j t", j=3),
                           u=A.bf16(3 * 512).rearrange("p (j t) -> p j t", j=3),
                           yc=A.bf16(3 * 512).rearrange("p (j t) -> p j t", j=3),
                           ycf=A.bf16(2 * 512).rearrange("p (j t) -> p j t", j=2),
                           yrg=A.bf16(3 * 512).rearrange("p (j t) -> p j t", j=3),
                           yrgb=A.bf16(3 * 512).rearrange("p (j t) -> p j t", j=3),
                           gs=A.bf16(8 * 512).rearrange("p (j t) -> p j t", j=8),
                           xr=A.f32(4 * D).rearrange("p (s d) -> p s d", s=4)))
        yraw = A.bf16(3 * 512).rearrange("p (j t) -> p j t", j=3)
        yr32 = A.f32(3 * 512).rearrange("p (j t) -> p j t", j=3)
        yrs = A.bf16(3 * 512).rearrange("p (j t) -> p j t", j=3)
        sqb = A.bf16(8 * 512).rearrange("p (j t) -> p j t", j=8)
        gsr = A.bf16(8 * 512).rearrange("p (j t) -> p j t", j=8)
        rstd = A.f32(3 * 512).rearrange("p (j t) -> p j t", j=3)
        ym = [A.bf16(8 * 512).rearrange("p (j t) -> p j t", j=8) for _ in range(2)]
        xn = [A.f32(D) for _ in range(2)]
        junk = A.bf16(D)
        fss = A.f32(64)
        fsd = A.f32(64)
        frs = A.f32(64)
        yo = [A.f32(D) for _ in range(2)]
        wok = ["wo%d" % k for k in range(8)]

        def load_wo():
            for k in range(8):
                key = "wst%d" % (k % 2)
                P.dma(wst[k % 2], self.w_out_d[l][:, k, :], writes=[key], slot=key)
                self.ts("dve" if k % 2 == 0 else "pool", wo[:, k, :], wst[k % 2], cp[:, GG + k:GG + k + 1], ALU.mult,
                        r=[key, ck], w=["wo%d" % k])
        ydst = self.y_d[s] if last else self.X1_d[s]
        grp = [0, 0, 0, 1, 1, 2, 2, 2]
        gw = [384.0, 256.0, 384.0]
        gr = [(0, 3), (3, 5), (5, 8)]
        nbc = [0]

        def loads_bf(tl):
            t0 = tl * 512
            S = LD[tl % 2]
            sk = "p4_%d" % (tl % 2)
            for nm, dten, nj in [("x0", self.X0_d, 3), ("u", self.U_d, 3), ("yc", self.YC_d, 3),
                                 ("yrg", self.YRG_d, 3), ("yrgb", self.YRGB_d, 3), ("ycf", self.YCF_d, 2),
                                 ("gs", self.P_d[2048:3072, :], 8)]:
                P.dma(S[nm], dten.rearrange("(j p) t -> p j t", p=128)[:, :, t0:t0 + 512],
                      reads=["X0", "U", "YC", "YCF", "YRG", "P"], writes=[sk + nm], slot=sk + nm)

        def load_xr(tl):
            t0 = tl * 512
            S = LD[tl % 2]
            sk = "p4_%d" % (tl % 2)
            P.dma(S["xr"], xsrc[t0:t0 + 512, :].rearrange("(s p) d -> p s d", p=128), reads=["X1"],
                  writes=[sk + "xr"], slot=sk + "xr")

        def ysrc_of(tl):
            S = LD[tl % 2]
            sk = "p4_%d" % (tl % 2)
            ysrc = [(yraw[:, i, :], "yraw%d" % i) for i in range(3)]
            ysrc += [(S["ycf"][:, j, :], sk + "ycf") for j in range(2)]
            ysrc += [(yrs[:, j, :], "yrs%d" % j) for j in range(3)]
            return ysrc

        def E1(tl):
            S = LD[tl % 2]
            sk = "p4_%d" % (tl % 2)
            for i in range(3):
                self.stt(yr32[:, i, :], S["u"][:, i, :], cp[:, HD + i:HD + i + 1], S["yc"][:, i, :], ALU.mult, ALU.add,
                         r=[sk + "u", sk + "yc", ck], w=["yr32_%d" % i])
                self.tt("pool", yraw[:, i, :], yr32[:, i, :], S["x0"][:, i, :], ALU.mult,
                        r=["yr32_%d" % i, sk + "x0"], w=["yraw%d" % i])
            for j in range(3):
                self.tt("dve", yrs[:, j, :], S["yrg"][:, j, :], S["yrgb"][:, j, :], ALU.add,
                        r=[sk + "yrg", sk + "yrgb"], w=["yrs%d" % j])
            ysrc = ysrc_of(tl)
            for k in range(8):
                self.act(sqb[:, k, :], ysrc[k][0], AF.Square, r=[ysrc[k][1]], w=["sq%d" % k])

        def E2a(tl):
            for g in range(3):
                lo, hi = gr[g]
                for k in range(lo, hi):
                    self.mm(self.psA[:, g, :], self.ones_b, sqb[:, k, :], k == lo, k == hi - 1,
                            r=["ones", "sq%d" % k], w=["ps%d" % g])
                self.act(rstd[:, g, :], self.psA[:, g, :], AF.Ln, r=["ps%d" % g], w=["rstd%d" % g],
                         scale=1.0 / gw[g], bias=EPS)
                self.act(rstd[:, g, :], rstd[:, g, :], AF.Exp, r=["rstd%d" % g], w=["rstd%d" % g], scale=-0.5)

        def E2b1(tl):
            S = LD[tl % 2]
            sk = "p4_%d" % (tl % 2)
            for k in range(8):
                self.tt("pool" if k in (3, 6) else "dve", gsr[:, k, :], S["gs"][:, k, :], rstd[:, grp[k], :], ALU.mult,
                        r=[sk + "gs", "rstd%d" % grp[k]], w=["gsr%d" % k])

        def E2b2(tl):
            ysrc = ysrc_of(tl)
            for k in range(8):
                self.tt("dve", ym[tl % 2][:, k, :], ysrc[k][0], gsr[:, k, :], ALU.mult,
                        r=[ysrc[k][1], "gsr%d" % k], w=["ym%d_%d" % (tl % 2, k)])

        mbank = {}

        def M_mm(tl, half):
            ymk = ["ym%d_%d" % (tl % 2, k) for k in range(8)]
            for s4 in (2 * half, 2 * half + 1):
                for nh in range(2):
                    b = 3 + nbc[0] % 5
                    nbc[0] += 1
                    mbank[(tl, s4, nh)] = b
                    pk = "ps%d" % b
                    for k in range(8):
                        self.mm(self.psA[:, b, :], ym[tl % 2][:, k, s4 * 128:(s4 + 1) * 128],
                                wo[:, k, nh * 512:(nh + 1) * 512], k == 0, k == 7, r=ymk + wok, w=[pk])

        def M_add(tl, half):
            t0 = tl * 512
            S = LD[tl % 2]
            sk = "p4_%d" % (tl % 2)
            for s4 in (2 * half, 2 * half + 1):
                it = tl * 4 + s4
                xk = "xn%d" % (it % 2)
                for nh in range(2):
                    b = mbank[(tl, s4, nh)]
                    pk = "ps%d" % b
                    self.tt("dve", xn[it % 2][:, nh * 512:(nh + 1) * 512], self.psA[:, b, :],
                            S["xr"][:, s4, nh * 512:(nh + 1) * 512], ALU.add, r=[pk, sk + "xr"], w=[xk + "_%d" % nh])
                xks = [xk + "_0", xk + "_1"]
                rows = slice(t0 + s4 * 128, t0 + (s4 + 1) * 128)
                if not last:
                    P.dma(ydst[rows, :], xn[it % 2], reads=xks, writes=["X1"], slot=xk)
                else:
                    c1 = slice(it % 64, it % 64 + 1)
                    self.act(junk, xn[it % 2], AF.Square, r=xks, w=["fss%d" % it, "junk"], accum=fss[:, c1])
                    self.ts("pool", fsd[:, c1], fss[:, c1], 1.0 / D, ALU.mult, EPS, ALU.add,
                            r=["fss%d" % it], w=["fsd%d" % it])
                    self.tt("pool", frs[:, c1], fsd[:, c1], self.negh[:, 0:1], ALU.pow,
                            r=["fsd%d" % it, "negh"], w=["frs%d" % it])
                    ok = "yo%d" % (it % 2)
                    self.stt(yo[it % 2], xn[it % 2], frs[:, c1], self.fg, ALU.mult, ALU.mult,
                             r=xks + ["frs%d" % it, "fg"], w=[ok])
                    P.dma(ydst[rows, :], yo[it % 2], reads=[ok], writes=["Y"], slot=ok)

        load_wo()
        loads_bf(0)
        load_xr(0)
        loads_bf(1)
        load_xr(1)
        E1(0)
        E2a(0)
        E2b1(0)
        E2b2(0)
        for tl in range(16):
            nx = tl + 1 < 16
            if tl + 2 < 16:
                loads_bf(tl + 2)
            if nx:
                E1(tl + 1)
                E2a(tl + 1)
            M_mm(tl, 0)
            if nx:
                E2b1(tl + 1)
            M_add(tl, 0)
            M_mm(tl, 1)
            if nx:
                E2b2(tl + 1)
            M_add(tl, 1)
            if tl + 2 < 16:
                load_xr(tl + 2)


def _consts():
    bf = ml_dtypes.bfloat16
    N = 2 * L
    hi = np.arange(128)[:, None]
    q = np.arange(128)[None, :]
    phi = 2 * np.pi * ((hi * q) % 128) / 128.0
    F1 = np.stack([np.cos(phi), -np.sin(phi), -np.cos(phi)], axis=-1).reshape(128, 384)
    F1b = np.stack([-np.sin(phi), np.cos(phi), np.sin(phi)], axis=-1).reshape(128, 384)[0:64]
    p = np.arange(128)[:, None, None]
    qq = np.arange(128)[None, :, None]
    r = np.arange(128)[None, None, :]
    th = 2 * np.pi * ((p * (qq + 128 * r)) % N) / N
    GT = np.stack([np.cos(th), np.sin(th)], axis=2).reshape(128, 128 * 2 * 128)
    kh = np.arange(128)[:, None]
    lo = np.arange(128)[None, :]
    psi = 2 * np.pi * kh * lo / 128.0
    FIa = np.stack([np.cos(psi), -np.sin(psi)], axis=-1).reshape(128, 256)
    FIb = np.stack([-np.sin(psi), -np.cos(psi)], axis=-1).reshape(128, 256)
    t = np.linspace(0.0, 1.0, L, dtype=np.float32).astype(np.float64)
    w = (2.0 * np.pi / L) * np.arange(L, dtype=np.float64)
    f = np.linspace(1e-4, 15.0, 16, dtype=np.float32).astype(np.float64)
    zT = np.concatenate([t[None, :], np.cos(f[:, None] * w[None, :]), -np.sin(f[:, None] * w[None, :])], axis=0)
    max_decay = math.log(1e-2) / 0.3
    min_decay = math.log(1e-2) / 1.5
    deltas = np.abs(np.linspace(min_decay, max_decay, 384, dtype=np.float32)).astype(np.float32)
    return dict(F1=F1.astype(bf), F1b=F1b.astype(bf), GT=GT.astype(bf), FIa=FIa.astype(bf), FIb=FIb.astype(bf),
                zT=zT.astype(np.float32), tt=np.ascontiguousarray(np.broadcast_to(t.astype(np.float32), (128, L))),
                deltas=deltas, ident=np.eye(128, dtype=np.float32))


def _pack_cp(inp, C):
    cp = np.zeros((DEPTH, 128, NCP), np.float32)
    f = lambda a: np.asarray(a, np.float32)
    for l in range(DEPTH):
        def ch(v, n):
            return f(v).reshape(n, 128).T
        w = f(inp["hy_conv_w"][l])
        for j in range(9):
            for k in range(3):
                cp[l, :, HCW + j * 3 + k] = w[k, j * 128:(j + 1) * 128]
        cp[l, :, HCB:HCB + 9] = ch(inp["hy_conv_b"][l], 9)
        cp[l, :, HD:HD + 3] = ch(inp["hy_d"][l], 3)
        w = f(inp["cf_dw_w"][l])
        for j in range(2):
            for k in range(31):
                cp[l, :, CDW + j * 31 + k] = w[k, j * 128:(j + 1) * 128]
        cp[l, :, CDB:CDB + 2] = ch(inp["cf_dw_b"][l], 2)
        cp[l, :, CLG:CLG + 2] = ch(inp["cf_ln_g"][l], 2)
        cp[l, :, CLB:CLB + 2] = ch(inp["cf_ln_b"][l], 2)
        cp[l, :, CPB:CPB + 2] = ch(inp["cf_pw_b"][l], 2)
        w = f(inp["rg_conv_w"][l])
        for j in range(3):
            for k in range(4):
                cp[l, :, RCW + j * 4 + k] = w[k, j * 128:(j + 1) * 128]
        cp[l, :, RCB:RCB + 3] = ch(inp["rg_conv_b"][l], 3)
        for d in range(2):
            cp[l, :, RBA + d * 3:RBA + d * 3 + 3] = ch(inp["rg_ba"][l, d], 3)
            cp[l, :, RBX + d * 3:RBX + d * 3 + 3] = ch(inp["rg_bx"][l, d], 3)
            cp[l, :, RLAM + d * 3:RLAM + d * 3 + 3] = ch(inp["rg_lam"][l, d], 3)
        cp[l, :, GG:GG + 8] = ch(inp["grp_g"][l], 8)
        cp[l, :, NG:NG + 8] = ch(inp["norm_g"][l], 8)
        cp[l, 0:64, MLPB + 0] = f(inp["hy_b1"][l])
        cp[l, 0:64, MLPB + 1] = f(inp["hy_b2"][l])
        cp[l, 0:64, MLPB + 2] = f(inp["hy_b3"][l])
        cp[l, 0:64, MLPB + 3] = f(inp["hy_freq"][l])
        cp[l, :, DEL:DEL + 3] = -C["deltas"].reshape(3, 128).T
    return cp


_NC_CACHE = {}


def kernel(**inp):
    C = _consts()
    f = lambda a: np.ascontiguousarray(np.asarray(a, np.float32))
    w_in = f(inp["w_in"]).reshape(DEPTH, 8, 128, 3072).transpose(0, 2, 1, 3)
    w_out = f(inp["w_out"]).reshape(DEPTH, 8, 128, 1024).transpose(0, 2, 1, 3)
    cp = _pack_cp(inp, C)
    rgw = np.zeros((DEPTH, 3, 128, 4, 128), np.float32)
    wa, wx = f(inp["rg_wa"]), f(inp["rg_wx"])
    for l in range(DEPTH):
        for d in range(2):
            for h in range(6):
                j, o = h // 2, (h % 2) * 64
                rgw[l, j, o:o + 64, d * 2 + 0, o:o + 64] = wa[l, d, h]
                rgw[l, j, o:o + 64, d * 2 + 1, o:o + 64] = wx[l, d, h]
    pww = f(inp["cf_pw_w"]).reshape(DEPTH, 2, 128, 256).transpose(0, 2, 1, 3)
    common = {
        "w_in": np.ascontiguousarray(w_in), "w_out": np.ascontiguousarray(w_out), "cp": cp,
        "fg": np.ascontiguousarray(np.broadcast_to(f(inp["final_g"])[None, :], (128, D))),
        "hw1": f(inp["hy_w1"]), "hw2": f(inp["hy_w2"]), "hw3": f(inp["hy_w3"]), "hw4": f(inp["hy_w4"]),
        "zT": C["zT"], "tt": C["tt"], "rgw": rgw, "pww": np.ascontiguousarray(pww), "ident": C["ident"],
        "F1": C["F1"], "F1b": C["F1b"], "GT": C["GT"], "FIa": C["FIa"], "FIb": C["FIb"],
    }
    xp, xs = f(inp["x_prompt"]), f(inp["x_sample"])
    in_maps = []
    for i in range(8):
        m = dict(common)
        m["xa"] = xp[i]
        m["xb"] = xs[i % 4]
        in_maps.append(m)
    if "nc" not in _NC_CACHE:
        _NC_CACHE["nc"] = Builder().build()
    res = run_bass_kernel_spmd(_NC_CACHE["nc"], in_maps, core_ids=list(range(8)))
    yp = np.stack([np.asarray(res.results[i]["ya"], np.float32) for i in range(8)], axis=0)
    ys = np.stack([np.asarray(res.results[i]["yb"], np.float32) for i in range(4)], axis=0)
    return (yp, ys)
```
